# Optimizing a Trainium2 kernel written in Bass

```python
import math
import jax, jax.numpy as jnp
from jax import lax
import numpy as np

D_MODEL = 1024
BATCH = 16
SEQ = 2048
DEPTH = 2
DEC_BATCH = 128
DEC_SEQ = 1
PAST_LEN = 16384
PAGE_SIZE = 128

WINDOW = 128
HA_Q = 16
HA_KV = 4
HD_A = D_MODEL // HA_Q
NUM_BUCKETS = 32
MAX_DISTANCE = WINDOW
HB = 8
DK_B = (D_MODEL // 2) // HB
DV_B = D_MODEL // HB
RET_CHUNK = 128
D_INNER_C = D_MODEL
HD_C = 64
HC = D_INNER_C // HD_C
N_C = 128
G_C = 2
CONV_W = 4
CONV_DIM = D_INNER_C + 2 * G_C * N_C
SSD_CHUNK = 128
D_FF = 4 * D_MODEL
EPS = 1e-6
SPLIT_SIZES = (HA_Q * HD_A, HA_KV * HD_A, HA_KV * HD_A,
               HB * DK_B, HB * DK_B, HB * DV_B, HB * DV_B,
               D_INNER_C, CONV_DIM, HC, 3 * D_MODEL)
D_IN = sum(SPLIT_SIZES)

kernel_name = "hybrid_swa_retention_ssd_decoder_step"

F32 = jnp.float32


def _rmsnorm(x, w):
    xf = x.astype(F32)
    y = xf * lax.rsqrt(jnp.mean(xf * xf, axis=-1, keepdims=True) + EPS)
    return (y * w.astype(F32)).astype(x.dtype)


def _rms(x):
    xf = x.astype(F32)
    return xf * lax.rsqrt(jnp.mean(xf * xf, axis=-1, keepdims=True) + EPS)


def _split_proj(z):
    out = []
    start = 0
    for s in SPLIT_SIZES:
        out.append(z[..., start:start + s])
        start += s
    return out


def _t5_bucket(dist):
    max_exact = NUM_BUCKETS // 2
    n = jnp.maximum(dist, 0)
    nf = jnp.maximum(n, 1).astype(F32)
    large = max_exact + (jnp.log(nf / max_exact) / math.log(MAX_DISTANCE / max_exact)
                         * (NUM_BUCKETS - max_exact)).astype(jnp.int32)
    large = jnp.minimum(large, NUM_BUCKETS - 1)
    return jnp.where(n < max_exact, n, large)


def _window_attn(q, k, v, dist, valid, sinks, rel_table):
    G = HA_Q // HA_KV
    qg = q.reshape(q.shape[:-2] + (HA_KV, G, HD_A)).astype(F32)
    s = jnp.einsum('...qkgd,...skd->...kgqs', qg, k.astype(F32)) * (HD_A ** -0.5)
    bias = rel_table.astype(F32)[_t5_bucket(dist)]
    bias = jnp.moveaxis(bias, -1, -3)
    bias = bias.reshape((HA_KV, G) + bias.shape[-2:])
    s = jnp.where(valid[..., None, None, :, :], s + bias, -jnp.inf)
    sink = sinks.astype(F32).reshape(HA_KV, G, 1, 1)
    m = jnp.maximum(jnp.max(s, axis=-1, keepdims=True), sink)
    p = jnp.exp(s - m)
    p = p / (jnp.sum(p, axis=-1, keepdims=True) + jnp.exp(sink - m))
    o = jnp.einsum('...kgqs,...skd->...qkgd', p, v.astype(F32))
    return o.reshape(o.shape[:-3] + (HA_Q * HD_A,))


def _attn_prompt(q, k, v, sinks, rel_table):
    B, T = q.shape[0], q.shape[1]
    nb = T // WINDOW
    qb = q.reshape(B, nb, WINDOW, HA_Q, HD_A)
    pad = jnp.zeros((B, WINDOW, HA_KV, HD_A), k.dtype)
    kb = jnp.concatenate([pad, k], axis=1).reshape(B, nb + 1, WINDOW, HA_KV, HD_A)
    vb = jnp.concatenate([pad.astype(v.dtype), v], axis=1).reshape(B, nb + 1, WINDOW, HA_KV, HD_A)
    kk = jnp.concatenate([kb[:, :-1], kb[:, 1:]], axis=2)
    vv = jnp.concatenate([vb[:, :-1], vb[:, 1:]], axis=2)
    qi = jnp.arange(WINDOW)
    kj = jnp.arange(2 * WINDOW) - WINDOW
    dist = qi[:, None] - kj[None, :]
    kpos = jnp.arange(nb)[:, None] * WINDOW + kj[None, :]
    valid = (dist >= 0) & (dist < WINDOW) & (kpos[:, None, :] >= 0)
    o = _window_attn(qb, kk, vv, dist, valid, sinks, rel_table)
    return o.reshape(B, T, HA_Q * HD_A)


def _attn_sample(q, k_new, v_new, k_buf, v_buf, sinks, rel_table):
    Wb = k_buf.shape[1]
    T = q.shape[1]
    kk = jnp.concatenate([k_buf.astype(k_new.dtype), k_new], axis=1)
    vv = jnp.concatenate([v_buf.astype(v_new.dtype), v_new], axis=1)
    qpos = PAST_LEN + jnp.arange(T)
    kpos = PAST_LEN - Wb + jnp.arange(Wb + T)
    dist = qpos[:, None] - kpos[None, :]
    valid = (dist >= 0) & (dist < WINDOW)
    o = _window_attn(q, kk, vv, dist, valid, sinks, rel_table)
    return o, kk[:, T:], vv[:, T:]


def _xpos_rotate(x, pos):
    d = x.shape[-1]
    theta = 1.0 / (10000.0 ** jnp.linspace(0.0, 1.0, d // 2, dtype=F32))
    ang = pos.astype(F32)[:, None] * theta[None, :]
    sin = jnp.sin(ang)[:, None, :]
    cos = jnp.cos(ang)[:, None, :]
    x1, x2 = x[..., 0::2], x[..., 1::2]
    return jnp.stack([x1 * cos - x2 * sin, x1 * sin + x2 * cos], axis=-1).reshape(x.shape)


def _retention_chunk(S, q, k, v, log_gamma):
    L = q.shape[1]
    i = jnp.arange(L, dtype=F32)
    diff = i[:, None] - i[None, :]
    decay = jnp.where(diff >= 0, jnp.exp(log_gamma[:, None, None] * jnp.maximum(diff, 0.0)), 0.0)
    inner = jnp.einsum('blhd,bmhd->bhlm', q, k) * decay
    o = jnp.einsum('bhlm,bmhe->blhe', inner, v)
    q_dec = jnp.exp(log_gamma[None, :] * (i[:, None] + 1.0))
    o = o + jnp.einsum('blhd,bhde->blhe', q * q_dec[None, :, :, None], S)
    k_dec = jnp.exp(log_gamma[None, :] * (L - 1.0 - i[:, None]))
    S_new = (jnp.exp(log_gamma * L)[None, :, None, None] * S
             + jnp.einsum('blhd,blhe->bhde', k * k_dec[None, :, :, None], v))
    return S_new, o


def _ssd_chunk(h, x, dt, Bm, Cm, A):
    L = x.shape[1]
    acum = jnp.cumsum(dt * A, axis=1)
    seg = acum[:, :, None, :] - acum[:, None, :, :]
    causal = (jnp.arange(L)[:, None] >= jnp.arange(L)[None, :])[None, :, :, None]
    Lmat = jnp.where(causal, jnp.exp(jnp.where(causal, seg, 0.0)), 0.0)
    Bh = jnp.repeat(Bm, HC // G_C, axis=2)
    Ch = jnp.repeat(Cm, HC // G_C, axis=2)
    cb = jnp.einsum('bihn,bjhn->bijh', Ch, Bh)
    y = jnp.einsum('bijh,bjh,bjhp->bihp', cb * Lmat, dt, x)
    y = y + jnp.einsum('bihn,bhpn->bihp', Ch * jnp.exp(acum)[..., None], h)
    w = jnp.exp(acum[:, -1:, :] - acum) * dt
    h_new = (jnp.exp(acum[:, -1, :])[:, :, None, None] * h
             + jnp.einsum('bjh,bjhp,bjhn->bhpn', w, x, Bh))
    return h_new, y


def _scan_chunks(step, carry, xs, chunk):
    B, T = xs[0].shape[0], xs[0].shape[1]
    nc = T // chunk
    xs_c = tuple(jnp.moveaxis(a.reshape((B, nc, chunk) + a.shape[2:]), 1, 0) for a in xs)
    carry, ys = lax.scan(lambda c, xx: step(c, *xx), carry, xs_c)
    ys = jnp.moveaxis(ys, 0, 1)
    return carry, ys.reshape((B, T) + ys.shape[3:])


def _dwconv(xpad, w, b):
    T = xpad.shape[1] - (CONV_W - 1)
    out = b
    for i in range(CONV_W):
        out = out + xpad[:, i:i + T] * w[i]
    return out


def _layer(x, c, p, rel_table, state, is_prompt):
    (sinks, n1, n2, aw, ab, w_in, cw, cbias, dtb, alog, dsk, snw, w_out, w_up, w_down) = p
    B_, T = x.shape[0], x.shape[1]
    xdt = x.dtype
    mod = jnp.dot(jax.nn.silu(c), aw) + ab
    sh1, sc1, g1, sh2, sc2, g2 = jnp.split(mod[:, None, :], 6, axis=-1)
    h = _rmsnorm(x, n1) * (1 + sc1) + sh1
    aq, ak, av, bq, bk, bv, bg, cz, cxbc, cdt, gts = _split_proj(jnp.dot(h, w_in))

    aq = aq.reshape(B_, T, HA_Q, HD_A)
    ak = ak.reshape(B_, T, HA_KV, HD_A)
    av = av.reshape(B_, T, HA_KV, HD_A)
    if is_prompt:
        oa = _attn_prompt(aq, ak, av, sinks, rel_table)
        kbuf, vbuf = ak[:, T - WINDOW:], av[:, T - WINDOW:]
    else:
        oa, kbuf, vbuf = _attn_sample(aq, ak, av, state[0], state[1], sinks, rel_table)

    pos = (0 if is_prompt else PAST_LEN) + jnp.arange(T)
    log_gamma = jnp.log(1.0 - 2.0 ** (-5.0 - jnp.arange(HB, dtype=F32)))
    qb = _xpos_rotate(bq.reshape(B_, T, HB, DK_B).astype(F32), pos)
    kb = _xpos_rotate(bk.reshape(B_, T, HB, DK_B).astype(F32) * (DK_B ** -0.5), pos)
    vb = bv.reshape(B_, T, HB, DV_B).astype(F32)
    ret_step = lambda S, q, k, v: _retention_chunk(S, q, k, v, log_gamma)
    if is_prompt:
        S_ret, ob = _scan_chunks(ret_step, jnp.zeros((B_, HB, DK_B, DV_B), F32), (qb, kb, vb), RET_CHUNK)
    else:
        S_ret, ob = ret_step(state[2].astype(F32), qb, kb, vb)
    ob = jax.nn.silu(bg.astype(F32)) * _rms(ob).reshape(B_, T, HB * DV_B)

    if is_prompt:
        hist = jnp.zeros((B_, CONV_W - 1, CONV_DIM), cxbc.dtype)
    else:
        hist = state[4].astype(cxbc.dtype)
    xpad = jnp.concatenate([hist, cxbc], axis=1)
    conv_new = xpad[:, T:]
    xbc = jax.nn.silu(_dwconv(xpad, cw, cbias).astype(F32))
    xc = xbc[..., :D_INNER_C].reshape(B_, T, HC, HD_C)
    Bc = xbc[..., D_INNER_C:D_INNER_C + G_C * N_C].reshape(B_, T, G_C, N_C)
    Cc = xbc[..., D_INNER_C + G_C * N_C:].reshape(B_, T, G_C, N_C)
    dtc = jax.nn.softplus(cdt.astype(F32) + dtb.astype(F32))
    A = -jnp.exp(alog.astype(F32))
    ssd_step = lambda hs, xx, dd, bb, cc: _ssd_chunk(hs, xx, dd, bb, cc, A)
    if is_prompt:
        h_ssm, yc = _scan_chunks(ssd_step, jnp.zeros((B_, HC, HD_C, N_C), F32), (xc, dtc, Bc, Cc), SSD_CHUNK)
    else:
        h_ssm, yc = ssd_step(state[3].astype(F32), xc, dtc, Bc, Cc)
    yc = yc + dsk.astype(F32)[:, None] * xc
    yc = yc.reshape(B_, T, D_INNER_C) * jax.nn.silu(cz.astype(F32))
    oc = _rms(yc.reshape(B_, T, G_C, D_INNER_C // G_C)).reshape(B_, T, D_INNER_C) * snw.astype(F32)

    ga, gb, gc = jnp.split(jax.nn.sigmoid(gts.astype(F32)), 3, axis=-1)
    mix = (ga * oa + gb * ob + gc * oc).astype(xdt)
    x = x + g1 * jnp.dot(mix, w_out)

    h2 = _rmsnorm(x, n2) * (1 + sc2) + sh2
    x = x + g2 * jnp.dot(jnp.square(jax.nn.relu(jnp.dot(h2, w_up))), w_down)

    if is_prompt:
        dts = (xdt, xdt, xdt, xdt, xdt)
    else:
        dts = tuple(s.dtype for s in state)
    new_state = (kbuf.astype(dts[0]), vbuf.astype(dts[1]), S_ret.astype(dts[2]),
                 h_ssm.astype(dts[3]), conv_new.astype(dts[4]))
    return x, new_state


def setup_inputs(seed: int = 0) -> dict:
    key = jax.random.key(seed)
    ks = jax.random.split(key, 32)
    nrm = lambda k, shape, s: jax.random.normal(k, shape, F32) * s
    win_buf = min(WINDOW, PAST_LEN)
    dt0 = jnp.exp(jax.random.uniform(ks[20], (DEPTH, HC), F32, math.log(1e-3), math.log(1e-1)))
    return {
        "x_prompt": nrm(ks[0], (BATCH, SEQ, D_MODEL), 1.0),
        "x_sample": nrm(ks[1], (DEC_BATCH, DEC_SEQ, D_MODEL), 1.0),
        "cache_win_k": nrm(ks[2], (DEPTH, DEC_BATCH, win_buf, HA_KV, HD_A), 1.0),
        "cache_win_v": nrm(ks[3], (DEPTH, DEC_BATCH, win_buf, HA_KV, HD_A), 1.0),
        "state_ret": nrm(ks[4], (DEPTH, DEC_BATCH, HB, DK_B, DV_B), 0.1),
        "state_ssm": nrm(ks[5], (DEPTH, DEC_BATCH, HC, HD_C, N_C), 0.1),
        "state_conv": nrm(ks[6], (DEPTH, DEC_BATCH, CONV_W - 1, CONV_DIM), 1.0),
        "c_prompt": nrm(ks[7], (BATCH, D_MODEL), 1.0),
        "c_sample": nrm(ks[8], (DEC_BATCH, D_MODEL), 1.0),
        "rel_bias_table": nrm(ks[9], (NUM_BUCKETS, HA_Q), 0.5),
        "attn_sinks": nrm(ks[10], (DEPTH, HA_Q), 1.0),
        "norm1_w": 1.0 + nrm(ks[11], (DEPTH, D_MODEL), 0.01),
        "norm2_w": 1.0 + nrm(ks[12], (DEPTH, D_MODEL), 0.01),
        "ada_w": nrm(ks[13], (DEPTH, D_MODEL, 6 * D_MODEL), 0.5 * D_MODEL ** -0.5),
        "ada_b": nrm(ks[14], (DEPTH, 6 * D_MODEL), 0.02),
        "w_in": nrm(ks[15], (DEPTH, D_MODEL, D_IN), D_MODEL ** -0.5),
        "conv_w": nrm(ks[16], (DEPTH, CONV_W, CONV_DIM), CONV_W ** -0.5),
        "conv_b": nrm(ks[17], (DEPTH, CONV_DIM), 0.02),
        "dt_bias": dt0 + jnp.log(-jnp.expm1(-dt0)),
        "A_log": jnp.log(jax.random.uniform(ks[18], (DEPTH, HC), F32, 1.0, 16.0)),
        "D_skip": 1.0 + nrm(ks[19], (DEPTH, HC), 0.1),
        "ssm_norm_w": 1.0 + nrm(ks[21], (DEPTH, D_INNER_C), 0.01),
        "w_out": nrm(ks[22], (DEPTH, D_MODEL, D_MODEL), D_MODEL ** -0.5),
        "w_up": nrm(ks[23], (DEPTH, D_MODEL, D_FF), D_MODEL ** -0.5),
        "w_down": nrm(ks[24], (DEPTH, D_FF, D_MODEL), D_FF ** -0.5),
        "final_norm_w": 1.0 + nrm(ks[25], (D_MODEL,), 0.01),
    }


def reference(x_prompt, x_sample, cache_win_k, cache_win_v, state_ret, state_ssm, state_conv,
              c_prompt, c_sample, rel_bias_table, attn_sinks, norm1_w, norm2_w, ada_w, ada_b,
              w_in, conv_w, conv_b, dt_bias, A_log, D_skip, ssm_norm_w, w_out, w_up, w_down,
              final_norm_w):
    xp, xs = x_prompt, x_sample
    sp, ss = [], []
    for l in range(DEPTH):
        p = (attn_sinks[l], norm1_w[l], norm2_w[l], ada_w[l], ada_b[l], w_in[l], conv_w[l], conv_b[l],
             dt_bias[l], A_log[l], D_skip[l], ssm_norm_w[l], w_out[l], w_up[l], w_down[l])
        xp, st_p = _layer(xp, c_prompt, p, rel_bias_table, None, True)
        xs, st_s = _layer(xs, c_sample, p, rel_bias_table,
                          (cache_win_k[l], cache_win_v[l], state_ret[l], state_ssm[l], state_conv[l]), False)
        sp.append(st_p)
        ss.append(st_s)
    stk = lambda sts, i: jnp.stack([s[i] for s in sts], axis=0)
    y_prompt = _rmsnorm(xp, final_norm_w)
    y_sample = _rmsnorm(xs, final_norm_w)
    return (y_prompt, y_sample,
            stk(sp, 0), stk(sp, 1), stk(sp, 2), stk(sp, 3), stk(sp, 4),
            stk(ss, 0), stk(ss, 1), stk(ss, 2), stk(ss, 3), stk(ss, 4))
```

```python
import math
from contextlib import ExitStack
import numpy as np
import concourse.bass as bass
import concourse.mybir as mybir
from concourse.bass_utils import run_bass_kernel_spmd

F32 = mybir.dt.float32
BF16 = mybir.dt.bfloat16
AF = mybir.ActivationFunctionType
ALU = mybir.AluOpType
AX = mybir.AxisListType

NCORE = 8
D = 1024
SEQ = 2048
NB = 2
NS = 16
DEPTH = 2
PAST = 16384
DIN = 10256
DFF = 4096
EPS = 1e-6
NEG = -30000.0
RUN_B, RUN_T, RUN_SAMPLE, RUN_CORES = NB, 16, True, NCORE
RUN_STAGE = 9
DEBUG = False
T_LAST = 15
TRACE = False
DO_KV = True
DO_CONV = True
DBG_T = 1
RUN_SUB = 99
QORDER = [0, 4, 1, 5, 2, 6, 3, 7, 8, 12, 9, 13, 10, 14, 11, 15]
O_AQ, O_AK, O_AV, O_BQ, O_BK, O_BV, O_BG, O_CZ, O_XBC, O_DT, O_G = 0, 1024, 1280, 1536, 2048, 2560, 3584, 4608, 5632, 7168, 7184


class Buf:
    __slots__ = ("name", "w", "r", "excl")

    def __init__(self, name):
        self.name = name
        self.w = None
        self.r = []
        self.excl = False


class TT:
    def __init__(self, t, name, nb=1):
        self.t = t
        self.b = [Buf(name + str(i)) for i in range(nb)]

    def __getitem__(self, k):
        return self.t[k]


class Sync:
    def __init__(self, nc, stack, n_dma_sems=24, n_pool_sems=70):
        self.nc = nc
        self.eng = {"pe": nc.tensor, "dve": nc.vector, "act": nc.scalar, "pool": nc.gpsimd, "sp": nc.sync}
        self.sems = {}
        self.cnt = {}
        for e in self.eng:
            self.sems[e] = stack.enter_context(nc.semaphore("s_" + e))
            self.cnt[e] = 0
        self.dsems = []
        self.n_sp = n_dma_sems
        for i in range(n_dma_sems + n_pool_sems):
            k = "d%d" % i
            self.sems[k] = stack.enter_context(nc.semaphore("dq_%d" % i))
            self.cnt[k] = 0
            self.dsems.append(k)
        self.dnext = 0
        self.dnext_pool = 0
        self.waited = {e: {} for e in self.eng}

    def _wait(self, e, ev):
        if ev is None:
            return
        k, v = ev
        if self.waited[e].get(k, 0) >= v:
            return
        self.eng[e].wait_ge(self.sems[k], v)
        self.waited[e][k] = v

    @staticmethod
    def _bl(xs):
        out = []
        for x in xs:
            if isinstance(x, TT):
                out.extend(x.b)
            elif isinstance(x, Buf):
                out.append(x)
            else:
                out.extend(x)
        return out

    def _deps(self, e, reads, writes):
        for b in reads:
            self._wait(e, b.w)
            if b.excl:
                for ev in b.r:
                    if ev[0] != e:
                        self._wait(e, ev)
        for b in writes:
            if b.w is not None and (b.w[0] != e or e != "pe"):
                self._wait(e, b.w)
            for ev in b.r:
                if ev[0] != e or e != "pe":
                    self._wait(e, ev)

    def op(self, e, fn, r=(), w=(), serial=False):
        reads, writes = self._bl(r), self._bl(w)
        self._deps(e, reads, writes)
        if serial and self.cnt[e] > 0:
            self._wait(e, (e, self.cnt[e]))
        ins = fn(self.eng[e])
        self.cnt[e] += 1
        ins.then_inc(self.sems[e], 1)
        ev = (e, self.cnt[e])
        for b in reads:
            b.r = [x for x in b.r if x[0] != e] + [ev]
        for b in writes:
            b.w = ev
            b.r = []
        return ins

    def dma(self, q, out, in_, r=(), w=(), **kw):
        reads, writes = self._bl(r), self._bl(w)
        half = self.n_sp
        if q == "pool":
            k = self.dsems[half + self.dnext_pool]
            self.dnext_pool = (self.dnext_pool + 1) % (len(self.dsems) - half)
        else:
            k = self.dsems[self.dnext]
            self.dnext = (self.dnext + 1) % half
        if self.cnt[k] > 0:
            self._wait(q, (k, self.cnt[k]))
        self._deps(q, reads, writes)
        ins = self.eng[q].dma_start(out=out, in_=in_, **kw)
        self.cnt[k] += 16
        ins.then_inc(self.sems[k], 16)
        ev = (k, self.cnt[k])
        for b in reads:
            b.r = b.r + [ev]
        for b in writes:
            b.w = ev
            b.r = []
        return ins

    def barrier(self):
        evs = [(k, v) for k, v in self.cnt.items() if v > 0]
        for e in self.eng:
            for ev in evs:
                if ev[0] != e:
                    self._wait(e, ev)

    def finish(self, e="sp"):
        for k, v in self.cnt.items():
            if v > 0 and k != e:
                self._wait(e, (k, v))


def host_consts():
    c = {}
    theta = (1.0 / (10000.0 ** np.linspace(0.0, 1.0, 32, dtype=np.float32))).astype(np.float32)
    pos = np.arange(SEQ, dtype=np.float32)
    ang = (pos[:, None] * theta[None, :]).astype(np.float32)
    c["cosp"] = np.ascontiguousarray(np.cos(ang).astype(np.float32).reshape(16, 128, 32).transpose(1, 0, 2))
    c["sinp"] = np.ascontiguousarray(np.sin(ang).astype(np.float32).reshape(16, 128, 32).transpose(1, 0, 2))
    angs = (np.float32(PAST) * theta).astype(np.float32)
    c["coss"] = np.tile(np.cos(angs).astype(np.float32)[None, :], (128, 1))
    c["sins"] = np.tile(np.sin(angs).astype(np.float32)[None, :], (128, 1))
    lg = np.log(1.0 - 2.0 ** (-5.0 - np.arange(8, dtype=np.float64)))
    i = np.arange(128, dtype=np.float64)
    diff = i[None, :] - i[:, None]
    dec = np.where(diff[None] >= 0, np.exp(lg[:, None, None] * np.maximum(diff[None], 0.0)), 0.0) * 0.125
    c["decT"] = np.ascontiguousarray(dec.transpose(1, 0, 2)).astype(np.float32)
    c["qdec"] = np.exp(lg[None, :] * (i[:, None] + 1.0)).astype(np.float32)
    c["kdec"] = (np.exp(lg[None, :] * (127.0 - i[:, None])) * 0.125).astype(np.float32)
    gl = np.exp(lg * 128.0)
    gL = np.zeros((128, 4), np.float64)
    for h in range(8):
        gL[(h % 2) * 64:(h % 2) * 64 + 64, h // 2] = gl[h]
    c["gL"] = gL.astype(np.float32)
    c["g1p"] = np.tile(np.exp(lg), 16).astype(np.float32).reshape(128, 1)
    jj = np.arange(128)
    c["tri"] = (jj[:, None] <= jj[None, :]).astype(np.float32)
    c["mneg"] = np.where(jj[:, None] <= jj[None, :], 0.0, NEG).astype(np.float32)
    return c


def bucket_of(d):
    d = np.asarray(d)
    nf = np.maximum(d, 1).astype(np.float32)
    large = 16 + (np.log(nf / np.float32(16)) / np.float32(math.log(128 / 16)) * np.float32(16)).astype(np.int32)
    large = np.minimum(large, 31)
    return np.where(d < 16, d, large)


def build():
    nc = bass.Bass("TRN2", target_bir_lowering=False)
    dt_in = lambda n, s, dt=F32: nc.dram_tensor(n, list(s), dt, kind="ExternalInput").ap()
    dt_out = lambda n, s: nc.dram_tensor(n, list(s), F32, kind="ExternalOutput").ap()
    dt_int = lambda n, s, dt=F32: nc.dram_tensor(n, list(s), dt, kind="Internal").ap()
    xp = dt_in("xp", [NB, SEQ, D]); xs = dt_in("xs", [NS, D]); cc = dt_in("cc", [NB + NS, D])
    cache_k = dt_in("cache_k", [DEPTH, NS, 128, 256]); cache_v = dt_in("cache_v", [DEPTH, NS, 128, 256])
    st_ret = dt_in("st_ret", [DEPTH, NS, 8, 64, 128]); st_ssm = dt_in("st_ssm", [DEPTH, NS, 16, 64, 128])
    st_conv = dt_in("st_conv", [DEPTH, NS, 3, 1536])
    rel_tab = dt_in("rel_tab", [32, 16]); sinks = dt_in("sinks", [DEPTH, 16])
    n1w = dt_in("n1w", [DEPTH, D]); n2w = dt_in("n2w", [DEPTH, D])
    ada_w = dt_in("ada_w", [DEPTH, D, 6 * D]); ada_b = dt_in("ada_b", [DEPTH, 6 * D])
    w_in = dt_in("w_in", [DEPTH, D, DIN]); conv_w = dt_in("conv_w", [DEPTH, 4, 1536]); conv_b = dt_in("conv_b", [DEPTH, 1536])
    dt_bias = dt_in("dt_bias", [DEPTH, 16]); a_log = dt_in("a_log", [DEPTH, 16]); d_skip = dt_in("d_skip", [DEPTH, 16])
    snw = dt_in("snw", [DEPTH, D]); w_out = dt_in("w_out", [DEPTH, D, D]); w_up = dt_in("w_up", [DEPTH, D, DFF])
    w_down = dt_in("w_down", [DEPTH, DFF, D]); fnw = dt_in("fnw", [D])
    hc = {k: dt_in("c_" + k, v.shape) for k, v in host_consts().items()}

    y_p = dt_out("y_p", [NB, SEQ, D]); y_s = dt_out("y_s", [NS, D])
    wk_p = dt_out("wk_p", [DEPTH, NB, 128, 256]); wv_p = dt_out("wv_p", [DEPTH, NB, 128, 256])
    ret_p = dt_out("ret_p", [DEPTH, NB, 8, 64, 128]); ssm_p = dt_out("ssm_p", [DEPTH, NB, 16, 64, 128])
    conv_p = dt_out("conv_p", [DEPTH, NB, 3, 1536])
    wk_s = dt_out("wk_s", [DEPTH, NS, 128, 256]); wv_s = dt_out("wv_s", [DEPTH, NS, 128, 256])
    ret_s = dt_out("ret_s", [DEPTH, NS, 8, 64, 128]); ssm_s = dt_out("ssm_s", [DEPTH, NS, 16, 64, 128])
    conv_s = dt_out("conv_s", [DEPTH, NS, 3, 1536])
    dbg = dt_out("dbg", [3, 128, D]) if DEBUG else None

    winT = dt_int("winT", [DEPTH, 21, 128, 4096], BF16); adaT = dt_int("adaT", [DEPTH, 12, 128, 4096], BF16)
    woutT = dt_int("woutT", [DEPTH, 2, 128, 4096], BF16); wupT = dt_int("wupT", [DEPTH, 8, 128, 4096], BF16)
    wdownT = dt_int("wdownT", [DEPTH, 8, 128, 4096], BF16)
    vecx = dt_int("vecx", [16, 384]); vecf = dt_int("vecf", [16, 384]); modd = dt_int("modd", [DEPTH, NB + NS, 6 * D]); zsd = dt_int("zsd", [NS, DIN])
    osd = dt_int("osd", [3, NS, D])

    with ExitStack() as st:
        S = Sync(nc, st)
        sb = lambda n, s, dt=F32, nb=1: TT(st.enter_context(nc.sbuf_tensor(n, list(s), dt)), n, nb)
        def pb(n, s, dt=F32, nb=1):
            t_ = TT(st.enter_context(nc.psum_tensor(n, list(s), dt)), n, nb)
            for b_ in t_.b:
                b_.excl = True
            return t_
        dram_b = {n: Buf(n) for n in ["winb", "adab", "woutb", "wupb", "wdownb", "vecx", "vecf", "modd", "zsd", "osd", "out"]}
        OUT = dram_b["out"]

        def cast(key, dst, src):
            b_ = Buf(key)
            dram_b.setdefault(key, []).append(b_)
            S.dma("pool", dst, src, w=[b_])

        def tl(T_, l, ti, n=512):
            return T_[l, ti, :, 0:8 * n].rearrange("p (k n) -> p k n", k=8)

        def srcv(w, l, r0, c0, n):
            return w[l, r0:r0 + 1024, c0:c0 + n].rearrange("(k p) n -> p k n", p=128)

        for l in range(DEPTH):
            for ct in range(12):
                cast("adab%d" % l, tl(adaT, l, ct), srcv(ada_w, l, 0, ct * 512, 512))
        def cast_layer(l):
            for j, h in enumerate(QORDER):
                cast("winb%d" % l, tl(winT, l, j // 8)[:, :, (j % 8) * 64:(j % 8 + 1) * 64], srcv(w_in, l, 0, h * 64, 64))
            for ti in range(2, 14):
                cast("winb%d" % l, tl(winT, l, ti), srcv(w_in, l, 0, ti * 512, 512))
            for j in range(6):
                cast("winb%d" % l, tl(winT, l, 14 + j), srcv(w_in, l, 0, O_G + j * 512, 512))
            cast("winb%d" % l, tl(winT, l, 20, 16), srcv(w_in, l, 0, O_DT, 16))
            for j in range(2):
                cast("woutb%d" % l, tl(woutT, l, j), srcv(w_out, l, 0, j * 512, 512))
            for j in range(8):
                cast("wupb%d" % l, tl(wupT, l, j), srcv(w_up, l, 0, j * 512, 512))
            for ct in range(2):
                for sl in range(4):
                    cast("wdownb%d" % l, tl(wdownT, l, ct * 4 + sl), srcv(w_down, l, sl * 1024, ct * 512, 512))


        cast_done = {1: False}
        cast_layer(0)

        identf = sb("identf", [128, 128])
        ident = sb("ident", [128, 128], BF16)
        wbuf = [sb("wbuf%d" % i, [128, 8, 512], BF16) for i in range(3)]
        PD = [pb("PD%d" % i, [128, 512]) for i in range(2)]
        PTb = pb("PTb", [128, 8, 128], BF16, nb=2)
        PS = pb("PS", [128, 512])
        PO = pb("PO", [128, 2048], F32, nb=4)
        xres = sb("xres", [128, D])
        sq = sb("sq", [128, D])
        st8 = sb("st8", [128, 32])
        xn = sb("xn", [128, D], BF16)
        tmp = sb("tmp", [128, D])
        tmp2 = sb("tmp2", [128, D])
        scT = sb("scT", [128, 8, NB + NS], BF16)
        n1c = sb("n1c", [128, DEPTH, 8])
        n2c = sb("n2c", [128, DEPTH, 8])
        cwc = sb("cwc", [128, DEPTH, 4, 12])
        cbc = sb("cbc", [128, DEPTH, 12])
        dtb = sb("dtb", [128, DEPTH, 16])
        Arow = sb("Arow", [128, DEPTH, 16])
        Dsk = sb("Dsk", [128, DEPTH, 16])
        esink = sb("esink", [128, DEPTH, 16])
        snwb = sb("snwb", [128, DEPTH, D])
        fnwb = sb("fnwb", [128, D])
        hT = sb("hT", [128, 8, 128], BF16)
        hT.b = [Buf('hT_%d' % i_) for i_ in range(8)]
        mix = sb("mix", [128, D])
        g1bL = [sb("g1b%d" % l, [128, D]) for l in range(DEPTH)]
        g2bL = [sb("g2b%d" % l, [128, D]) for l in range(DEPTH)]
        modcL = [sb("modc%d" % l, [128, 4, 8]) for l in range(DEPTH)]
        A1L = [sb("A1%d" % l, [128, 8]) for l in range(DEPTH)]
        A2L = [sb("A2%d" % l, [128, 8]) for l in range(DEPTH)]
        gates = sb("gates", [128, 3, D])
        gates.b = [Buf('gates_%d' % i_) for i_ in range(6)]
        bgs = sb("bgs", [128, D])
        bgs.b = [Buf('bgs_%d' % i_) for i_ in range(2)]
        uT = sb("uT", [128, 32, 128], BF16)
        uT.b = [Buf('uT_%d' % i_) for i_ in range(8)]
        stP = st.enter_context(ExitStack())
        sbP = lambda n, s, dt=F32, nb=1: TT(stP.enter_context(nc.sbuf_tensor(n, list(s), dt)), n, nb)
        tri = sbP("tri", [128, 128])
        ones = sbP("ones", [128, 128])
        mnegb = sbP("mnegb", [128, 128], BF16)
        Jb = sbP("Jb", [128, 128], BF16)
        decT = sbP("decT", [128, 8, 128])
        qdec = sbP("qdec", [128, 8])
        kdec = sbP("kdec", [128, 8])
        gL = sbP("gL", [128, 4])
        cosp = sbP("cosp", [128, 16, 32])
        sinp = sbP("sinp", [128, 16, 32])
        BT = sbP("BT", [128, 16, 2, 128], BF16)
        qT = sbP("qT", [128, 8, 128], BF16)
        qT.b = [Buf('qT_%d' % i_) for i_ in range(8)]
        kT = [sbP("kT%d" % l, [128, 2, 256], BF16) for l in range(DEPTH)]
        Vx = [sbP("Vx%d" % l, [128, 2, 4, 65], BF16) for l in range(DEPTH)]
        PTa = sbP("PTa", [128, 2, 512], BF16)
        rec = sbP("rec", [128, 16])
        rtq, rtk = [], []
        for i_ in range(4):
            r_ = TT(tmp.t[:, i_ * 256:(i_ + 1) * 256].rearrange("p (h f) -> p h f", h=8), "rtq%d" % i_)
            r_.b = tmp.b
            rtq.append(r_)
            r_ = TT(tmp2.t[:, i_ * 256:(i_ + 1) * 256].rearrange("p (h f) -> p h f", h=8), "rtk%d" % i_)
            r_.b = tmp2.b
            rtk.append(r_)
        bqk = sbP("bqk", [128, 1024])
        bqk.b = [Buf('bqk_%d' % i_) for i_ in range(2)]
        czs = sbP("czs", [128, D])
        czs.b = [Buf('czs_%d' % i_) for i_ in range(2)]
        qrot = sbP("qrot", [128, 512], BF16)
        krot = sbP("krot", [128, 512], BF16)
        qd = sbP("qd", [128, 512], BF16)
        kd = sbP("kd", [128, 512], BF16)
        qkT = sbP("qkT", [128, 12, 128], BF16)
        bv = sbP("bv", [128, D], BF16)
        bv.b = [Buf('bv_%d' % i_) for i_ in range(2)]
        Sret = [sbP("Sret%d" % l, [128, 4, 128]) for l in range(DEPTH)]
        Sbf = [sbP("Sbf%d" % l, [128, 4, 128], BF16) for l in range(DEPTH)]
        xbcT = sbP("xbcT", [128, 12, 131])
        xbcT.b = [Buf('xbcT_%d' % i_) for i_ in range(12)]
        chist = [sbP("chist%d" % l, [128, 12, 3]) for l in range(DEPTH)]
        xcT = sbP("xcT", [128, 12, 128])
        bcTb = sbP("bcTb", [128, 4, 128], BF16)
        dts = sbP("dts", [128, 8, 16])
        tabT = TT(dts.t[0:16, 0:2, :].rearrange("p a b -> p (a b)"), "tabT"); tabT.b = dts.b
        vx = TT(bqk.t[0:16, 0:384], "vx"); vx.b = bqk.b
        dtAb = sbP("dtAb", [128, 4, 128])
        LT = sbP("LT", [128, 8, 128])
        MT = sbP("MT", [128, 16, 128], BF16)
        xdt = sbP("xdt", [128, D], BF16)
        xw = sbP("xw", [128, D], BF16)
        Btm = sbP("Btm", [128, 256], BF16)
        hS = [sbP("hS%d" % l, [128, D]) for l in range(DEPTH)]
        hSb = [sbP("hSb%d" % l, [128, D], BF16) for l in range(DEPTH)]
        S.op("pool", lambda e: e.memset(identf[:], 0.0), w=[identf])
        S.op("pool", lambda e: e.affine_select(out=identf[:], in_=identf[:], pattern=[[-1, 128]], compare_op=ALU.not_equal,
                                               fill=1.0, base=0, channel_multiplier=1), r=[identf], w=[identf])
        S.op("dve", lambda e: e.tensor_copy(out=ident[:], in_=identf[:]), r=[identf], w=[ident])
        S.op("pool", lambda e: e.memset(ones[:], 1.0), w=[ones])
        for t, k in [(tri, "tri"), (decT, "decT"), (qdec, "qdec"), (kdec, "kdec"), (gL, "gL"), (cosp, "cosp"), (sinp, "sinp")]:
            S.dma("sp", t[:], hc[k], w=[t])
        S.dma("pool", mnegb[:], hc["mneg"], w=[mnegb])

        vxf = TT(bqk.t[0:16, 384:768], "vxf"); vxf.b = bqk.b
        S.dma("sp", tabT[:], rel_tab.rearrange("b h -> h b"), w=[tabT], allow_slow_non_contiguous=True)
        S.op("dve", lambda e: e.memset(vx[:], NEG), w=[vx])
        S.op("dve", lambda e: e.memset(vxf[:], NEG), w=[vxf])
        bk = bucket_of(np.arange(128))
        d0 = 0
        while d0 < 128:
            d1 = d0
            while d1 + 1 < 128 and bk[d1 + 1] == bk[d0]:
                d1 += 1
            b = int(bk[d0])
            S.op("dve", lambda e, lo=255 - d1, hi=255 - d0 + 1, b=b: e.tensor_scalar(
                out=vx[:, lo:hi], in0=vx[:, lo:hi], scalar1=0.0, scalar2=tabT[:, b:b + 1], op0=ALU.mult, op1=ALU.add),
                r=[tabT, vx], w=[vx])
            S.op("dve", lambda e, lo=127 + d0, hi=127 + d1 + 1, b=b: e.tensor_scalar(
                out=vxf[:, lo:hi], in0=vxf[:, lo:hi], scalar1=0.0, scalar2=tabT[:, b:b + 1], op0=ALU.mult, op1=ALU.add),
                r=[tabT, vxf], w=[vxf])
            d0 = d1 + 1
        S.dma("sp", vecx, vx[:], r=[vx], w=[dram_b["vecx"]])
        S.dma("sp", vecf, vxf[:], r=[vxf], w=[dram_b["vecf"]])
        S.op("pool", lambda e: e.memset(ones[:], 0.0), w=[ones])
        S.op("pool", lambda e: e.affine_select(out=ones[:], in_=ones[:], pattern=[[1, 128]], compare_op=ALU.not_equal,
                                               fill=1.0, base=-127, channel_multiplier=1), r=[ones], w=[ones])
        S.op("dve", lambda e: e.tensor_copy(out=Jb[:], in_=ones[:]), r=[ones], w=[Jb])
        S.op("pool", lambda e: e.memset(ones[:], 1.0), r=[Jb], w=[ones])
        for h4 in range(4):
            for hh in range(4):
                h = 4 * h4 + hh
                for blk, off in ((1, 0), (0, 128)):
                    src = bass.AP(vecf.tensor, off + 384 * h, [[1, 128], [1, 128]])
                    S.dma("sp", tmp[:, (hh * 2 + blk) * 128:(hh * 2 + blk + 1) * 128], src, r=[dram_b["vecf"]], w=[tmp])
            S.op("dve", lambda e: e.tensor_scalar(out=xn[:], in0=tmp[:], scalar1=8.0, scalar2=None, op0=ALU.mult), r=[tmp], w=[xn])
            for hf2 in range(2):
                pd = PD[hf2]
                S.op("pe", lambda e, pd=pd, hf2=hf2: e.matmul(out=pd[:], lhsT=Jb[:], rhs=xn[:, hf2 * 512:(hf2 + 1) * 512], start=True, stop=True), r=[Jb, xn], w=[pd])
                S.op("act", lambda e, pd=pd, hf2=hf2, h4=h4: e.copy(out=BT[:, 4 * h4 + 2 * hf2:4 * h4 + 2 * hf2 + 2, :, :].rearrange("p h b q -> p (h b q)"), in_=pd[:]),
                     r=[pd], w=[BT])

        csb, csil, adabias, modt = tmp, xn, tmp2, sq
        sq.b = [Buf('sq_a'), Buf('sq_b')]
        xn.b = [Buf('xn_a'), Buf('xn_b')]
        sqh, xnh = sq.b, xn.b
        S.dma("sp", csb[0:NB + NS, :], cc, w=[csb])
        S.op("act", lambda e: e.activation(out=csil[0:NB + NS, :], in_=csb[0:NB + NS, :], func=AF.Silu), r=[csb], w=[csil])
        for k in range(8):
            S.op("pe", lambda e, k=k: e.transpose(out=PTb[0:128, k, 0:NB + NS], in_=csil[0:NB + NS, k * 128:(k + 1) * 128],
                                                  identity=ident[0:NB + NS, 0:NB + NS]), r=[csil, ident], w=[PTb])
        S.op("dve", lambda e: e.tensor_copy(out=scT[:], in_=PTb[:, :, 0:NB + NS]), r=[PTb], w=[scT])
        R = NB + NS
        for l in range(DEPTH):
            for ct in range(12):
                wb_ = wbuf[ct % 3]
                S.dma("sp", wb_[:], tl(adaT, l, ct), r=[dram_b["adab%d" % l]], w=[wb_])
                S.dma("pool", adabias[0:R, 0:512], bass.AP(ada_b.tensor, l * 6 * D + ct * 512, [[0, R], [1, 512]]), w=[adabias])
                pd = PD[ct % 2]
                for k in range(8):
                    S.op("pe", lambda e, k=k, pd=pd, wb_=wb_: e.matmul(out=pd[0:R, :], lhsT=scT[:, k, :], rhs=wb_[:, k, :],
                                                                     start=(k == 0), stop=(k == 7)), r=[scT, wb_], w=[pd])
                S.op("dve", lambda e, pd=pd: e.tensor_tensor(out=modt[0:R, 0:512], in0=pd[0:R, :], in1=adabias[0:R, 0:512], op=ALU.add),
                     r=[pd, adabias], w=[modt])
                S.dma("sp", modd[l, :, ct * 512:(ct + 1) * 512], modt[0:R, 0:512], r=[modt], w=[dram_b["modd"]])

        for l in range(DEPTH):
            S.dma("sp", n1c[:, l, :], n1w[l].rearrange("(k p) -> p k", p=128), w=[n1c], allow_slow_non_contiguous=True)
            S.dma("sp", n2c[:, l, :], n2w[l].rearrange("(k p) -> p k", p=128), w=[n2c], allow_slow_non_contiguous=True)
        for l in range(DEPTH):
            for i in range(4):
                S.dma("sp", cwc[:, l, i, :], conv_w[l, i].rearrange("(j p) -> p j", p=128), w=[cwc], allow_slow_non_contiguous=True)
            S.dma("sp", cbc[:, l, :], conv_b[l].rearrange("(j p) -> p j", p=128), w=[cbc], allow_slow_non_contiguous=True)
        bc16 = lambda t, l: bass.AP(t.tensor, l * 16, [[0, 128], [1, 16]])
        for l in range(DEPTH):
            S.dma("sp", dtb[:, l, :], bc16(dt_bias, l), w=[dtb])
            S.dma("sp", Arow[:, l, :], bc16(a_log, l), w=[Arow])
            S.dma("sp", Dsk[:, l, :], bc16(d_skip, l), w=[Dsk])
            S.dma("sp", esink[:, l, :], bc16(sinks, l), w=[esink])
        S.op("act", lambda e: e.activation(out=Arow[:], in_=Arow[:], func=AF.Exp), r=[Arow], w=[Arow])
        S.op("dve", lambda e: e.tensor_scalar(out=Arow[:], in0=Arow[:], scalar1=-1.0, scalar2=None, op0=ALU.mult), r=[Arow], w=[Arow])
        S.op("act", lambda e: e.activation(out=esink[:], in_=esink[:], func=AF.Exp), r=[esink], w=[esink])
        for l in range(DEPTH):
            S.dma("sp", snwb[:, l, :], bass.AP(snw.tensor, l * D, [[0, 128], [1, D]]), w=[snwb])
        S.dma("sp", fnwb[:], bass.AP(fnw.tensor, 0, [[0, 128], [1, D]]), w=[fnwb])

        xcs = sq
        innT = TT(MT.t[:, 0:8, :], "innT"); innT.b = MT.b
        xres1 = sbP("xres1", [128, D])
        xresL = [xres, xres1]
        hT_main = hT
        hTm = [sbP("hTm%d" % i, [128, 8, 128], BF16) for i in range(2)]
        for h_ in hTm:
            h_.b = [Buf("hTm_%d" % i_) for i_ in range(8)]
        uTb = sbP("uTb", [128, 32, 128], BF16)
        uTb.b = [Buf('uTb_%d' % i_) for i_ in range(8)]
        uTL = [uT, uTb]

        wq = {"i": 0}

        def wload(src_ap, ncols=512, dep=()):
            wb_ = wbuf[wq["i"] % 3]
            wq["i"] += 1
            S.dma("sp", wb_[:, :, 0:ncols], src_ap, r=[dep], w=[wb_])
            return wb_

        def win_tile(l, c0, n=512):
            ti = 20 if c0 == O_DT else (14 + (c0 - O_G) // 512 if c0 >= O_G else c0 // 512)
            return wload(tl(winT, l, ti, n), n, dram_b["winb%d" % l]), None

        def rms_stats(xt, rows, col):
            S.op("act", lambda e: e.activation(out=sq[0:rows, :], in_=xt[0:rows, :], func=AF.Square, accum_out=st8[0:rows, col:col + 1]),
                 r=[xt], w=[sq, st8])
            S.op("act", lambda e: e.activation(out=st8[0:rows, col + 1:col + 2], in_=st8[0:rows, col:col + 1], func=AF.Ln,
                                               scale=1.0 / D, bias=EPS), r=[st8], w=[st8])
            S.op("act", lambda e: e.activation(out=st8[0:rows, col + 2:col + 3], in_=st8[0:rows, col + 1:col + 2], func=AF.Exp, scale=-0.5), r=[st8], w=[st8])
            return col + 2

        def norm_to_hT(A, shT, shj):
            sh = shT[:, shj, :]
            c = rms_stats(xres, 128, 0)
            S.op("dve", lambda e: e.tensor_scalar(out=xn[:], in0=xres[:], scalar1=st8[:, c:c + 1], scalar2=None, op0=ALU.mult),
                 r=[xres, st8], w=[xn])
            for k in range(8):
                S.op("pe", lambda e, k=k: e.transpose(out=PTb[:, k, :], in_=xn[:, k * 128:(k + 1) * 128], identity=ident[:]),
                     r=[xn, ident], w=[PTb])
            for k in range(8):
                S.op("act", lambda e, k=k: e.activation(out=hT[:, k, :], in_=PTb[:, k, :], func=AF.Identity,
                                                        scale=A[:, k:k + 1], bias=sh[:, k:k + 1]), r=[PTb, A, shT], w=[hT.b[k]])

        def mm_tm(pd, wb_, c0, n, src=None):
            src = src or hT
            for k in range(8):
                S.op("pe", lambda e, k=k: e.matmul(out=pd[:, 0:n], lhsT=src[:, k, :], rhs=wb_[:, k, c0:c0 + n],
                                                   start=(k == 0), stop=(k == 7)), r=[src, wb_], w=[pd])

        def mm_fm(pd, wb_, c0):
            for k in range(8):
                S.op("pe", lambda e, k=k: e.matmul(out=pd[:, 0:128], lhsT=wb_[:, k, c0:c0 + 128], rhs=hT[:, k, :],
                                                   start=(k == 0), stop=(k == 7)), r=[hT, wb_], w=[pd])

        pdi = {"i": 0}

        def nextpd():
            pdi["i"] += 1
            return PD[pdi["i"] % 2]

        def layer_chunk(l, b, t, last_layer):
            modc, A1, A2, g1b, g2b = modcL[l], A1L[l], A2L[l], g1bL[l], g2bL[l]
            if t == 0:
                for j, col in enumerate((0, 1, 3, 4)):
                    S.dma("pool", modc[:, j, :], modd[l, b, col * D:(col + 1) * D].rearrange("(k p) -> p k", p=128),
                          r=[dram_b["modd"]], w=[modc], allow_slow_non_contiguous=True)
                S.op("dve", lambda e: e.scalar_tensor_tensor(out=A1[:], in0=modc[:, 1, :], scalar=1.0, in1=n1c[:, l, :],
                                                             op0=ALU.add, op1=ALU.mult), r=[modc, n1c], w=[A1])
                S.op("dve", lambda e: e.scalar_tensor_tensor(out=A2[:], in0=modc[:, 3, :], scalar=1.0, in1=n2c[:, l, :],
                                                             op0=ALU.add, op1=ALU.mult), r=[modc, n2c], w=[A2])
                S.dma("pool", g1b[:], bass.AP(modd.tensor, (l * R + b) * 6 * D + 2 * D, [[0, 128], [1, D]]), r=[dram_b["modd"]], w=[g1b])
                S.dma("pool", g2b[:], bass.AP(modd.tensor, (l * R + b) * 6 * D + 5 * D, [[0, 128], [1, D]]), r=[dram_b["modd"]], w=[g2b])
            norm_to_hT(A1, modc, 0)

            def gPA():
                if t > 0:
                    S.op("pool", lambda e: e.tensor_copy(out=kT[l][:, :, 0:128], in_=kT[l][:, :, 128:256]), r=[kT[l]], w=[kT[l]])
                    S.op("pool", lambda e: e.tensor_copy(out=Vx[l][:, 0, :, :], in_=Vx[l][:, 1, :, :]), r=[Vx[l]], w=[Vx[l]])
                else:
                    S.op("pool", lambda e: e.memset(Vx[l][:, :, :, 64:65], 1.0), w=[Vx[l]])
                for half in range(2):
                    wb_, wr = win_tile(l, O_AQ + half * 512)
                    for i in range(4):
                        pd = nextpd()
                        mm_fm(pd, wb_, i * 128)
                        S.op("act", lambda e, pd=pd, i=i: e.copy(out=qT[:, half * 4 + i, :], in_=pd[:, 0:128]), r=[pd], w=[qT.b[half * 4 + i]])
                    yield
                wb_, wr = win_tile(l, O_AK)
                for i in range(2):
                    pd = nextpd()
                    mm_fm(pd, wb_, i * 128)
                    S.op("act", lambda e, pd=pd, i=i: e.copy(out=kT[l][:, i, 128:256], in_=pd[:, 0:128]), r=[pd], w=[kT[l]])
                pd = nextpd()
                mm_tm(pd, wb_, 256, 256)
                S.op("dve", lambda e, pd=pd: e.tensor_copy(out=Vx[l][:, 1, :, 0:64], in_=pd[:, 0:256].rearrange("p (g d) -> p g d", g=4)),
                     r=[pd], w=[Vx[l]])
                if t == T_LAST and DO_KV:
                    S.op("act", lambda e, pd=pd: e.copy(out=tmp2[:, 256:512], in_=pd[:, 0:256]), r=[pd], w=[tmp2])
                    pd2 = nextpd()
                    mm_tm(pd2, wb_, 0, 256)
                    S.op("act", lambda e, pd2=pd2: e.copy(out=tmp2[:, 0:256], in_=pd2[:, 0:256]), r=[pd2], w=[tmp2])
                    S.dma("pool", wk_p[l, b], tmp2[:, 0:256], r=[tmp2], w=[OUT])
                    S.dma("pool", wv_p[l, b], tmp2[:, 256:512], r=[tmp2], w=[OUT])
                yield
                for j in range(6):
                    wb_, wr = win_tile(l, O_G + j * 512)
                    pd = nextpd()
                    mm_tm(pd, wb_, 0, 512)
                    S.op("act", lambda e, pd=pd, j=j: e.activation(out=gates[:, j // 2, (j % 2) * 512:(j % 2) * 512 + 512], in_=pd[:],
                                                                   func=AF.Sigmoid), r=[pd], w=[gates.b[j]])
                    yield

            def gMA():
                blocks = (1,) if t == 0 else (0, 1)
                for g in range(4):
                    hf = (g % 2) * 64
                    for bi, blk in enumerate(blocks):
                        pd = nextpd()
                        S.op("pe", lambda e, g=g, hf=hf, blk=blk, pd=pd: e.matmul(
                            out=pd[:], lhsT=kT[l][hf:hf + 64, g // 2, blk * 128:(blk + 1) * 128],
                            rhs=qT[hf:hf + 64, (g // 2) * 4:(g // 2) * 4 + 4, :], start=True, stop=False), r=[kT[l], qT], w=[pd], serial=True)
                        S.op("pe", lambda e, g=g, blk=blk, pd=pd: e.matmul(
                            out=pd[:], lhsT=ident[:], rhs=BT[:, 4 * g:4 * g + 4, blk, :], start=False, stop=True), r=[ident, BT], w=[pd], serial=True)
                        S.op("act", lambda e, bi=bi, pd=pd: e.activation(out=PTa[:, bi, :], in_=pd[:], func=AF.Exp, scale=0.125), r=[pd], w=[PTa])
                    yield
                    for hq in range(4):
                        h = 4 * g + hq
                        hs = h % 8
                        for bi, blk in enumerate(blocks):
                            S.op("pe", lambda e, hs=hs, hq=hq, blk=blk, bi=bi: e.matmul(
                                out=PO[:, hs * 128:hs * 128 + 65], lhsT=PTa[:, bi, hq * 128:(hq + 1) * 128], rhs=Vx[l][:, blk, g, :],
                                start=(bi == 0), stop=(bi == len(blocks) - 1)), r=[PTa, Vx[l]], w=[PO.b[hs // 4]])
                    yield
                    if g % 2 == 1:
                        r_ = g // 2
                        po3 = PO[:, 0:1024].rearrange("p (h c) -> p h c", h=8)
                        S.op("dve", lambda e, r_=r_: e.tensor_tensor(out=rec[:, 8 * r_:8 * r_ + 8], in0=po3[:, :, 64], in1=esink[:, l, 8 * r_:8 * r_ + 8], op=ALU.add),
                             r=[PO.b[0], PO.b[1], esink], w=[rec])
                        S.op("dve", lambda e, r_=r_: e.reciprocal(out=rec[:, 8 * r_:8 * r_ + 8], in_=rec[:, 8 * r_:8 * r_ + 8]), r=[rec], w=[rec])
                        S.op("dve", lambda e, r_=r_: e.tensor_tensor(out=czs[:, r_ * 512:(r_ + 1) * 512].rearrange("p (h d) -> p h d", h=8), in0=po3[:, :, 0:64],
                                                              in1=rec[:, 8 * r_:8 * r_ + 8].unsqueeze(2).to_broadcast([128, 8, 64]), op=ALU.mult),
                             r=[PO.b[0], PO.b[1], rec], w=[czs])
                        yield
                S.op("dve", lambda e: e.tensor_tensor(out=czs[:], in0=czs[:], in1=gates[:, 0, :], op=ALU.mult), r=[czs, gates], w=[czs])
                S.op("dve", lambda e: e.tensor_tensor(out=mix[:], in0=mix[:], in1=czs[:], op=ALU.add), r=[czs, mix], w=[mix])

            def gPB():
                for j in range(2):
                    wb_, wr = win_tile(l, O_BQ + j * 512)
                    pd = nextpd()
                    mm_tm(pd, wb_, 0, 512)
                    S.op("act", lambda e, pd=pd, j=j: e.copy(out=bqk[:, j * 512:(j + 1) * 512], in_=pd[:]), r=[pd], w=[bqk.b[j]])
                for j in range(2):
                    wb_, wr = win_tile(l, O_BV + j * 512)
                    pd = nextpd()
                    mm_tm(pd, wb_, 0, 512)
                    S.op("act", lambda e, pd=pd, j=j: e.copy(out=bv[:, j * 512:(j + 1) * 512], in_=pd[:]), r=[pd], w=[bv.b[j]])
                    yield
                for j in range(2):
                    wb_, wr = win_tile(l, O_BG + j * 512)
                    pd = nextpd()
                    mm_tm(pd, wb_, 0, 512)
                    S.op("act", lambda e, pd=pd, j=j: e.activation(out=bgs[:, j * 512:(j + 1) * 512], in_=pd[:], func=AF.Silu), r=[pd], w=[bgs.b[j]])
                    yield
                S.op("pool", lambda e: e.tensor_tensor(out=bgs[:], in0=bgs[:], in1=gates[:, 1, :], op=ALU.mult), r=[bgs, gates], w=[bgs])

            def gMB():
                cb_ = cosp[:, t, :].unsqueeze(1).to_broadcast([128, 8, 32])
                sb_ = sinp[:, t, :].unsqueeze(1).to_broadcast([128, 8, 32])
                for j, (dst, sct, dect) in enumerate(((qrot, qd, qdec), (krot, kd, kdec))):
                    v4 = bqk[:, j * 512:(j + 1) * 512].rearrange("p (h f two) -> p h f two", h=8, two=2)
                    x1, x2 = v4[:, :, :, 0], v4[:, :, :, 1]
                    d4 = dst[:].rearrange("p (h f two) -> p h f two", h=8, two=2)
                    eng = "dve" if j == 0 else "pool"
                    rt = rtq if j == 0 else rtk
                    S.op(eng, lambda e, x1=x1: e.tensor_tensor(out=rt[0][:], in0=x1, in1=cb_, op=ALU.mult), r=[bqk, cosp], w=[rt[0]])
                    S.op(eng, lambda e, x2=x2: e.tensor_tensor(out=rt[1][:], in0=x2, in1=sb_, op=ALU.mult), r=[bqk, sinp], w=[rt[1]])
                    S.op(eng, lambda e, d4=d4: e.tensor_tensor(out=d4[:, :, :, 0], in0=rt[0][:], in1=rt[1][:], op=ALU.subtract), r=[rt[0], rt[1]], w=[dst])
                    S.op(eng, lambda e, x1=x1: e.tensor_tensor(out=rt[2][:], in0=x1, in1=sb_, op=ALU.mult), r=[bqk, sinp], w=[rt[2]])
                    S.op(eng, lambda e, x2=x2: e.tensor_tensor(out=rt[3][:], in0=x2, in1=cb_, op=ALU.mult), r=[bqk, cosp], w=[rt[3]])
                    S.op(eng, lambda e, d4=d4: e.tensor_tensor(out=d4[:, :, :, 1], in0=rt[2][:], in1=rt[3][:], op=ALU.add), r=[rt[2], rt[3]], w=[dst])
                    S.op(eng, lambda e, dst=dst, sct=sct, dect=dect: e.tensor_tensor(
                        out=sct[:].rearrange("p (h d) -> p h d", h=8), in0=dst[:].rearrange("p (h d) -> p h d", h=8),
                        in1=dect[:].unsqueeze(2).to_broadcast([128, 8, 64]), op=ALU.mult), r=[dst, dect], w=[sct])
                yield
                for gi, srcb in enumerate((qrot, qd, krot)):
                    for i in range(4):
                        S.op("pe", lambda e, srcb=srcb, i=i: e.transpose(out=PTb[:, i, :], in_=srcb[:, i * 128:(i + 1) * 128], identity=ident[:]),
                             r=[srcb, ident], w=[PTb])
                    S.op("act", lambda e, gi=gi: e.copy(out=qkT[:, gi * 4:gi * 4 + 4, :], in_=PTb[:, 0:4, :]), r=[PTb], w=[qkT])
                yield
                for h in range(8):
                    hf = (h % 2) * 64
                    S.op("pe", lambda e, h=h, hf=hf: e.matmul(out=PO[:, 1024 + h * 128:1024 + (h + 1) * 128], lhsT=qkT[hf:hf + 64, 8 + h // 2, :],
                                                              rhs=qkT[hf:hf + 64, h // 2, :], start=True, stop=True), r=[qkT], w=[PO.b[2 + h // 4]], serial=True)
                S.op("dve", lambda e: e.tensor_tensor(out=innT[:], in0=PO[:, 1024:2048].rearrange("p (h q) -> p h q", h=8), in1=decT[:], op=ALU.mult),
                     r=[PO.b[2], PO.b[3], decT], w=[innT])
                for h in range(8):
                    hf = (h % 2) * 64
                    S.op("pe", lambda e, h=h: e.matmul(out=PO[:, 1024 + h * 128:1024 + (h + 1) * 128], lhsT=innT[:, h, :],
                                                       rhs=bv[:, h * 128:(h + 1) * 128], start=True, stop=False), r=[innT, bv], w=[PO.b[2 + h // 4]])
                    S.op("pe", lambda e, h=h, hf=hf: e.matmul(out=PO[:, 1024 + h * 128:1024 + (h + 1) * 128], lhsT=qkT[hf:hf + 64, 4 + h // 2, :],
                                                              rhs=Sbf[l][hf:hf + 64, h // 2, :], start=False, stop=True), r=[qkT, Sbf[l]], w=[PO.b[2 + h // 4]], serial=True)
                yield
                for i in range(4):
                    pd = nextpd()
                    S.op("pe", lambda e, i=i, pd=pd: e.matmul(out=pd[:, 0:256], lhsT=kd[:, i * 128:(i + 1) * 128], rhs=bv[:, i * 256:(i + 1) * 256],
                                                              start=True, stop=True), r=[kd, bv], w=[pd])
                    for hfi in range(2):
                        rs = slice(hfi * 64, hfi * 64 + 64)
                        S.op("dve", lambda e, i=i, pd=pd, rs=rs, hfi=hfi: e.scalar_tensor_tensor(
                            out=Sret[l][rs, i, :], in0=Sret[l][rs, i, :], scalar=gL[rs, i:i + 1], in1=pd[rs, hfi * 128:(hfi + 1) * 128],
                            op0=ALU.mult, op1=ALU.add), r=[Sret[l], gL, pd, Sbf[l]], w=[Sret[l]])
                S.op("act", lambda e: e.copy(out=Sbf[l][:], in_=Sret[l][:]), r=[Sret[l]], w=[Sbf[l]])
                yield
                o3 = PO[:, 1024:2048].rearrange("p (h e) -> p h e", h=8)
                S.op("act", lambda e: e.activation(out=sq[:], in_=PO[:, 1024:2048], func=AF.Square), r=[PO.b[2], PO.b[3]], w=[sq])
                S.op("dve", lambda e: e.tensor_reduce(out=st8[:, 8:16], in_=sq[:].rearrange("p (h e) -> p h e", h=8), axis=AX.X, op=ALU.add),
                     r=[sq], w=[st8])
                S.op("act", lambda e: e.activation(out=st8[:, 8:16], in_=st8[:, 8:16], func=AF.Ln, scale=1.0 / 128, bias=EPS), r=[st8], w=[st8])
                S.op("act", lambda e: e.activation(out=st8[:, 8:16], in_=st8[:, 8:16], func=AF.Exp, scale=-0.5), r=[st8], w=[st8])
                S.op("dve", lambda e: e.tensor_tensor(out=tmp[:].rearrange("p (h e) -> p h e", h=8), in0=o3,
                                                      in1=st8[:, 8:16].unsqueeze(2).to_broadcast([128, 8, 128]), op=ALU.mult), r=[PO.b[2], PO.b[3], st8], w=[tmp])
                S.op("pool", lambda e: e.tensor_tensor(out=tmp[:], in0=tmp[:], in1=bgs[:], op=ALU.mult), r=[tmp, bgs], w=[tmp])
                S.op("pool", lambda e: e.tensor_tensor(out=mix[:], in0=mix[:], in1=tmp[:], op=ALU.add), r=[tmp, mix], w=[mix])


            def gPC():
                for j in range(2):
                    wb_, wr = win_tile(l, O_CZ + j * 512)
                    pd = nextpd()
                    mm_tm(pd, wb_, 0, 512)
                    S.op("act", lambda e, pd=pd, j=j: e.activation(out=czs[:, j * 512:(j + 1) * 512], in_=pd[:], func=AF.Silu), r=[pd], w=[czs.b[j]])
                    yield
                S.op("pool", lambda e: e.tensor_copy(out=xbcT[:, :, 0:3], in_=chist[l][:]), r=[chist[l]], w=[xbcT])
                for j3 in range(3):
                    wb_, wr = win_tile(l, O_XBC + j3 * 512)
                    for i in range(4):
                        pd = nextpd()
                        mm_fm(pd, wb_, i * 128)
                        S.op("act", lambda e, pd=pd, jj=j3 * 4 + i: e.copy(out=xbcT[:, jj, 3:131], in_=pd[:, 0:128]), r=[pd], w=[xbcT.b[j3 * 4 + i]])
                    yield
                yield
                S.op("pool", lambda e: e.tensor_copy(out=chist[l][:], in_=xbcT[:, :, 128:131]), r=[xbcT], w=[chist[l]])
                if t == T_LAST and DO_CONV:
                    for j in range(12):
                        S.op("pe", lambda e, j=j: e.transpose(out=PO[0:3, j * 128:(j + 1) * 128], in_=chist[l][:, j, :], identity=identf[:]),
                             r=[chist[l], identf], w=[PO.b[j // 4]])
                    S.op("act", lambda e: e.copy(out=tmp[0:3, 0:1024], in_=PO[0:3, 0:1024]), r=[PO.b[0], PO.b[1]], w=[tmp])
                    S.op("act", lambda e: e.copy(out=sq[0:3, 0:512], in_=PO[0:3, 1024:1536]), r=[PO.b[2]], w=[sq])
                    S.dma("pool", conv_p[l, b, :, 0:1024], tmp[0:3, 0:1024], r=[tmp], w=[OUT])
                    S.dma("pool", conv_p[l, b, :, 1024:1536], sq[0:3, 0:512], r=[sq], w=[OUT])
                wb_, wr = win_tile(l, O_DT, 16)
                pd = nextpd()
                mm_tm(pd, wb_, 0, 16)
                S.op("dve", lambda e, pd=pd: e.tensor_tensor(out=dts[:, 0, :], in0=pd[:, 0:16], in1=dtb[:, l, :], op=ALU.add), r=[pd, dtb], w=[dts])

            def gMC():
                for jj in range(12):
                    if jj % 4 == 0:
                        yield
                    eng = "dve"
                    S.op(eng, lambda e, jj=jj: e.tensor_scalar(out=xcT[:, jj, :], in0=xbcT[:, jj, 0:128], scalar1=cwc[:, l, 0, jj:jj + 1],
                                                               scalar2=cbc[:, l, jj:jj + 1], op0=ALU.mult, op1=ALU.add), r=[xbcT, cwc, cbc], w=[xcT])
                    for i in range(1, 4):
                        S.op(eng, lambda e, jj=jj, i=i: e.scalar_tensor_tensor(out=xcT[:, jj, :], in0=xbcT[:, jj, i:i + 128], scalar=cwc[:, l, i, jj:jj + 1],
                                                                               in1=xcT[:, jj, :], op0=ALU.mult, op1=ALU.add), r=[xbcT, cwc, xcT], w=[xcT])
                yield
                S.op("act", lambda e: e.activation(out=xcT[:], in_=xcT[:], func=AF.Silu), r=[xcT], w=[xcT])
                S.op("pool", lambda e: e.tensor_copy(out=bcTb[:], in_=xcT[:, 8:12, :]), r=[xcT], w=[bcTb])
                S.op("act", lambda e: e.activation(out=dts[:, 1, :], in_=dts[:, 0, :], func=AF.Exp), r=[dts], w=[dts])
                S.op("act", lambda e: e.activation(out=dts[:, 2, :], in_=dts[:, 1, :], func=AF.Ln, bias=1.0), r=[dts], w=[dts])
                S.op("dve", lambda e: e.tensor_tensor(out=dts[:, 3, :], in0=dts[:, 2, :], in1=Arow[:, l, :], op=ALU.mult), r=[dts, Arow], w=[dts])
                yield
                pd = nextpd()
                S.op("pe", lambda e, pd=pd: e.matmul(out=pd[:, 0:16], lhsT=tri[:], rhs=dts[:, 3, :], start=True, stop=True), r=[tri, dts], w=[pd])
                S.op("pe", lambda e, pd=pd: e.matmul(out=pd[:, 16:32], lhsT=ones[:], rhs=dts[:, 3, :], start=True, stop=True), r=[ones, dts], w=[pd])
                S.op("dve", lambda e, pd=pd: e.tensor_copy(out=dts[:, 4, :], in_=pd[:, 0:16]), r=[pd], w=[dts])
                S.op("dve", lambda e, pd=pd: e.tensor_scalar(out=dts[:, 5, :], in0=pd[:, 0:16], scalar1=-1.0, scalar2=None, op0=ALU.mult), r=[pd], w=[dts])
                S.op("dve", lambda e, pd=pd: e.tensor_tensor(out=dts[:, 6, :], in0=pd[:, 16:32], in1=dts[:, 4, :], op=ALU.subtract), r=[pd, dts], w=[dts])
                S.op("act", lambda e, pd=pd: e.activation(out=dts[:, 7, :], in_=pd[:, 16:32], func=AF.Exp), r=[pd], w=[dts])
                S.op("act", lambda e: e.activation(out=dts[:, 6, :], in_=dts[:, 6, :], func=AF.Exp), r=[dts], w=[dts])
                S.op("dve", lambda e: e.tensor_tensor(out=dts[:, 6, :], in0=dts[:, 6, :], in1=dts[:, 2, :], op=ALU.mult), r=[dts], w=[dts])
                S.op("act", lambda e: e.activation(out=dts[:, 1, :], in_=dts[:, 4, :], func=AF.Exp), r=[dts], w=[dts])
                for g in range(2):
                    for hg2 in range(2):
                        yield
                        hg = g * 2 + hg2
                        S.op("pool", lambda e, hg=hg: e.tensor_copy(out=dtAb[:], in_=dts[:, 3, 4 * hg:4 * hg + 4].unsqueeze(2).to_broadcast([128, 4, 128])),
                             r=[dts], w=[dtAb])
                        for hh in range(4):
                            S.op("pe", lambda e, hh=hh: e.matmul(out=PS[:, hh * 128:(hh + 1) * 128], lhsT=dtAb[:, hh, :], rhs=tri[:], start=True, stop=False),
                                 r=[dtAb, tri], w=[PS])
                            S.op("pe", lambda e, hh=hh: e.matmul(out=PS[:, hh * 128:(hh + 1) * 128], lhsT=ident[:], rhs=mnegb[:], start=False, stop=True),
                                 r=[ident, mnegb], w=[PS])
                        for hh in range(4):
                            h = hg * 4 + hh
                            S.op("act", lambda e, h=h, hh=hh, hg2=hg2: e.activation(out=LT[:, hg2 * 4 + hh, :], in_=PS[:, hh * 128:(hh + 1) * 128], func=AF.Exp,
                                                                           bias=dts[:, 5, h:h + 1]), r=[PS, dts], w=[LT])
                    pd = nextpd()
                    S.op("pe", lambda e, g=g, pd=pd: e.matmul(out=pd[:, 0:128], lhsT=bcTb[:, g, :], rhs=bcTb[:, 2 + g, :], start=True, stop=True), r=[bcTb], w=[pd])
                    S.op("dve", lambda e, g=g, pd=pd: e.tensor_tensor(out=MT[:, 8 * g:8 * g + 8, :], in0=LT[:],
                                                                      in1=pd[:, 0:128].unsqueeze(1).to_broadcast([128, 8, 128]), op=ALU.mult), r=[pd, LT], w=[MT])
                yield
                for i in range(8):
                    S.op("pe", lambda e, i=i: e.transpose(out=PO[:, 1024 + i * 128:1024 + (i + 1) * 128], in_=xcT[:, i, :], identity=identf[:]),
                         r=[xcT, identf], w=[PO.b[2 + i // 4]])
                S.op("act", lambda e: e.copy(out=xcs[:], in_=PO[:, 1024:2048]), r=[PO.b[2], PO.b[3]], w=[xcs])
                x3 = xcs[:].rearrange("p (h d) -> p h d", h=16)
                S.op("dve", lambda e: e.tensor_tensor(out=xdt[:].rearrange("p (h d) -> p h d", h=16), in0=x3,
                                                      in1=dts[:, 2, :].unsqueeze(2).to_broadcast([128, 16, 64]), op=ALU.mult), r=[xcs, dts], w=[xdt])
                S.op("pool", lambda e: e.tensor_tensor(out=xw[:].rearrange("p (h d) -> p h d", h=16), in0=x3,
                                                       in1=dts[:, 6, :].unsqueeze(2).to_broadcast([128, 16, 64]), op=ALU.mult), r=[xcs, dts], w=[xw])
                yield
                for g in range(2):
                    S.op("pe", lambda e, g=g: e.transpose(out=PTb[:, g, :], in_=bcTb[:, g, :], identity=ident[:]), r=[bcTb, ident], w=[PTb])
                S.op("act", lambda e: e.copy(out=Btm[:].rearrange("p (g n) -> p g n", g=2), in_=PTb[:, 0:2, :]), r=[PTb], w=[Btm])
                yield
                for h in range(16):
                    S.op("pe", lambda e, h=h: e.matmul(out=PO[:, h * 64:(h + 1) * 64], lhsT=MT[:, h, :], rhs=xdt[:, h * 64:(h + 1) * 64],
                                                       start=True, stop=True), r=[MT, xdt], w=[PO.b[h // 8]])
                for h in range(16):
                    S.op("pe", lambda e, h=h: e.matmul(out=PO[:, 1024 + h * 64:1024 + (h + 1) * 64], lhsT=bcTb[:, 2 + h // 8, :],
                                                       rhs=hSb[l][:, h * 64:(h + 1) * 64], start=True, stop=True), r=[bcTb, hSb[l]], w=[PO.b[2 + h // 8]])
                S.op("dve", lambda e: e.tensor_tensor(out=tmp[:].rearrange("p (h d) -> p h d", h=16), in0=PO[:, 1024:2048].rearrange("p (h d) -> p h d", h=16),
                                                      in1=dts[:, 1, :].unsqueeze(2).to_broadcast([128, 16, 64]), op=ALU.mult), r=[PO.b[2], PO.b[3], dts], w=[tmp])
                S.op("dve", lambda e: e.tensor_tensor(out=tmp[:], in0=tmp[:], in1=PO[:, 0:1024], op=ALU.add), r=[tmp, PO.b[0], PO.b[1]], w=[tmp])
                S.op("pool", lambda e: e.tensor_tensor(out=tmp2[:].rearrange("p (h d) -> p h d", h=16), in0=x3,
                                                       in1=Dsk[:, l, :].unsqueeze(2).to_broadcast([128, 16, 64]), op=ALU.mult), r=[xcs, Dsk], w=[tmp2])
                S.op("pool", lambda e: e.tensor_tensor(out=tmp[:], in0=tmp[:], in1=tmp2[:], op=ALU.add), r=[tmp, tmp2], w=[tmp])
                S.op("pool", lambda e: e.tensor_tensor(out=tmp[:], in0=tmp[:], in1=czs[:], op=ALU.mult), r=[tmp, czs], w=[tmp])
                yield
                for g in range(2):
                    pd = nextpd()
                    S.op("pe", lambda e, g=g, pd=pd: e.matmul(out=pd[:], lhsT=Btm[:, g * 128:(g + 1) * 128], rhs=xw[:, g * 512:(g + 1) * 512], start=True, stop=True),
                         r=[Btm, xw], w=[pd])
                    S.op("pool", lambda e, g=g: e.tensor_tensor(out=hS[l][:, g * 512:(g + 1) * 512].rearrange("p (h d) -> p h d", h=8),
                                                                in0=hS[l][:, g * 512:(g + 1) * 512].rearrange("p (h d) -> p h d", h=8),
                                                                in1=dts[:, 7, 8 * g:8 * g + 8].unsqueeze(2).to_broadcast([128, 8, 64]), op=ALU.mult),
                         r=[hS[l], dts, hSb[l]], w=[hS[l]])
                    S.op("dve", lambda e, g=g, pd=pd: e.tensor_tensor(out=hS[l][:, g * 512:(g + 1) * 512], in0=hS[l][:, g * 512:(g + 1) * 512], in1=pd[:], op=ALU.add),
                         r=[hS[l], pd], w=[hS[l]])
                S.op("act", lambda e: e.copy(out=hSb[l][:], in_=hS[l][:]), r=[hS[l]], w=[hSb[l]])
                yield
                for g in range(2):
                    S.op("act", lambda e, g=g: e.activation(out=sq[:, g * 512:(g + 1) * 512], in_=tmp[:, g * 512:(g + 1) * 512], func=AF.Square,
                                                            accum_out=st8[:, 16 + g:17 + g]), r=[tmp], w=[sq, st8])
                S.op("act", lambda e: e.activation(out=st8[:, 16:18], in_=st8[:, 16:18], func=AF.Ln, scale=1.0 / 512, bias=EPS), r=[st8], w=[st8])
                S.op("act", lambda e: e.activation(out=st8[:, 16:18], in_=st8[:, 16:18], func=AF.Exp, scale=-0.5), r=[st8], w=[st8])
                S.op("dve", lambda e: e.tensor_tensor(out=tmp[:].rearrange("p (g d) -> p g d", g=2), in0=tmp[:].rearrange("p (g d) -> p g d", g=2),
                                                      in1=st8[:, 16:18].unsqueeze(2).to_broadcast([128, 2, 512]), op=ALU.mult), r=[tmp, st8], w=[tmp])
                S.op("pool", lambda e: e.tensor_tensor(out=tmp[:], in0=tmp[:], in1=snwb[:, l, :], op=ALU.mult), r=[tmp, snwb], w=[tmp])
                yield "FINAL"
                S.op("dve", lambda e: e.tensor_tensor(out=mix[:], in0=tmp[:], in1=gates[:, 2, :], op=ALU.mult), r=[tmp, gates], w=[mix])


            def drain(g_):
                for _ in g_:
                    pass

            def chain(*gs):
                for g_ in gs:
                    yield from g_

            drain(gPC())
            gp = chain(gPA(), gPB())
            for tok in gMC():
                if tok == "FINAL":
                    drain(gp)
                else:
                    next(gp, None)
            drain(gp)
            ga_, gb_ = gMA(), gMB()
            alive = [ga_, gb_]
            while alive:
                for g_ in list(alive):
                    try:
                        next(g_)
                    except StopIteration:
                        alive.remove(g_)
            dense_tail(l, g1b)


        def dense_tail(l, g1b, rows=128):
            S.op("act", lambda e: e.copy(out=xn[0:rows, :], in_=mix[0:rows, :]), r=[mix], w=[xn])
            for k in range(8):
                S.op("pe", lambda e, k=k: e.transpose(out=PTb[:, k, 0:rows], in_=xn[0:rows, k * 128:(k + 1) * 128], identity=ident[0:rows, 0:rows]),
                     r=[xn, ident], w=[PTb])
            S.op("act", lambda e: e.copy(out=hT[:, :, 0:rows], in_=PTb[:, :, 0:rows]), r=[PTb], w=[hT])
            for j in range(2):
                wb_ = wload(tl(woutT, l, j), 512, dram_b["woutb%d" % l])
                pd = nextpd()
                for k in range(8):
                    S.op("pe", lambda e, k=k, pd=pd, wb_=wb_: e.matmul(out=pd[0:rows, :], lhsT=hT[:, k, 0:rows], rhs=wb_[:, k, :], start=(k == 0), stop=(k == 7)),
                         r=[hT, wb_], w=[pd])
                S.op("dve", lambda e, j=j, pd=pd: e.tensor_tensor(out=tmp[0:rows, j * 512:(j + 1) * 512], in0=pd[0:rows, :], in1=g1b[0:rows, j * 512:(j + 1) * 512], op=ALU.mult),
                     r=[pd, g1b], w=[tmp])
            S.op("dve", lambda e: e.tensor_tensor(out=xres[0:rows, :], in0=xres[0:rows, :], in1=tmp[0:rows, :], op=ALU.add), r=[tmp, xres], w=[xres])

        def mlp(l, g2b, xrs, hTs, uTs, rows=128):
            nch = len(xrs)
            cnt_ = 0
            for j in range(8):
                wb_ = wload(tl(wupT, l, j), 512, dram_b["wupb%d" % l])
                for c in range(nch):
                    hT_, uT_ = hTs[c], uTs[c]
                    pd = nextpd()
                    for k in range(8):
                        S.op("pe", lambda e, k=k, pd=pd, wb_=wb_, hT_=hT_: e.matmul(out=pd[0:rows, :], lhsT=hT_[:, k, 0:rows], rhs=wb_[:, k, :],
                                                                                  start=(k == 0), stop=(k == 7)), r=[hT_, wb_], w=[pd])
                    hsel = cnt_ % 2
                    cnt_ += 1
                    S.op("act", lambda e, pd=pd, hsel=hsel: e.activation(out=sq[0:rows, hsel * 512:(hsel + 1) * 512], in_=pd[0:rows, :], func=AF.Relu),
                         r=[pd], w=[sqh[hsel]])
                    S.op("pool", lambda e, hsel=hsel: e.tensor_tensor(out=xn[0:rows, hsel * 512:(hsel + 1) * 512], in0=sq[0:rows, hsel * 512:(hsel + 1) * 512],
                                                                      in1=sq[0:rows, hsel * 512:(hsel + 1) * 512], op=ALU.mult), r=[sqh[hsel]], w=[xnh[hsel]])
                    for i in range(4):
                        S.op("pe", lambda e, i=i, hsel=hsel: e.transpose(out=PTb[:, hsel * 4 + i, 0:rows], in_=xn[0:rows, hsel * 512 + i * 128:hsel * 512 + (i + 1) * 128],
                                                                        identity=ident[0:rows, 0:rows]), r=[xnh[hsel], ident], w=[PTb.b[hsel]])
                    S.op("act", lambda e, j=j, hsel=hsel, uT_=uT_: e.copy(out=uT_[:, j * 4:j * 4 + 4, 0:rows], in_=PTb[:, hsel * 4:hsel * 4 + 4, 0:rows]),
                         r=[PTb.b[hsel]], w=[uT_.b[j]])
            for ct in range(2):
                for sl in range(4):
                    wb_ = wload(tl(wdownT, l, ct * 4 + sl), 512, dram_b["wdownb%d" % l])
                    for c in range(nch):
                        pd = PD[c]
                        uT_ = uTs[c]
                        for k in range(8):
                            S.op("pe", lambda e, k=k, pd=pd, wb_=wb_, sl=sl, uT_=uT_: e.matmul(out=pd[0:rows, :], lhsT=uT_[:, sl * 8 + k, 0:rows], rhs=wb_[:, k, :],
                                                                                             start=(sl == 0 and k == 0), stop=(sl == 3 and k == 7)), r=[uT_, wb_], w=[pd])
                for c in range(nch):
                    pd, xr = PD[c], xrs[c]
                    S.op("dve", lambda e, ct=ct, pd=pd: e.tensor_tensor(out=tmp[0:rows, ct * 512:(ct + 1) * 512], in0=pd[0:rows, :], in1=g2b[0:rows, ct * 512:(ct + 1) * 512], op=ALU.mult),
                         r=[pd, g2b], w=[tmp])
                    S.op("pool", lambda e, ct=ct, xr=xr: e.tensor_tensor(out=xr[0:rows, ct * 512:(ct + 1) * 512], in0=xr[0:rows, ct * 512:(ct + 1) * 512],
                                                                         in1=tmp[0:rows, ct * 512:(ct + 1) * 512], op=ALU.add), r=[tmp, xr], w=[xr])

        def final_out(dst_ap, rows=128):
            c = rms_stats(xres, rows, 0)
            S.op("dve", lambda e: e.scalar_tensor_tensor(out=tmp2[0:rows, :], in0=xres[0:rows, :], scalar=st8[0:rows, c:c + 1], in1=fnwb[0:rows, :],
                                                         op0=ALU.mult, op1=ALU.mult), r=[xres, st8, fnwb], w=[tmp2])
            S.dma("pool", dst_ap, tmp2[0:rows, :], r=[tmp2], w=[OUT])

        for b in range(RUN_B):
            for l in range(DEPTH):
                S.op("pool", lambda e, l=l: e.memset(Sret[l][:], 0.0), w=[Sret[l]])
                S.op("pool", lambda e, l=l: e.memset(Sbf[l][:], 0.0), w=[Sbf[l]])
                S.op("pool", lambda e, l=l: e.memset(hS[l][:], 0.0), w=[hS[l]])
                S.op("pool", lambda e, l=l: e.memset(hSb[l][:], 0.0), w=[hSb[l]])
                S.op("pool", lambda e, l=l: e.memset(chist[l][:], 0.0), w=[chist[l]])
            for tp in range(0, RUN_T, 2):
                ts_ = [t for t in (tp, tp + 1) if t < RUN_T]
                for c, t in enumerate(ts_):
                    S.dma("sp", xresL[c][:], xp[b, t * 128:(t + 1) * 128, :], w=[xresL[c]])
                for l in range(DEPTH):
                    for c, t in enumerate(ts_):
                        xres = xresL[c]
                        hT = hT_main
                        layer_chunk(l, b, t, l == DEPTH - 1)
                        if not cast_done[1]:
                            cast_layer(1)
                            cast_done[1] = True
                        hT = hTm[c]
                        norm_to_hT(A2L[l], modcL[l], 2)
                        hT = hT_main
                    mlp(l, g2bL[l], xresL[:len(ts_)], hTm[:len(ts_)], uTL[:len(ts_)])
                for c, t in enumerate(ts_):
                    xres = xresL[c]
                    final_out(y_p[b, t * 128:(t + 1) * 128, :])
            xres = xresL[0]
            for l in range(DEPTH):
                S.dma("pool", bass.AP(ret_p.tensor, (l * NB + b) * 8 * 64 * 128, [[128, 128], [16384, 4], [1, 128]]), Sret[l][:], r=[Sret[l]], w=[OUT])
                for half in range(2):
                    for i in range(4):
                        S.op("pe", lambda e, i=i, half=half, l=l: e.transpose(out=PO[:, i * 128:(i + 1) * 128], in_=hS[l][:, (half * 4 + i) * 128:(half * 4 + i + 1) * 128],
                                                                             identity=identf[:]), r=[hS[l], identf], w=[PO.b[0]])
                    S.op("act", lambda e, half=half: e.copy(out=tmp[:, half * 512:(half + 1) * 512], in_=PO[:, 0:512]), r=[PO.b[0]], w=[tmp])
                S.dma("pool", bass.AP(ssm_p.tensor, (l * NB + b) * 16 * 64 * 128, [[128, 128], [16384, 8], [1, 128]]), tmp[:], r=[tmp], w=[OUT])


        if not cast_done[1]:
            cast_layer(1)
            cast_done[1] = True
        stP.close()
        S.barrier()
        big = sb("big", [128, 8192]); prodb = sb("prodb", [128, 8192]); Vp = sb("Vp", [128, 8192])
        pa = sb("pa", [128, 768]); pb_ = sb("pb_", [128, 512]); pc = sb("pc", [128, 512]); smallp = sb("smallp", [128, 64])
        Z = dram_b["zsd"]; OS = dram_b["osd"]
        segs = {"aq": 1024, "ak": 256, "av": 256, "bq": 512, "bk": 512, "bv": 1024, "bg": 1024, "cz": 1024, "xbc": 1536, "dt": 16,
                "ga": 1024, "gb": 1024, "gc": 1024}
        zd = {k: dt_int("z_" + k, [NS, n]) for k, n in segs.items()}
        zx = dt_int("z_x", [NS, 1024]); zB = dt_int("z_B", [NS, 256]); zC = dt_int("z_C", [NS, 256]); zBr = dt_int("z_Br", [NS, 2, 8, 128])
        zCr = dt_int("z_Cr", [NS, 2, 8, 128]); zsm = dt_int("z_sm", [NS, 16, 4])
        R16 = NS
        S.dma("sp", xres[0:R16, :], xs, w=[xres])
        rowb = lambda t, off, n, rows=R16: bass.AP(t.tensor, off, [[0, rows], [1, n]])

        def s_norm(l, nw, col_sc, col_sh):
            S.dma("pool", tmp2[0:R16, :], rowb(nw, l * D, D), w=[tmp2])
            S.dma("pool", tmp[0:R16, :], modd[l, NB:NB + NS, col_sc * D:(col_sc + 1) * D], r=[dram_b["modd"]], w=[tmp])
            S.op("dve", lambda e: e.scalar_tensor_tensor(out=tmp[0:R16, :], in0=tmp[0:R16, :], scalar=1.0, in1=tmp2[0:R16, :], op0=ALU.add, op1=ALU.mult),
                 r=[tmp, tmp2], w=[tmp])
            c = rms_stats(xres, R16, 0)
            S.op("dve", lambda e: e.scalar_tensor_tensor(out=sq[0:R16, :], in0=xres[0:R16, :], scalar=st8[0:R16, c:c + 1], in1=tmp[0:R16, :], op0=ALU.mult, op1=ALU.mult),
                 r=[xres, st8, tmp], w=[sq])
            S.dma("pool", tmp2[0:R16, :], modd[l, NB:NB + NS, col_sh * D:(col_sh + 1) * D], r=[dram_b["modd"]], w=[tmp2])
            S.op("dve", lambda e: e.tensor_tensor(out=xn[0:R16, :], in0=sq[0:R16, :], in1=tmp2[0:R16, :], op=ALU.add), r=[sq, tmp2], w=[xn])
            for k in range(8):
                S.op("pe", lambda e, k=k: e.transpose(out=PTb[:, k, 0:R16], in_=xn[0:R16, k * 128:(k + 1) * 128], identity=ident[0:R16, 0:R16]),
                     r=[xn, ident], w=[PTb])
            S.op("act", lambda e: e.copy(out=hT[:, :, 0:R16], in_=PTb[:, :, 0:R16]), r=[PTb], w=[hT])

        def pairs_out(src_ap, rows, slot, width, srcT):
            S.dma("pool", osd[slot].rearrange("s (h c) -> (s h) c", c=width), src_ap, r=[srcT], w=[OS])

        for l in range(DEPTH if RUN_SAMPLE else 0):
            s_norm(l, n1w, 1, 0)
            order = ["aq", "ak", "av", "bq", "bk", "bv", "bg", "cz", "xbc", "ga", "gb", "gc", "dt"]
            segoff = {}
            o_ = 0
            for k in order:
                segoff[k] = o_
                o_ += segs[k]
            ti = 0
            for c0 in range(0, DIN, 512):
                n = min(512, DIN - c0)
                wb_ = wload(tl(winT, l, c0 // 512, n), n, dram_b["winb%d" % l])
                pd = nextpd()
                for k in range(8):
                    S.op("pe", lambda e, k=k, pd=pd, wb_=wb_, n=n: e.matmul(out=pd[0:R16, 0:n], lhsT=hT[:, k, 0:R16], rhs=wb_[:, k, 0:n], start=(k == 0), stop=(k == 7)),
                         r=[hT, wb_], w=[pd])
                stage = pa if ti % 2 == 0 else pb_
                ti += 1
                if c0 < 1024:
                    S.op("act", lambda e, pd=pd, stage=stage: e.copy(out=stage[0:R16, 0:512].rearrange("p (g1 hq d) -> p hq g1 d", g1=2, hq=4),
                                                                 in_=pd[0:R16, 0:512].rearrange("p (hq g1 d) -> p hq g1 d", hq=4, g1=2)), r=[pd], w=[stage])
                else:
                    S.op("act", lambda e, pd=pd, stage=stage, n=n: e.copy(out=stage[0:R16, 0:n], in_=pd[0:R16, 0:n]), r=[pd], w=[stage])
                for k in order:
                    a0, a1 = max(c0, segoff[k]), min(c0 + n, segoff[k] + segs[k])
                    if a0 < a1:
                        S.dma("pool", zd[k][:, a0 - segoff[k]:a1 - segoff[k]], stage[0:R16, a0 - c0:a1 - c0], r=[stage], w=[Z])
            for gi, k in enumerate(("ga", "gb", "gc")):
                S.dma("pool", gates[0:R16, gi, :], zd[k], r=[Z], w=[gates])
            S.op("act", lambda e: e.activation(out=gates[0:R16, :, :], in_=gates[0:R16, :, :], func=AF.Sigmoid), r=[gates], w=[gates])

            S.dma("pool", wk_s[l, :, 0:127, :], cache_k[l, :, 1:128, :], w=[OUT])
            S.dma("pool", wk_s[l, :, 127, :], zd["ak"], r=[Z], w=[OUT])
            S.dma("pool", wv_s[l, :, 0:127, :], cache_v[l, :, 1:128, :], w=[OUT])
            S.dma("pool", wv_s[l, :, 127, :], zd["av"], r=[Z], w=[OUT])
            K3 = big[0:64, :].rearrange("p (k d) -> p k d", d=64)
            V3 = Vp[0:64, :].rearrange("p (k d) -> p k d", d=64)
            P3 = prodb[0:64, :].rearrange("p (k d) -> p k d", d=64)
            for s_ in range(NS):
                S.dma("sp", K3[4 * s_:4 * s_ + 4, 0:127, :], cache_k[l, s_, 1:128, :].rearrange("k (g d) -> g k d", g=4), w=[big])
                S.dma("sp", V3[4 * s_:4 * s_ + 4, 0:127, :], cache_v[l, s_, 1:128, :].rearrange("k (g d) -> g k d", g=4), w=[Vp])
            S.dma("pool", K3[:, 127, :], zd["ak"].rearrange("s (g d) -> (s g) d", g=4), r=[Z], w=[big])
            S.dma("pool", V3[:, 127, :], zd["av"].rearrange("s (g d) -> (s g) d", g=4), r=[Z], w=[Vp])
            S.dma("pool", pa[0:64, 0:256], zd["aq"].rearrange("s (g c) -> (s g) c", g=4), r=[Z], w=[pa])
            for s_ in range(NS):
                S.dma("pool", pb_[4 * s_:4 * s_ + 4, 0:512], bass.AP(vecx.tensor, 128, [[4 * 384, 4], [384, 4], [1, 128]]), r=[dram_b["vecx"]], w=[pb_])
                S.dma("pool", smallp[4 * s_:4 * s_ + 4, 0:4], sinks[l].rearrange("(g hq) -> g hq", g=4), w=[smallp])
            S.op("act", lambda e: e.activation(out=smallp[0:64, 0:4], in_=smallp[0:64, 0:4], func=AF.Exp), r=[smallp], w=[smallp])
            for hq in range(4):
                S.op("dve", lambda e, hq=hq: e.tensor_tensor(out=P3, in0=K3, in1=pa[0:64, hq * 64:(hq + 1) * 64].unsqueeze(1).to_broadcast([64, 128, 64]), op=ALU.mult),
                     r=[big, pa], w=[prodb])
                S.op("dve", lambda e, hq=hq: e.tensor_reduce(out=pc[0:64, hq * 128:(hq + 1) * 128], in_=P3, axis=AX.X, op=ALU.add), r=[prodb], w=[pc])
            S.op("dve", lambda e: e.scalar_tensor_tensor(out=pc[0:64, 0:512], in0=pc[0:64, 0:512], scalar=0.125, in1=pb_[0:64, 0:512], op0=ALU.mult, op1=ALU.add),
                 r=[pc, pb_], w=[pc])
            S.op("act", lambda e: e.activation(out=pc[0:64, 0:512], in_=pc[0:64, 0:512], func=AF.Exp), r=[pc], w=[pc])
            S.op("dve", lambda e: e.tensor_reduce(out=smallp[0:64, 4:8], in_=pc[0:64, 0:512].rearrange("p (h k) -> p h k", h=4), axis=AX.X, op=ALU.add), r=[pc], w=[smallp])
            S.op("dve", lambda e: e.tensor_tensor(out=smallp[0:64, 4:8], in0=smallp[0:64, 4:8], in1=smallp[0:64, 0:4], op=ALU.add), r=[smallp], w=[smallp])
            S.op("dve", lambda e: e.reciprocal(out=smallp[0:64, 4:8], in_=smallp[0:64, 4:8]), r=[smallp], w=[smallp])
            PV3 = prodb[0:64, :].rearrange("p (d k) -> p d k", k=128)
            for hq in range(4):
                S.op("dve", lambda e, hq=hq: e.tensor_tensor(out=PV3, in0=V3.rearrange("p k d -> p d k"),
                                                             in1=pc[0:64, hq * 128:(hq + 1) * 128].unsqueeze(1).to_broadcast([64, 64, 128]), op=ALU.mult), r=[Vp, pc], w=[prodb])
                S.op("dve", lambda e, hq=hq: e.tensor_reduce(out=pa[0:64, 256 + hq * 64:256 + (hq + 1) * 64], in_=PV3, axis=AX.X, op=ALU.add), r=[prodb], w=[pa])
            S.op("dve", lambda e: e.tensor_tensor(out=pa[0:64, 512:768].rearrange("p (h d) -> p h d", h=4), in0=pa[0:64, 256:512].rearrange("p (h d) -> p h d", h=4),
                                                  in1=smallp[0:64, 4:8].unsqueeze(2).to_broadcast([64, 4, 64]), op=ALU.mult), r=[pa, smallp], w=[pa])
            S.dma("pool", osd[0].rearrange("s (g c) -> (s g) c", g=4), pa[0:64, 512:768], r=[pa], w=[OS])
            S.dma("pool", tmp[0:R16, :], osd[0], r=[OS], w=[tmp])
            S.op("dve", lambda e: e.tensor_tensor(out=mix[0:R16, :], in0=tmp[0:R16, :], in1=gates[0:R16, 0, :], op=ALU.mult), r=[tmp, gates], w=[mix])

            S3 = big[:, :].rearrange("p (d e) -> p d e", e=128)
            PR3 = prodb[:, :].rearrange("p (d e) -> p d e", e=128)
            S.dma("sp", S3, st_ret[l].rearrange("s h d e -> (s h) d e"), w=[big])
            S.dma("pool", pa[:, 0:64], zd["bq"].rearrange("s (h d) -> (s h) d", h=8), r=[Z], w=[pa])
            S.dma("pool", pa[:, 64:128], zd["bk"].rearrange("s (h d) -> (s h) d", h=8), r=[Z], w=[pa])
            S.dma("pool", pb_[:, 0:128], zd["bv"].rearrange("s (h e) -> (s h) e", h=8), r=[Z], w=[pb_])
            S.dma("pool", smallp[:, 8:40], hc["coss"], w=[smallp])
            S.dma("pool", smallp[:, 40:41], hc["g1p"], w=[smallp])
            S.dma("pool", pc[:, 0:32], hc["sins"], w=[pc])
            for j in range(2):
                v3 = pa[:, j * 64:(j + 1) * 64].rearrange("p (f two) -> p f two", two=2)
                o3_ = pa[:, 128 + j * 64:128 + (j + 1) * 64].rearrange("p (f two) -> p f two", two=2)
                x1, x2 = v3[:, :, 0], v3[:, :, 1]
                cs_, sn_ = smallp[:, 8:40], pc[:, 0:32]
                S.op("dve", lambda e, x1=x1, cs_=cs_: e.tensor_tensor(out=pc[:, 32:64], in0=x1, in1=cs_, op=ALU.mult), r=[pa, smallp], w=[pc])
                S.op("dve", lambda e, x2=x2, sn_=sn_: e.tensor_tensor(out=pc[:, 64:96], in0=x2, in1=sn_, op=ALU.mult), r=[pa, pc], w=[pc])
                S.op("dve", lambda e, o3_=o3_: e.tensor_tensor(out=o3_[:, :, 0], in0=pc[:, 32:64], in1=pc[:, 64:96], op=ALU.subtract), r=[pc], w=[pa])
                S.op("dve", lambda e, x1=x1, sn_=sn_: e.tensor_tensor(out=pc[:, 32:64], in0=x1, in1=sn_, op=ALU.mult), r=[pa, pc], w=[pc])
                S.op("dve", lambda e, x2=x2, cs_=cs_: e.tensor_tensor(out=pc[:, 64:96], in0=x2, in1=cs_, op=ALU.mult), r=[pa, smallp], w=[pc])
                S.op("dve", lambda e, o3_=o3_: e.tensor_tensor(out=o3_[:, :, 1], in0=pc[:, 32:64], in1=pc[:, 64:96], op=ALU.add), r=[pc], w=[pa])
            S.op("dve", lambda e: e.tensor_scalar(out=pa[:, 192:256], in0=pa[:, 192:256], scalar1=0.125, scalar2=None, op0=ALU.mult), r=[pa], w=[pa])
            S.op("dve", lambda e: e.tensor_tensor(out=PR3, in0=pa[:, 192:256].unsqueeze(2).to_broadcast([128, 64, 128]),
                                                  in1=pb_[:, 0:128].unsqueeze(1).to_broadcast([128, 64, 128]), op=ALU.mult), r=[pa, pb_], w=[prodb])
            S.op("dve", lambda e: e.scalar_tensor_tensor(out=big[:, :], in0=big[:, :], scalar=smallp[:, 40:41], in1=prodb[:, :], op0=ALU.mult, op1=ALU.add),
                 r=[big, smallp, prodb], w=[big])
            S.dma("sp", ret_s[l].rearrange("s h d e -> (s h) d e"), S3, r=[big], w=[OUT])
            S.op("dve", lambda e: e.tensor_tensor(out=PR3, in0=S3, in1=pa[:, 128:192].unsqueeze(2).to_broadcast([128, 64, 128]), op=ALU.mult), r=[big, pa], w=[prodb])
            S.op("dve", lambda e: e.tensor_reduce(out=pb_[:, 128:256], in_=PR3.rearrange("p d e -> p e d"), axis=AX.X, op=ALU.add), r=[prodb], w=[pb_])
            S.dma("pool", osd[1].rearrange("s (h e) -> (s h) e", h=8), pb_[:, 128:256], r=[pb_], w=[OS])
            S.dma("pool", tmp[0:R16, :], osd[1], r=[OS], w=[tmp])
            S.dma("pool", bgs[0:R16, :], zd["bg"], r=[Z], w=[bgs])
            S.op("act", lambda e: e.activation(out=bgs[0:R16, :], in_=bgs[0:R16, :], func=AF.Silu), r=[bgs], w=[bgs])
            S.op("act", lambda e: e.activation(out=sq[0:R16, :], in_=tmp[0:R16, :], func=AF.Square), r=[tmp], w=[sq])
            S.op("dve", lambda e: e.tensor_reduce(out=st8[0:R16, 8:16], in_=sq[0:R16, :].rearrange("p (h e) -> p h e", h=8), axis=AX.X, op=ALU.add), r=[sq], w=[st8])
            S.op("act", lambda e: e.activation(out=st8[0:R16, 8:16], in_=st8[0:R16, 8:16], func=AF.Ln, scale=1.0 / 128, bias=EPS), r=[st8], w=[st8])
            S.op("act", lambda e: e.activation(out=st8[0:R16, 8:16], in_=st8[0:R16, 8:16], func=AF.Exp, scale=-0.5), r=[st8], w=[st8])
            S.op("dve", lambda e: e.tensor_tensor(out=tmp[0:R16, :].rearrange("p (h e) -> p h e", h=8), in0=tmp[0:R16, :].rearrange("p (h e) -> p h e", h=8),
                                                  in1=st8[0:R16, 8:16].unsqueeze(2).to_broadcast([R16, 8, 128]), op=ALU.mult), r=[tmp, st8], w=[tmp])
            S.op("dve", lambda e: e.tensor_tensor(out=tmp[0:R16, :], in0=tmp[0:R16, :], in1=bgs[0:R16, :], op=ALU.mult), r=[tmp, bgs], w=[tmp])
            S.op("dve", lambda e: e.tensor_tensor(out=tmp[0:R16, :], in0=tmp[0:R16, :], in1=gates[0:R16, 1, :], op=ALU.mult), r=[tmp, gates], w=[tmp])
            S.op("dve", lambda e: e.tensor_tensor(out=mix[0:R16, :], in0=mix[0:R16, :], in1=tmp[0:R16, :], op=ALU.add), r=[tmp, mix], w=[mix])

            hist = big[0:R16, 0:4608].rearrange("p (i c) -> p i c", i=3)
            cwb = Vp[0:R16, 0:6144].rearrange("p (i c) -> p i c", i=4)
            cbb, cx, acc, tb = prodb[0:R16, 0:1536], prodb[0:R16, 1536:3072], prodb[0:R16, 3072:4608], prodb[0:R16, 4608:6144]
            S.dma("sp", hist, st_conv[l], w=[big])
            S.dma("sp", cwb, bass.AP(conv_w.tensor, l * 4 * 1536, [[0, R16], [1536, 4], [1, 1536]]), w=[Vp])
            S.dma("sp", cbb, rowb(conv_b, l * 1536, 1536), w=[prodb])
            S.dma("sp", cx, zd["xbc"], r=[Z], w=[prodb])
            S.dma("pool", conv_s[l, :, 0:2, :], st_conv[l, :, 1:3, :], w=[OUT])
            S.dma("pool", conv_s[l, :, 2, :], zd["xbc"], r=[Z], w=[OUT])
            S.op("dve", lambda e: e.tensor_tensor(out=acc, in0=cx, in1=cwb[:, 3, :], op=ALU.mult), r=[prodb, big, Vp], w=[prodb])
            S.op("dve", lambda e: e.tensor_tensor(out=acc, in0=acc, in1=cbb, op=ALU.add), r=[prodb], w=[prodb])
            for i in range(3):
                S.op("dve", lambda e, i=i: e.tensor_tensor(out=tb, in0=hist[:, i, :], in1=cwb[:, i, :], op=ALU.mult), r=[big, prodb, Vp], w=[prodb])
                S.op("dve", lambda e: e.tensor_tensor(out=acc, in0=acc, in1=tb, op=ALU.add), r=[prodb], w=[prodb])
            S.op("act", lambda e: e.activation(out=acc, in_=acc, func=AF.Silu), r=[prodb], w=[prodb])
            S.dma("pool", zx, prodb[0:R16, 3072:3072 + 1024], r=[prodb], w=[Z])
            S.dma("pool", zB, prodb[0:R16, 3072 + 1024:3072 + 1280], r=[prodb], w=[Z])
            S.dma("pool", zC, prodb[0:R16, 3072 + 1280:3072 + 1536], r=[prodb], w=[Z])
            S.dma("pool", zBr.rearrange("s g r n -> (s g) r n"), bass.AP(zB.tensor, 0, [[128, 2 * NS], [0, 8], [1, 128]]), r=[Z], w=[Z])
            S.dma("pool", zCr.rearrange("s g r n -> (s g) r n"), bass.AP(zC.tensor, 0, [[128, 2 * NS], [0, 8], [1, 128]]), r=[Z], w=[Z])
            S.dma("pool", st8[0:R16, 0:16], zd["dt"], r=[Z], w=[st8])
            S.op("dve", lambda e: e.tensor_tensor(out=st8[0:R16, 0:16], in0=st8[0:R16, 0:16], in1=dtb[0:R16, l, :], op=ALU.add), r=[st8, dtb], w=[st8])
            S.op("act", lambda e: e.activation(out=st8[0:R16, 0:16], in_=st8[0:R16, 0:16], func=AF.Exp), r=[st8], w=[st8])
            sm4 = smallp[0:R16, 0:64].rearrange("p (h c) -> p h c", c=4)
            S.op("act", lambda e: e.activation(out=sm4[:, :, 0], in_=st8[0:R16, 0:16], func=AF.Ln, bias=1.0), r=[st8], w=[smallp])
            S.op("dve", lambda e: e.tensor_tensor(out=sm4[:, :, 1], in0=sm4[:, :, 0], in1=Arow[0:R16, l, :], op=ALU.mult), r=[smallp, Arow], w=[smallp])
            S.op("act", lambda e: e.activation(out=sm4[:, :, 1], in_=sm4[:, :, 1], func=AF.Exp), r=[smallp], w=[smallp])
            S.op("dve", lambda e: e.tensor_copy(out=sm4[:, :, 2], in_=Dsk[0:R16, l, :]), r=[Dsk, smallp], w=[smallp])
            S.op("dve", lambda e: e.tensor_copy(out=sm4[:, :, 3], in_=Dsk[0:R16, l, :]), r=[Dsk, smallp], w=[smallp])
            S.dma("pool", zsm.rearrange("s h c -> s (h c)"), smallp[0:R16, 0:64], r=[smallp], w=[Z])
            for half in range(2):
                s0 = half * 8
                H3 = big[:, :].rearrange("p (q n) -> p q n", n=128)
                S.dma("sp", H3, st_ssm[l, s0:s0 + 8].rearrange("s h q n -> (s h) q n"), w=[big])
                S.dma("pool", pa[:, 0:64], zx[s0:s0 + 8, :].rearrange("s (h q) -> (s h) q", h=16), r=[Z], w=[pa])
                S.dma("pool", pb_[:, 0:128], zBr[s0:s0 + 8].rearrange("s g r n -> (s g r) n"), r=[Z], w=[pb_])
                S.dma("pool", pb_[:, 128:256], zCr[s0:s0 + 8].rearrange("s g r n -> (s g r) n"), r=[Z], w=[pb_])
                S.dma("pool", pc[:, 0:4], zsm[s0:s0 + 8].rearrange("s h c -> (s h) c"), r=[Z], w=[pc])
                S.op("dve", lambda e: e.tensor_scalar(out=pa[:, 64:128], in0=pa[:, 0:64], scalar1=pc[:, 0:1], scalar2=None, op0=ALU.mult), r=[pa, pc], w=[pa])
                S.op("dve", lambda e: e.tensor_tensor(out=PR3.rearrange("p d e -> p d e"), in0=pa[:, 64:128].unsqueeze(2).to_broadcast([128, 64, 128]),
                                                      in1=pb_[:, 0:128].unsqueeze(1).to_broadcast([128, 64, 128]), op=ALU.mult), r=[pa, pb_], w=[prodb])
                S.op("dve", lambda e: e.scalar_tensor_tensor(out=big[:, :], in0=big[:, :], scalar=pc[:, 1:2], in1=prodb[:, :], op0=ALU.mult, op1=ALU.add),
                     r=[big, pc, prodb], w=[big])
                S.dma("sp", ssm_s[l, s0:s0 + 8].rearrange("s h q n -> (s h) q n"), H3, r=[big], w=[OUT])
                S.op("dve", lambda e: e.tensor_tensor(out=PR3, in0=H3, in1=pb_[:, 128:256].unsqueeze(1).to_broadcast([128, 64, 128]), op=ALU.mult), r=[big, pb_], w=[prodb])
                S.op("dve", lambda e: e.tensor_reduce(out=pa[:, 128:192], in_=PR3, axis=AX.X, op=ALU.add), r=[prodb], w=[pa])
                S.op("dve", lambda e: e.scalar_tensor_tensor(out=pa[:, 128:192], in0=pa[:, 0:64], scalar=pc[:, 2:3], in1=pa[:, 128:192], op0=ALU.mult, op1=ALU.add),
                     r=[pa, pc], w=[pa])
                S.dma("pool", osd[2, s0:s0 + 8].rearrange("s (h q) -> (s h) q", h=16), pa[:, 128:192], r=[pa], w=[OS])
            S.dma("pool", tmp[0:R16, :], osd[2], r=[OS], w=[tmp])
            S.dma("pool", bgs[0:R16, :], zd["cz"], r=[Z], w=[bgs])
            S.op("act", lambda e: e.activation(out=bgs[0:R16, :], in_=bgs[0:R16, :], func=AF.Silu), r=[bgs], w=[bgs])
            S.op("dve", lambda e: e.tensor_tensor(out=tmp[0:R16, :], in0=tmp[0:R16, :], in1=bgs[0:R16, :], op=ALU.mult), r=[tmp, bgs], w=[tmp])
            for g in range(2):
                S.op("act", lambda e, g=g: e.activation(out=sq[0:R16, g * 512:(g + 1) * 512], in_=tmp[0:R16, g * 512:(g + 1) * 512], func=AF.Square,
                                                        accum_out=st8[0:R16, 16 + g:17 + g]), r=[tmp], w=[sq, st8])
            S.op("act", lambda e: e.activation(out=st8[0:R16, 16:18], in_=st8[0:R16, 16:18], func=AF.Ln, scale=1.0 / 512, bias=EPS), r=[st8], w=[st8])
            S.op("act", lambda e: e.activation(out=st8[0:R16, 16:18], in_=st8[0:R16, 16:18], func=AF.Exp, scale=-0.5), r=[st8], w=[st8])
            S.op("dve", lambda e: e.tensor_tensor(out=tmp[0:R16, :].rearrange("p (g d) -> p g d", g=2), in0=tmp[0:R16, :].rearrange("p (g d) -> p g d", g=2),
                                                  in1=st8[0:R16, 16:18].unsqueeze(2).to_broadcast([R16, 2, 512]), op=ALU.mult), r=[tmp, st8], w=[tmp])
            S.op("dve", lambda e: e.tensor_tensor(out=tmp[0:R16, :], in0=tmp[0:R16, :], in1=snwb[0:R16, l, :], op=ALU.mult), r=[tmp, snwb], w=[tmp])
            S.op("dve", lambda e: e.tensor_tensor(out=tmp[0:R16, :], in0=tmp[0:R16, :], in1=gates[0:R16, 2, :], op=ALU.mult), r=[tmp, gates], w=[tmp])
            S.op("dve", lambda e: e.tensor_tensor(out=mix[0:R16, :], in0=mix[0:R16, :], in1=tmp[0:R16, :], op=ALU.add), r=[tmp, mix], w=[mix])
            S.dma("pool", g1bL[l][0:R16, :], modd[l, NB:NB + NS, 2 * D:3 * D], r=[dram_b["modd"]], w=[g1bL[l]])
            S.dma("pool", g2bL[l][0:R16, :], modd[l, NB:NB + NS, 5 * D:6 * D], r=[dram_b["modd"]], w=[g2bL[l]])
            dense_tail(l, g1bL[l], rows=R16)
            s_norm(l, n2w, 4, 3)
            mlp(l, g2bL[l], [xres], [hT], [uT], rows=R16)
        if RUN_SAMPLE:
            final_out(y_s, rows=R16)

        S.finish("sp")
        build.counts = dict(S.cnt)
    return nc


_NC = None


def kernel(**inp):
    global _NC
    f = lambda a: np.ascontiguousarray(np.asarray(a, dtype=np.float32))
    hcst = host_consts()
    if _NC is None:
        _NC = build()
    in_maps = []
    for c in range(RUN_CORES):
        ps, ss = slice(c * NB, (c + 1) * NB), slice(c * NS, (c + 1) * NS)
        m = {
            "xp": f(inp["x_prompt"][ps]), "xs": f(inp["x_sample"][ss, 0]),
            "cc": f(np.concatenate([np.asarray(inp["c_prompt"])[ps], np.asarray(inp["c_sample"])[ss]], 0)),
            "cache_k": f(np.asarray(inp["cache_win_k"])[:, ss].reshape(DEPTH, NS, 128, 256)),
            "cache_v": f(np.asarray(inp["cache_win_v"])[:, ss].reshape(DEPTH, NS, 128, 256)),
            "st_ret": f(np.asarray(inp["state_ret"])[:, ss]), "st_ssm": f(np.asarray(inp["state_ssm"])[:, ss]),
            "st_conv": f(np.asarray(inp["state_conv"])[:, ss]),
            "rel_tab": f(inp["rel_bias_table"]), "sinks": f(inp["attn_sinks"]), "n1w": f(inp["norm1_w"]), "n2w": f(inp["norm2_w"]),
            "ada_w": f(inp["ada_w"]), "ada_b": f(inp["ada_b"]), "w_in": f(inp["w_in"]), "conv_w": f(inp["conv_w"]), "conv_b": f(inp["conv_b"]),
            "dt_bias": f(inp["dt_bias"]), "a_log": f(inp["A_log"]), "d_skip": f(inp["D_skip"]), "snw": f(inp["ssm_norm_w"]),
            "w_out": f(inp["w_out"]), "w_up": f(inp["w_up"]), "w_down": f(inp["w_down"]), "fnw": f(inp["final_norm_w"]),
        }
        for k, v in hcst.items():
            m["c_" + k] = v
        in_maps.append(m)
    res = run_bass_kernel_spmd(_NC, in_maps, core_ids=list(range(RUN_CORES)), **({'trace': True} if TRACE else {}))
    if TRACE:
        print('EXEC_NS', res.exec_time_ns, flush=True)
    R = res.results
    cat = lambda k, ax: np.concatenate([np.asarray(r[k]) for r in R], axis=ax)
    y_p = cat("y_p", 0)
    y_s = cat("y_s", 0).reshape(RUN_CORES * NS, 1, D)
    wk_p = cat("wk_p", 1).reshape(DEPTH, RUN_CORES * NB, 128, 4, 64)
    wv_p = cat("wv_p", 1).reshape(DEPTH, RUN_CORES * NB, 128, 4, 64)
    ret_p = cat("ret_p", 1); ssm_p = cat("ssm_p", 1); conv_p = cat("conv_p", 1)
    wk_s = cat("wk_s", 1).reshape(DEPTH, RUN_CORES * NS, 128, 4, 64)
    wv_s = cat("wv_s", 1).reshape(DEPTH, RUN_CORES * NS, 128, 4, 64)
    ret_s = cat("ret_s", 1); ssm_s = cat("ssm_s", 1); conv_s = cat("conv_s", 1)
    if DEBUG:
        kernel.dbg = np.asarray(R[0]["dbg"])
    return (y_p, y_s, wk_p, wv_p, ret_p, ssm_p, conv_p, wk_s, wv_s, ret_s, ssm_s, conv_s)
```

```python
import math
from contextlib import ExitStack
import numpy as np
import concourse.bass as bass
import concourse.mybir as mybir
from concourse.bass_utils import run_bass_kernel_spmd

F32 = mybir.dt.float32
BF16 = mybir.dt.bfloat16
AF = mybir.ActivationFunctionType
ALU = mybir.AluOpType
AX = mybir.AxisListType

NCORE = 8
D = 1024
SEQ = 2048
NB = 2
NS = 16
DEPTH = 2
PAST = 16384
DIN = 10256
DFF = 4096
EPS = 1e-6
NEG = -30000.0
RUN_B, RUN_T, RUN_SAMPLE, RUN_CORES = NB, 16, True, NCORE
RUN_STAGE = 9
DEBUG = False
T_LAST = 15
TRACE = False
DO_KV = True
DO_CONV = True
DBG_T = 1
RUN_SUB = 99
QORDER = [0, 4, 1, 5, 2, 6, 3, 7, 8, 12, 9, 13, 10, 14, 11, 15]
O_AQ, O_AK, O_AV, O_BQ, O_BK, O_BV, O_BG, O_CZ, O_XBC, O_DT, O_G = 0, 1024, 1280, 1536, 2048, 2560, 3584, 4608, 5632, 7168, 7184


class Buf:
    __slots__ = ("name", "w", "r", "excl")

    def __init__(self, name):
        self.name = name
        self.w = None
        self.r = []
        self.excl = False


class TT:
    def __init__(self, t, name, nb=1):
        self.t = t
        self.b = [Buf(name + str(i)) for i in range(nb)]

    def __getitem__(self, k):
        return self.t[k]


class Sync:
    def __init__(self, nc, stack, n_dma_sems=24, n_pool_sems=70):
        self.nc = nc
        self.eng = {"pe": nc.tensor, "dve": nc.vector, "act": nc.scalar, "pool": nc.gpsimd, "sp": nc.sync}
        self.sems = {}
        self.cnt = {}
        for e in self.eng:
            self.sems[e] = stack.enter_context(nc.semaphore("s_" + e))
            self.cnt[e] = 0
        self.dsems = []
        self.n_sp = n_dma_sems
        for i in range(n_dma_sems + n_pool_sems):
            k = "d%d" % i
            self.sems[k] = stack.enter_context(nc.semaphore("dq_%d" % i))
            self.cnt[k] = 0
            self.dsems.append(k)
        self.dnext = 0
        self.dnext_pool = 0
        self.waited = {e: {} for e in self.eng}

    def _wait(self, e, ev):
        if ev is None:
            return
        k, v = ev
        if self.waited[e].get(k, 0) >= v:
            return
        self.eng[e].wait_ge(self.sems[k], v)
        self.waited[e][k] = v

    @staticmethod
    def _bl(xs):
        out = []
        for x in xs:
            if isinstance(x, TT):
                out.extend(x.b)
            elif isinstance(x, Buf):
                out.append(x)
            else:
                out.extend(x)
        return out

    def _deps(self, e, reads, writes):
        for b in reads:
            self._wait(e, b.w)
            if b.excl:
                for ev in b.r:
                    if ev[0] != e:
                        self._wait(e, ev)
        for b in writes:
            if b.w is not None and (b.w[0] != e or e != "pe"):
                self._wait(e, b.w)
            for ev in b.r:
                if ev[0] != e or e != "pe":
                    self._wait(e, ev)

    def op(self, e, fn, r=(), w=(), serial=False):
        reads, writes = self._bl(r), self._bl(w)
        self._deps(e, reads, writes)
        if serial and self.cnt[e] > 0:
            self._wait(e, (e, self.cnt[e]))
        ins = fn(self.eng[e])
        self.cnt[e] += 1
        ins.then_inc(self.sems[e], 1)
        ev = (e, self.cnt[e])
        for b in reads:
            b.r = [x for x in b.r if x[0] != e] + [ev]
        for b in writes:
            b.w = ev
            b.r = []
        return ins

    def dma(self, q, out, in_, r=(), w=(), **kw):
        reads, writes = self._bl(r), self._bl(w)
        half = self.n_sp
        if q == "pool":
            k = self.dsems[half + self.dnext_pool]
            self.dnext_pool = (self.dnext_pool + 1) % (len(self.dsems) - half)
        else:
            k = self.dsems[self.dnext]
            self.dnext = (self.dnext + 1) % half
        if self.cnt[k] > 0:
            self._wait(q, (k, self.cnt[k]))
        self._deps(q, reads, writes)
        ins = self.eng[q].dma_start(out=out, in_=in_, **kw)
        self.cnt[k] += 16
        ins.then_inc(self.sems[k], 16)
        ev = (k, self.cnt[k])
        for b in reads:
            b.r = b.r + [ev]
        for b in writes:
            b.w = ev
            b.r = []
        return ins

    def barrier(self):
        evs = [(k, v) for k, v in self.cnt.items() if v > 0]
        for e in self.eng:
            for ev in evs:
                if ev[0] != e:
                    self._wait(e, ev)

    def finish(self, e="sp"):
        for k, v in self.cnt.items():
            if v > 0 and k != e:
                self._wait(e, (k, v))


def host_consts():
    c = {}
    theta = (1.0 / (10000.0 ** np.linspace(0.0, 1.0, 32, dtype=np.float32))).astype(np.float32)
    pos = np.arange(SEQ, dtype=np.float32)
    ang = (pos[:, None] * theta[None, :]).astype(np.float32)
    c["cosp"] = np.ascontiguousarray(np.cos(ang).astype(np.float32).reshape(16, 128, 32).transpose(1, 0, 2))
    c["sinp"] = np.ascontiguousarray(np.sin(ang).astype(np.float32).reshape(16, 128, 32).transpose(1, 0, 2))
    angs = (np.float32(PAST) * theta).astype(np.float32)
    c["coss"] = np.tile(np.cos(angs).astype(np.float32)[None, :], (128, 1))
    c["sins"] = np.tile(np.sin(angs).astype(np.float32)[None, :], (128, 1))
    lg = np.log(1.0 - 2.0 ** (-5.0 - np.arange(8, dtype=np.float64)))
    i = np.arange(128, dtype=np.float64)
    diff = i[None, :] - i[:, None]
    dec = np.where(diff[None] >= 0, np.exp(lg[:, None, None] * np.maximum(diff[None], 0.0)), 0.0) * 0.125
    c["decT"] = np.ascontiguousarray(dec.transpose(1, 0, 2)).astype(np.float32)
    c["qdec"] = np.exp(lg[None, :] * (i[:, None] + 1.0)).astype(np.float32)
    c["kdec"] = (np.exp(lg[None, :] * (127.0 - i[:, None])) * 0.125).astype(np.float32)
    gl = np.exp(lg * 128.0)
    gL = np.zeros((128, 4), np.float64)
    for h in range(8):
        gL[(h % 2) * 64:(h % 2) * 64 + 64, h // 2] = gl[h]
    c["gL"] = gL.astype(np.float32)
    c["g1p"] = np.tile(np.exp(lg), 16).astype(np.float32).reshape(128, 1)
    jj = np.arange(128)
    c["tri"] = (jj[:, None] <= jj[None, :]).astype(np.float32)
    c["mneg"] = np.where(jj[:, None] <= jj[None, :], 0.0, NEG).astype(np.float32)
    return c


def bucket_of(d):
    d = np.asarray(d)
    nf = np.maximum(d, 1).astype(np.float32)
    large = 16 + (np.log(nf / np.float32(16)) / np.float32(math.log(128 / 16)) * np.float32(16)).astype(np.int32)
    large = np.minimum(large, 31)
    return np.where(d < 16, d, large)


def build():
    nc = bass.Bass("TRN2", target_bir_lowering=False)
    dt_in = lambda n, s, dt=F32: nc.dram_tensor(n, list(s), dt, kind="ExternalInput").ap()
    dt_out = lambda n, s: nc.dram_tensor(n, list(s), F32, kind="ExternalOutput").ap()
    dt_int = lambda n, s, dt=F32: nc.dram_tensor(n, list(s), dt, kind="Internal").ap()
    xp = dt_in("xp", [NB, SEQ, D]); xs = dt_in("xs", [NS, D]); cc = dt_in("cc", [NB + NS, D])
    cache_k = dt_in("cache_k", [DEPTH, NS, 128, 256]); cache_v = dt_in("cache_v", [DEPTH, NS, 128, 256])
    st_ret = dt_in("st_ret", [DEPTH, NS, 8, 64, 128]); st_ssm = dt_in("st_ssm", [DEPTH, NS, 16, 64, 128])
    st_conv = dt_in("st_conv", [DEPTH, NS, 3, 1536])
    rel_tab = dt_in("rel_tab", [32, 16]); sinks = dt_in("sinks", [DEPTH, 16])
    n1w = dt_in("n1w", [DEPTH, D]); n2w = dt_in("n2w", [DEPTH, D])
    ada_w = dt_in("ada_w", [DEPTH, D, 6 * D]); ada_b = dt_in("ada_b", [DEPTH, 6 * D])
    w_in = dt_in("w_in", [DEPTH, D, DIN]); conv_w = dt_in("conv_w", [DEPTH, 4, 1536]); conv_b = dt_in("conv_b", [DEPTH, 1536])
    dt_bias = dt_in("dt_bias", [DEPTH, 16]); a_log = dt_in("a_log", [DEPTH, 16]); d_skip = dt_in("d_skip", [DEPTH, 16])
    snw = dt_in("snw", [DEPTH, D]); w_out = dt_in("w_out", [DEPTH, D, D]); w_up = dt_in("w_up", [DEPTH, D, DFF])
    w_down = dt_in("w_down", [DEPTH, DFF, D]); fnw = dt_in("fnw", [D])
    hc = {k: dt_in("c_" + k, v.shape) for k, v in host_consts().items()}

    y_p = dt_out("y_p", [NB, SEQ, D]); y_s = dt_out("y_s", [NS, D])
    wk_p = dt_out("wk_p", [DEPTH, NB, 128, 256]); wv_p = dt_out("wv_p", [DEPTH, NB, 128, 256])
    ret_p = dt_out("ret_p", [DEPTH, NB, 8, 64, 128]); ssm_p = dt_out("ssm_p", [DEPTH, NB, 16, 64, 128])
    conv_p = dt_out("conv_p", [DEPTH, NB, 3, 1536])
    wk_s = dt_out("wk_s", [DEPTH, NS, 128, 256]); wv_s = dt_out("wv_s", [DEPTH, NS, 128, 256])
    ret_s = dt_out("ret_s", [DEPTH, NS, 8, 64, 128]); ssm_s = dt_out("ssm_s", [DEPTH, NS, 16, 64, 128])
    conv_s = dt_out("conv_s", [DEPTH, NS, 3, 1536])
    dbg = dt_out("dbg", [3, 128, D]) if DEBUG else None

    winT = dt_int("winT", [DEPTH, 21, 128, 4096], BF16); adaT = dt_int("adaT", [DEPTH, 12, 128, 4096], BF16)
    woutT = dt_int("woutT", [DEPTH, 2, 128, 4096], BF16); wupT = dt_int("wupT", [DEPTH, 8, 128, 4096], BF16)
    wdownT = dt_int("wdownT", [DEPTH, 8, 128, 4096], BF16)
    vecx = dt_int("vecx", [16, 384]); vecf = dt_int("vecf", [16, 384]); modd = dt_int("modd", [DEPTH, NB + NS, 6 * D]); zsd = dt_int("zsd", [NS, DIN])
    osd = dt_int("osd", [3, NS, D])

    with ExitStack() as st:
        S = Sync(nc, st)
        sb = lambda n, s, dt=F32, nb=1: TT(st.enter_context(nc.sbuf_tensor(n, list(s), dt)), n, nb)
        def pb(n, s, dt=F32, nb=1):
            t_ = TT(st.enter_context(nc.psum_tensor(n, list(s), dt)), n, nb)
            for b_ in t_.b:
                b_.excl = True
            return t_
        dram_b = {n: Buf(n) for n in ["winb", "adab", "woutb", "wupb", "wdownb", "vecx", "vecf", "modd", "zsd", "osd", "out"]}
        OUT = dram_b["out"]

        def cast(key, dst, src):
            b_ = Buf(key)
            dram_b.setdefault(key, []).append(b_)
            S.dma("pool", dst, src, w=[b_])

        def tl(T_, l, ti, n=512):
            return T_[l, ti, :, 0:8 * n].rearrange("p (k n) -> p k n", k=8)

        def srcv(w, l, r0, c0, n):
            return w[l, r0:r0 + 1024, c0:c0 + n].rearrange("(k p) n -> p k n", p=128)

        for l in range(DEPTH):
            for ct in range(12):
                cast("adab%d" % l, tl(adaT, l, ct), srcv(ada_w, l, 0, ct * 512, 512))
        def cast_layer(l):
            for j, h in enumerate(QORDER):
                cast("winb%d" % l, tl(winT, l, j // 8)[:, :, (j % 8) * 64:(j % 8 + 1) * 64], srcv(w_in, l, 0, h * 64, 64))
            for ti in range(2, 14):
                cast("winb%d" % l, tl(winT, l, ti), srcv(w_in, l, 0, ti * 512, 512))
            for j in range(6):
                cast("winb%d" % l, tl(winT, l, 14 + j), srcv(w_in, l, 0, O_G + j * 512, 512))
            cast("winb%d" % l, tl(winT, l, 20, 16), srcv(w_in, l, 0, O_DT, 16))
            for j in range(2):
                cast("woutb%d" % l, tl(woutT, l, j), srcv(w_out, l, 0, j * 512, 512))
            for j in range(8):
                cast("wupb%d" % l, tl(wupT, l, j), srcv(w_up, l, 0, j * 512, 512))
            for ct in range(2):
                for sl in range(4):
                    cast("wdownb%d" % l, tl(wdownT, l, ct * 4 + sl), srcv(w_down, l, sl * 1024, ct * 512, 512))


        cast_done = {1: False}
        cast_layer(0)

        identf = sb("identf", [128, 128])
        ident = sb("ident", [128, 128], BF16)
        wbuf = [sb("wbuf%d" % i, [128, 8, 512], BF16) for i in range(3)]
        PD = [pb("PD%d" % i, [128, 512]) for i in range(2)]
        PTb = pb("PTb", [128, 8, 128], BF16, nb=2)
        PS = pb("PS", [128, 512])
        PO = pb("PO", [128, 2048], F32, nb=4)
        xres = sb("xres", [128, D])
        sq = sb("sq", [128, D])
        st8 = sb("st8", [128, 32])
        xn = sb("xn", [128, D], BF16)
        tmp = sb("tmp", [128, D])
        tmp2 = sb("tmp2", [128, D])
        scT = sb("scT", [128, 8, NB + NS], BF16)
        n1c = sb("n1c", [128, DEPTH, 8])
        n2c = sb("n2c", [128, DEPTH, 8])
        cwc = sb("cwc", [128, DEPTH, 4, 12])
        cbc = sb("cbc", [128, DEPTH, 12])
        dtb = sb("dtb", [128, DEPTH, 16])
        Arow = sb("Arow", [128, DEPTH, 16])
        Dsk = sb("Dsk", [128, DEPTH, 16])
        esink = sb("esink", [128, DEPTH, 16])
        snwb = sb("snwb", [128, DEPTH, D])
        fnwb = sb("fnwb", [128, D])
        hT = sb("hT", [128, 8, 128], BF16)
        hT.b = [Buf('hT_%d' % i_) for i_ in range(8)]
        mix = sb("mix", [128, D])
        g1bL = [sb("g1b%d" % l, [128, D]) for l in range(DEPTH)]
        g2bL = [sb("g2b%d" % l, [128, D]) for l in range(DEPTH)]
        modcL = [sb("modc%d" % l, [128, 4, 8]) for l in range(DEPTH)]
        A1L = [sb("A1%d" % l, [128, 8]) for l in range(DEPTH)]
        A2L = [sb("A2%d" % l, [128, 8]) for l in range(DEPTH)]
        gates = sb("gates", [128, 3, D])
        gates.b = [Buf('gates_%d' % i_) for i_ in range(6)]
        bgs = sb("bgs", [128, D])
        bgs.b = [Buf('bgs_%d' % i_) for i_ in range(2)]
        uT = sb("uT", [128, 32, 128], BF16)
        uT.b = [Buf('uT_%d' % i_) for i_ in range(8)]
        stP = st.enter_context(ExitStack())
        sbP = lambda n, s, dt=F32, nb=1: TT(stP.enter_context(nc.sbuf_tensor(n, list(s), dt)), n, nb)
        tri = sbP("tri", [128, 128])
        ones = sbP("ones", [128, 128])
        mnegb = sbP("mnegb", [128, 128], BF16)
        Jb = sbP("Jb", [128, 128], BF16)
        decT = sbP("decT", [128, 8, 128])
        qdec = sbP("qdec", [128, 8])
        kdec = sbP("kdec", [128, 8])
        gL = sbP("gL", [128, 4])
        cosp = sbP("cosp", [128, 16, 32])
        sinp = sbP("sinp", [128, 16, 32])
        BT = sbP("BT", [128, 16, 2, 128], BF16)
        qT = sbP("qT", [128, 8, 128], BF16)
        qT.b = [Buf('qT_%d' % i_) for i_ in range(8)]
        kT = [sbP("kT%d" % l, [128, 2, 256], BF16) for l in range(DEPTH)]
        Vx = [sbP("Vx%d" % l, [128, 2, 4, 65], BF16) for l in range(DEPTH)]
        PTa = sbP("PTa", [128, 2, 512], BF16)
        rec = sbP("rec", [128, 16])
        rtq, rtk = [], []
        for i_ in range(4):
            r_ = TT(tmp.t[:, i_ * 256:(i_ + 1) * 256].rearrange("p (h f) -> p h f", h=8), "rtq%d" % i_)
            r_.b = tmp.b
            rtq.append(r_)
            r_ = TT(tmp2.t[:, i_ * 256:(i_ + 1) * 256].rearrange("p (h f) -> p h f", h=8), "rtk%d" % i_)
            r_.b = tmp2.b
            rtk.append(r_)
        bqk = sbP("bqk", [128, 1024])
        bqk.b = [Buf('bqk_%d' % i_) for i_ in range(2)]
        czs = sbP("czs", [128, D])
        czs.b = [Buf('czs_%d' % i_) for i_ in range(2)]
        qrot = sbP("qrot", [128, 512], BF16)
        krot = sbP("krot", [128, 512], BF16)
        qd = sbP("qd", [128, 512], BF16)
        kd = sbP("kd", [128, 512], BF16)
        qkT = sbP("qkT", [128, 12, 128], BF16)
        bv = sbP("bv", [128, D], BF16)
        bv.b = [Buf('bv_%d' % i_) for i_ in range(2)]
        Sret = [sbP("Sret%d" % l, [128, 4, 128]) for l in range(DEPTH)]
        Sbf = [sbP("Sbf%d" % l, [128, 4, 128], BF16) for l in range(DEPTH)]
        xbcT = sbP("xbcT", [128, 12, 131])
        xbcT.b = [Buf('xbcT_%d' % i_) for i_ in range(12)]
        chist = [sbP("chist%d" % l, [128, 12, 3]) for l in range(DEPTH)]
        xcT = sbP("xcT", [128, 12, 128])
        bcTb = sbP("bcTb", [128, 4, 128], BF16)
        dts = sbP("dts", [128, 8, 16])
        tabT = TT(dts.t[0:16, 0:2, :].rearrange("p a b -> p (a b)"), "tabT"); tabT.b = dts.b
        vx = TT(bqk.t[0:16, 0:384], "vx"); vx.b = bqk.b
        dtAb = sbP("dtAb", [128, 4, 128])
        LT = sbP("LT", [128, 8, 128])
        MT = sbP("MT", [128, 16, 128], BF16)
        xdt = sbP("xdt", [128, D], BF16)
        xw = sbP("xw", [128, D], BF16)
        Btm = sbP("Btm", [128, 256], BF16)
        hS = [sbP("hS%d" % l, [128, D]) for l in range(DEPTH)]
        hSb = [sbP("hSb%d" % l, [128, D], BF16) for l in range(DEPTH)]
        S.op("pool", lambda e: e.memset(identf[:], 0.0), w=[identf])
        S.op("pool", lambda e: e.affine_select(out=identf[:], in_=identf[:], pattern=[[-1, 128]], compare_op=ALU.not_equal,
                                               fill=1.0, base=0, channel_multiplier=1), r=[identf], w=[identf])
        S.op("dve", lambda e: e.tensor_copy(out=ident[:], in_=identf[:]), r=[identf], w=[ident])
        S.op("pool", lambda e: e.memset(ones[:], 1.0), w=[ones])
        for t, k in [(tri, "tri"), (decT, "decT"), (qdec, "qdec"), (kdec, "kdec"), (gL, "gL"), (cosp, "cosp"), (sinp, "sinp")]:
            S.dma("sp", t[:], hc[k], w=[t])
        S.dma("pool", mnegb[:], hc["mneg"], w=[mnegb])

        vxf = TT(bqk.t[0:16, 384:768], "vxf"); vxf.b = bqk.b
        S.dma("sp", tabT[:], rel_tab.rearrange("b h -> h b"), w=[tabT], allow_slow_non_contiguous=True)
        S.op("dve", lambda e: e.memset(vx[:], NEG), w=[vx])
        S.op("dve", lambda e: e.memset(vxf[:], NEG), w=[vxf])
        bk = bucket_of(np.arange(128))
        d0 = 0
        while d0 < 128:
            d1 = d0
            while d1 + 1 < 128 and bk[d1 + 1] == bk[d0]:
                d1 += 1
            b = int(bk[d0])
            S.op("dve", lambda e, lo=255 - d1, hi=255 - d0 + 1, b=b: e.tensor_scalar(
                out=vx[:, lo:hi], in0=vx[:, lo:hi], scalar1=0.0, scalar2=tabT[:, b:b + 1], op0=ALU.mult, op1=ALU.add),
                r=[tabT, vx], w=[vx])
            S.op("dve", lambda e, lo=127 + d0, hi=127 + d1 + 1, b=b: e.tensor_scalar(
                out=vxf[:, lo:hi], in0=vxf[:, lo:hi], scalar1=0.0, scalar2=tabT[:, b:b + 1], op0=ALU.mult, op1=ALU.add),
                r=[tabT, vxf], w=[vxf])
            d0 = d1 + 1
        S.dma("sp", vecx, vx[:], r=[vx], w=[dram_b["vecx"]])
        S.dma("sp", vecf, vxf[:], r=[vxf], w=[dram_b["vecf"]])
        S.op("pool", lambda e: e.memset(ones[:], 0.0), w=[ones])
        S.op("pool", lambda e: e.affine_select(out=ones[:], in_=ones[:], pattern=[[1, 128]], compare_op=ALU.not_equal,
                                               fill=1.0, base=-127, channel_multiplier=1), r=[ones], w=[ones])
        S.op("dve", lambda e: e.tensor_copy(out=Jb[:], in_=ones[:]), r=[ones], w=[Jb])
        S.op("pool", lambda e: e.memset(ones[:], 1.0), r=[Jb], w=[ones])
        for h4 in range(4):
            for hh in range(4):
                h = 4 * h4 + hh
                for blk, off in ((1, 0), (0, 128)):
                    src = bass.AP(vecf.tensor, off + 384 * h, [[1, 128], [1, 128]])
                    S.dma("sp", tmp[:, (hh * 2 + blk) * 128:(hh * 2 + blk + 1) * 128], src, r=[dram_b["vecf"]], w=[tmp])
            S.op("dve", lambda e: e.tensor_scalar(out=xn[:], in0=tmp[:], scalar1=8.0, scalar2=None, op0=ALU.mult), r=[tmp], w=[xn])
            for hf2 in range(2):
                pd = PD[hf2]
                S.op("pe", lambda e, pd=pd, hf2=hf2: e.matmul(out=pd[:], lhsT=Jb[:], rhs=xn[:, hf2 * 512:(hf2 + 1) * 512], start=True, stop=True), r=[Jb, xn], w=[pd])
                S.op("act", lambda e, pd=pd, hf2=hf2, h4=h4: e.copy(out=BT[:, 4 * h4 + 2 * hf2:4 * h4 + 2 * hf2 + 2, :, :].rearrange("p h b q -> p (h b q)"), in_=pd[:]),
                     r=[pd], w=[BT])

        csb, csil, adabias, modt = tmp, xn, tmp2, sq
        sq.b = [Buf('sq_a'), Buf('sq_b')]
        xn.b = [Buf('xn_a'), Buf('xn_b')]
        sqh, xnh = sq.b, xn.b
        S.dma("sp", csb[0:NB + NS, :], cc, w=[csb])
        S.op("act", lambda e: e.activation(out=csil[0:NB + NS, :], in_=csb[0:NB + NS, :], func=AF.Silu), r=[csb], w=[csil])
        for k in range(8):
            S.op("pe", lambda e, k=k: e.transpose(out=PTb[0:128, k, 0:NB + NS], in_=csil[0:NB + NS, k * 128:(k + 1) * 128],
                                                  identity=ident[0:NB + NS, 0:NB + NS]), r=[csil, ident], w=[PTb])
        S.op("dve", lambda e: e.tensor_copy(out=scT[:], in_=PTb[:, :, 0:NB + NS]), r=[PTb], w=[scT])
        R = NB + NS
        for l in range(DEPTH):
            for ct in range(12):
                wb_ = wbuf[ct % 3]
                S.dma("sp", wb_[:], tl(adaT, l, ct), r=[dram_b["adab%d" % l]], w=[wb_])
                S.dma("pool", adabias[0:R, 0:512], bass.AP(ada_b.tensor, l * 6 * D + ct * 512, [[0, R], [1, 512]]), w=[adabias])
                pd = PD[ct % 2]
                for k in range(8):
                    S.op("pe", lambda e, k=k, pd=pd, wb_=wb_: e.matmul(out=pd[0:R, :], lhsT=scT[:, k, :], rhs=wb_[:, k, :],
                                                                     start=(k == 0), stop=(k == 7)), r=[scT, wb_], w=[pd])
                S.op("dve", lambda e, pd=pd: e.tensor_tensor(out=modt[0:R, 0:512], in0=pd[0:R, :], in1=adabias[0:R, 0:512], op=ALU.add),
                     r=[pd, adabias], w=[modt])
                S.dma("sp", modd[l, :, ct * 512:(ct + 1) * 512], modt[0:R, 0:512], r=[modt], w=[dram_b["modd"]])

        for l in range(DEPTH):
            S.dma("sp", n1c[:, l, :], n1w[l].rearrange("(k p) -> p k", p=128), w=[n1c], allow_slow_non_contiguous=True)
            S.dma("sp", n2c[:, l, :], n2w[l].rearrange("(k p) -> p k", p=128), w=[n2c], allow_slow_non_contiguous=True)
        for l in range(DEPTH):
            for i in range(4):
                S.dma("sp", cwc[:, l, i, :], conv_w[l, i].rearrange("(j p) -> p j", p=128), w=[cwc], allow_slow_non_contiguous=True)
            S.dma("sp", cbc[:, l, :], conv_b[l].rearrange("(j p) -> p j", p=128), w=[cbc], allow_slow_non_contiguous=True)
        bc16 = lambda t, l: bass.AP(t.tensor, l * 16, [[0, 128], [1, 16]])
        for l in range(DEPTH):
            S.dma("sp", dtb[:, l, :], bc16(dt_bias, l), w=[dtb])
            S.dma("sp", Arow[:, l, :], bc16(a_log, l), w=[Arow])
            S.dma("sp", Dsk[:, l, :], bc16(d_skip, l), w=[Dsk])
            S.dma("sp", esink[:, l, :], bc16(sinks, l), w=[esink])
        S.op("act", lambda e: e.activation(out=Arow[:], in_=Arow[:], func=AF.Exp), r=[Arow], w=[Arow])
        S.op("dve", lambda e: e.tensor_scalar(out=Arow[:], in0=Arow[:], scalar1=-1.0, scalar2=None, op0=ALU.mult), r=[Arow], w=[Arow])
        S.op("act", lambda e: e.activation(out=esink[:], in_=esink[:], func=AF.Exp), r=[esink], w=[esink])
        for l in range(DEPTH):
            S.dma("sp", snwb[:, l, :], bass.AP(snw.tensor, l * D, [[0, 128], [1, D]]), w=[snwb])
        S.dma("sp", fnwb[:], bass.AP(fnw.tensor, 0, [[0, 128], [1, D]]), w=[fnwb])

        xcs = sq
        innT = TT(MT.t[:, 0:8, :], "innT"); innT.b = MT.b
        xres1 = sbP("xres1", [128, D])
        xresL = [xres, xres1]
        hT_main = hT
        hTm = [sbP("hTm%d" % i, [128, 8, 128], BF16) for i in range(2)]
        for h_ in hTm:
            h_.b = [Buf("hTm_%d" % i_) for i_ in range(8)]
        uTb = sbP("uTb", [128, 32, 128], BF16)
        uTb.b = [Buf('uTb_%d' % i_) for i_ in range(8)]
        uTL = [uT, uTb]

        wq = {"i": 0}

        def wload(src_ap, ncols=512, dep=()):
            wb_ = wbuf[wq["i"] % 3]
            wq["i"] += 1
            S.dma("sp", wb_[:, :, 0:ncols], src_ap, r=[dep], w=[wb_])
            return wb_

        def win_tile(l, c0, n=512):
            ti = 20 if c0 == O_DT else (14 + (c0 - O_G) // 512 if c0 >= O_G else c0 // 512)
            return wload(tl(winT, l, ti, n), n, dram_b["winb%d" % l]), None

        def rms_stats(xt, rows, col):
            S.op("act", lambda e: e.activation(out=sq[0:rows, :], in_=xt[0:rows, :], func=AF.Square, accum_out=st8[0:rows, col:col + 1]),
                 r=[xt], w=[sq, st8])
            S.op("act", lambda e: e.activation(out=st8[0:rows, col + 1:col + 2], in_=st8[0:rows, col:col + 1], func=AF.Ln,
                                               scale=1.0 / D, bias=EPS), r=[st8], w=[st8])
            S.op("act", lambda e: e.activation(out=st8[0:rows, col + 2:col + 3], in_=st8[0:rows, col + 1:col + 2], func=AF.Exp, scale=-0.5), r=[st8], w=[st8])
            return col + 2

        def norm_to_hT(A, shT, shj):
            sh = shT[:, shj, :]
            c = rms_stats(xres, 128, 0)
            S.op("dve", lambda e: e.tensor_scalar(out=xn[:], in0=xres[:], scalar1=st8[:, c:c + 1], scalar2=None, op0=ALU.mult),
                 r=[xres, st8], w=[xn])
            for k in range(8):
                S.op("pe", lambda e, k=k: e.transpose(out=PTb[:, k, :], in_=xn[:, k * 128:(k + 1) * 128], identity=ident[:]),
                     r=[xn, ident], w=[PTb])
            for k in range(8):
                S.op("act", lambda e, k=k: e.activation(out=hT[:, k, :], in_=PTb[:, k, :], func=AF.Identity,
                                                        scale=A[:, k:k + 1], bias=sh[:, k:k + 1]), r=[PTb, A, shT], w=[hT.b[k]])

        def mm_tm(pd, wb_, c0, n, src=None):
            src = src or hT
            for k in range(8):
                S.op("pe", lambda e, k=k: e.matmul(out=pd[:, 0:n], lhsT=src[:, k, :], rhs=wb_[:, k, c0:c0 + n],
                                                   start=(k == 0), stop=(k == 7)), r=[src, wb_], w=[pd])

        def mm_fm(pd, wb_, c0):
            for k in range(8):
                S.op("pe", lambda e, k=k: e.matmul(out=pd[:, 0:128], lhsT=wb_[:, k, c0:c0 + 128], rhs=hT[:, k, :],
                                                   start=(k == 0), stop=(k == 7)), r=[hT, wb_], w=[pd])

        pdi = {"i": 0}

        def nextpd():
            pdi["i"] += 1
            return PD[pdi["i"] % 2]

        def layer_chunk(l, b, t, last_layer):
            modc, A1, A2, g1b, g2b = modcL[l], A1L[l], A2L[l], g1bL[l], g2bL[l]
            if t == 0:
                for j, col in enumerate((0, 1, 3, 4)):
                    S.dma("pool", modc[:, j, :], modd[l, b, col * D:(col + 1) * D].rearrange("(k p) -> p k", p=128),
                          r=[dram_b["modd"]], w=[modc], allow_slow_non_contiguous=True)
                S.op("dve", lambda e: e.scalar_tensor_tensor(out=A1[:], in0=modc[:, 1, :], scalar=1.0, in1=n1c[:, l, :],
                                                             op0=ALU.add, op1=ALU.mult), r=[modc, n1c], w=[A1])
                S.op("dve", lambda e: e.scalar_tensor_tensor(out=A2[:], in0=modc[:, 3, :], scalar=1.0, in1=n2c[:, l, :],
                                                             op0=ALU.add, op1=ALU.mult), r=[modc, n2c], w=[A2])
                S.dma("pool", g1b[:], bass.AP(modd.tensor, (l * R + b) * 6 * D + 2 * D, [[0, 128], [1, D]]), r=[dram_b["modd"]], w=[g1b])
                S.dma("pool", g2b[:], bass.AP(modd.tensor, (l * R + b) * 6 * D + 5 * D, [[0, 128], [1, D]]), r=[dram_b["modd"]], w=[g2b])
            norm_to_hT(A1, modc, 0)

            def gPA():
                if t > 0:
                    S.op("pool", lambda e: e.tensor_copy(out=kT[l][:, :, 0:128], in_=kT[l][:, :, 128:256]), r=[kT[l]], w=[kT[l]])
                    S.op("pool", lambda e: e.tensor_copy(out=Vx[l][:, 0, :, :], in_=Vx[l][:, 1, :, :]), r=[Vx[l]], w=[Vx[l]])
                else:
                    S.op("pool", lambda e: e.memset(Vx[l][:, :, :, 64:65], 1.0), w=[Vx[l]])
                for half in range(2):
                    wb_, wr = win_tile(l, O_AQ + half * 512)
                    for i in range(4):
                        pd = nextpd()
                        mm_fm(pd, wb_, i * 128)
                        S.op("act", lambda e, pd=pd, i=i: e.copy(out=qT[:, half * 4 + i, :], in_=pd[:, 0:128]), r=[pd], w=[qT.b[half * 4 + i]])
                    yield
                wb_, wr = win_tile(l, O_AK)
                for i in range(2):
                    pd = nextpd()
                    mm_fm(pd, wb_, i * 128)
                    S.op("act", lambda e, pd=pd, i=i: e.copy(out=kT[l][:, i, 128:256], in_=pd[:, 0:128]), r=[pd], w=[kT[l]])
                pd = nextpd()
                mm_tm(pd, wb_, 256, 256)
                S.op("dve", lambda e, pd=pd: e.tensor_copy(out=Vx[l][:, 1, :, 0:64], in_=pd[:, 0:256].rearrange("p (g d) -> p g d", g=4)),
                     r=[pd], w=[Vx[l]])
                if t == T_LAST and DO_KV:
                    S.op("act", lambda e, pd=pd: e.copy(out=tmp2[:, 256:512], in_=pd[:, 0:256]), r=[pd], w=[tmp2])
                    pd2 = nextpd()
                    mm_tm(pd2, wb_, 0, 256)
                    S.op("act", lambda e, pd2=pd2: e.copy(out=tmp2[:, 0:256], in_=pd2[:, 0:256]), r=[pd2], w=[tmp2])
                    S.dma("pool", wk_p[l, b], tmp2[:, 0:256], r=[tmp2], w=[OUT])
                    S.dma("pool", wv_p[l, b], tmp2[:, 256:512], r=[tmp2], w=[OUT])
                yield
                for j in range(6):
                    wb_, wr = win_tile(l, O_G + j * 512)
                    pd = nextpd()
                    mm_tm(pd, wb_, 0, 512)
                    S.op("act", lambda e, pd=pd, j=j: e.activation(out=gates[:, j // 2, (j % 2) * 512:(j % 2) * 512 + 512], in_=pd[:],
                                                                   func=AF.Sigmoid), r=[pd], w=[gates.b[j]])
                    yield

            def gMA():
                blocks = (1,) if t == 0 else (0, 1)
                for g in range(4):
                    hf = (g % 2) * 64
                    for bi, blk in enumerate(blocks):
                        pd = nextpd()
                        S.op("pe", lambda e, g=g, hf=hf, blk=blk, pd=pd: e.matmul(
                            out=pd[:], lhsT=kT[l][hf:hf + 64, g // 2, blk * 128:(blk + 1) * 128],
                            rhs=qT[hf:hf + 64, (g // 2) * 4:(g // 2) * 4 + 4, :], start=True, stop=False), r=[kT[l], qT], w=[pd], serial=True)
                        S.op("pe", lambda e, g=g, blk=blk, pd=pd: e.matmul(
                            out=pd[:], lhsT=ident[:], rhs=BT[:, 4 * g:4 * g + 4, blk, :], start=False, stop=True), r=[ident, BT], w=[pd], serial=True)
                        S.op("act", lambda e, bi=bi, pd=pd: e.activation(out=PTa[:, bi, :], in_=pd[:], func=AF.Exp, scale=0.125), r=[pd], w=[PTa])
                    yield
                    for hq in range(4):
                        h = 4 * g + hq
                        hs = h % 8
                        for bi, blk in enumerate(blocks):
                            S.op("pe", lambda e, hs=hs, hq=hq, blk=blk, bi=bi: e.matmul(
                                out=PO[:, hs * 128:hs * 128 + 65], lhsT=PTa[:, bi, hq * 128:(hq + 1) * 128], rhs=Vx[l][:, blk, g, :],
                                start=(bi == 0), stop=(bi == len(blocks) - 1)), r=[PTa, Vx[l]], w=[PO.b[hs // 4]])
                    yield
                    if g % 2 == 1:
                        r_ = g // 2
                        po3 = PO[:, 0:1024].rearrange("p (h c) -> p h c", h=8)
                        S.op("dve", lambda e, r_=r_: e.tensor_tensor(out=rec[:, 8 * r_:8 * r_ + 8], in0=po3[:, :, 64], in1=esink[:, l, 8 * r_:8 * r_ + 8], op=ALU.add),
                             r=[PO.b[0], PO.b[1], esink], w=[rec])
                        S.op("dve", lambda e, r_=r_: e.reciprocal(out=rec[:, 8 * r_:8 * r_ + 8], in_=rec[:, 8 * r_:8 * r_ + 8]), r=[rec], w=[rec])
                        S.op("dve", lambda e, r_=r_: e.tensor_tensor(out=czs[:, r_ * 512:(r_ + 1) * 512].rearrange("p (h d) -> p h d", h=8), in0=po3[:, :, 0:64],
                                                              in1=rec[:, 8 * r_:8 * r_ + 8].unsqueeze(2).to_broadcast([128, 8, 64]), op=ALU.mult),
                             r=[PO.b[0], PO.b[1], rec], w=[czs])
                        yield
                S.op("dve", lambda e: e.tensor_tensor(out=czs[:], in0=czs[:], in1=gates[:, 0, :], op=ALU.mult), r=[czs, gates], w=[czs])
                S.op("dve", lambda e: e.tensor_tensor(out=mix[:], in0=mix[:], in1=czs[:], op=ALU.add), r=[czs, mix], w=[mix])

            def gPB():
                for j in range(2):
                    wb_, wr = win_tile(l, O_BQ + j * 512)
                    pd = nextpd()
                    mm_tm(pd, wb_, 0, 512)
                    S.op("act", lambda e, pd=pd, j=j: e.copy(out=bqk[:, j * 512:(j + 1) * 512], in_=pd[:]), r=[pd], w=[bqk.b[j]])
                for j in range(2):
                    wb_, wr = win_tile(l, O_BV + j * 512)
                    pd = nextpd()
                    mm_tm(pd, wb_, 0, 512)
                    S.op("act", lambda e, pd=pd, j=j: e.copy(out=bv[:, j * 512:(j + 1) * 512], in_=pd[:]), r=[pd], w=[bv.b[j]])
                    yield
                for j in range(2):
                    wb_, wr = win_tile(l, O_BG + j * 512)
                    pd = nextpd()
                    mm_tm(pd, wb_, 0, 512)
                    S.op("act", lambda e, pd=pd, j=j: e.activation(out=bgs[:, j * 512:(j + 1) * 512], in_=pd[:], func=AF.Silu), r=[pd], w=[bgs.b[j]])
                    yield
                S.op("pool", lambda e: e.tensor_tensor(out=bgs[:], in0=bgs[:], in1=gates[:, 1, :], op=ALU.mult), r=[bgs, gates], w=[bgs])

            def gMB():
                cb_ = cosp[:, t, :].unsqueeze(1).to_broadcast([128, 8, 32])
                sb_ = sinp[:, t, :].unsqueeze(1).to_broadcast([128, 8, 32])
                for j, (dst, sct, dect) in enumerate(((qrot, qd, qdec), (krot, kd, kdec))):
                    v4 = bqk[:, j * 512:(j + 1) * 512].rearrange("p (h f two) -> p h f two", h=8, two=2)
                    x1, x2 = v4[:, :, :, 0], v4[:, :, :, 1]
                    d4 = dst[:].rearrange("p (h f two) -> p h f two", h=8, two=2)
                    eng = "dve" if j == 0 else "pool"
                    rt = rtq if j == 0 else rtk
                    S.op(eng, lambda e, x1=x1: e.tensor_tensor(out=rt[0][:], in0=x1, in1=cb_, op=ALU.mult), r=[bqk, cosp], w=[rt[0]])
                    S.op(eng, lambda e, x2=x2: e.tensor_tensor(out=rt[1][:], in0=x2, in1=sb_, op=ALU.mult), r=[bqk, sinp], w=[rt[1]])
                    S.op(eng, lambda e, d4=d4: e.tensor_tensor(out=d4[:, :, :, 0], in0=rt[0][:], in1=rt[1][:], op=ALU.subtract), r=[rt[0], rt[1]], w=[dst])
                    S.op(eng, lambda e, x1=x1: e.tensor_tensor(out=rt[2][:], in0=x1, in1=sb_, op=ALU.mult), r=[bqk, sinp], w=[rt[2]])
                    S.op(eng, lambda e, x2=x2: e.tensor_tensor(out=rt[3][:], in0=x2, in1=cb_, op=ALU.mult), r=[bqk, cosp], w=[rt[3]])
                    S.op(eng, lambda e, d4=d4: e.tensor_tensor(out=d4[:, :, :, 1], in0=rt[2][:], in1=rt[3][:], op=ALU.add), r=[rt[2], rt[3]], w=[dst])
                    S.op(eng, lambda e, dst=dst, sct=sct, dect=dect: e.tensor_tensor(
                        out=sct[:].rearrange("p (h d) -> p h d", h=8), in0=dst[:].rearrange("p (h d) -> p h d", h=8),
                        in1=dect[:].unsqueeze(2).to_broadcast([128, 8, 64]), op=ALU.mult), r=[dst, dect], w=[sct])
                yield
                for gi, srcb in enumerate((qrot, qd, krot)):
                    for i in range(4):
                        S.op("pe", lambda e, srcb=srcb, i=i: e.transpose(out=PTb[:, i, :], in_=srcb[:, i * 128:(i + 1) * 128], identity=ident[:]),
                             r=[srcb, ident], w=[PTb])
                    S.op("act", lambda e, gi=gi: e.copy(out=qkT[:, gi * 4:gi * 4 + 4, :], in_=PTb[:, 0:4, :]), r=[PTb], w=[qkT])
                yield
                for h in range(8):
                    hf = (h % 2) * 64
                    S.op("pe", lambda e, h=h, hf=hf: e.matmul(out=PO[:, 1024 + h * 128:1024 + (h + 1) * 128], lhsT=qkT[hf:hf + 64, 8 + h // 2, :],
                                                              rhs=qkT[hf:hf + 64, h // 2, :], start=True, stop=True), r=[qkT], w=[PO.b[2 + h // 4]], serial=True)
                S.op("dve", lambda e: e.tensor_tensor(out=innT[:], in0=PO[:, 1024:2048].rearrange("p (h q) -> p h q", h=8), in1=decT[:], op=ALU.mult),
                     r=[PO.b[2], PO.b[3], decT], w=[innT])
                for h in range(8):
                    hf = (h % 2) * 64
                    S.op("pe", lambda e, h=h: e.matmul(out=PO[:, 1024 + h * 128:1024 + (h + 1) * 128], lhsT=innT[:, h, :],
                                                       rhs=bv[:, h * 128:(h + 1) * 128], start=True, stop=False), r=[innT, bv], w=[PO.b[2 + h // 4]])
                    S.op("pe", lambda e, h=h, hf=hf: e.matmul(out=PO[:, 1024 + h * 128:1024 + (h + 1) * 128], lhsT=qkT[hf:hf + 64, 4 + h // 2, :],
                                                              rhs=Sbf[l][hf:hf + 64, h // 2, :], start=False, stop=True), r=[qkT, Sbf[l]], w=[PO.b[2 + h // 4]], serial=True)
                yield
                for i in range(4):
                    pd = nextpd()
                    S.op("pe", lambda e, i=i, pd=pd: e.matmul(out=pd[:, 0:256], lhsT=kd[:, i * 128:(i + 1) * 128], rhs=bv[:, i * 256:(i + 1) * 256],
                                                              start=True, stop=True), r=[kd, bv], w=[pd])
                    for hfi in range(2):
                        rs = slice(hfi * 64, hfi * 64 + 64)
                        S.op("dve", lambda e, i=i, pd=pd, rs=rs, hfi=hfi: e.scalar_tensor_tensor(
                            out=Sret[l][rs, i, :], in0=Sret[l][rs, i, :], scalar=gL[rs, i:i + 1], in1=pd[rs, hfi * 128:(hfi + 1) * 128],
                            op0=ALU.mult, op1=ALU.add), r=[Sret[l], gL, pd, Sbf[l]], w=[Sret[l]])
                S.op("act", lambda e: e.copy(out=Sbf[l][:], in_=Sret[l][:]), r=[Sret[l]], w=[Sbf[l]])
                yield
                o3 = PO[:, 1024:2048].rearrange("p (h e) -> p h e", h=8)
                S.op("act", lambda e: e.activation(out=sq[:], in_=PO[:, 1024:2048], func=AF.Square), r=[PO.b[2], PO.b[3]], w=[sq])
                S.op("dve", lambda e: e.tensor_reduce(out=st8[:, 8:16], in_=sq[:].rearrange("p (h e) -> p h e", h=8), axis=AX.X, op=ALU.add),
                     r=[sq], w=[st8])
                S.op("act", lambda e: e.activation(out=st8[:, 8:16], in_=st8[:, 8:16], func=AF.Ln, scale=1.0 / 128, bias=EPS), r=[st8], w=[st8])
                S.op("act", lambda e: e.activation(out=st8[:, 8:16], in_=st8[:, 8:16], func=AF.Exp, scale=-0.5), r=[st8], w=[st8])
                S.op("dve", lambda e: e.tensor_tensor(out=tmp[:].rearrange("p (h e) -> p h e", h=8), in0=o3,
                                                      in1=st8[:, 8:16].unsqueeze(2).to_broadcast([128, 8, 128]), op=ALU.mult), r=[PO.b[2], PO.b[3], st8], w=[tmp])
                S.op("dve", lambda e: e.tensor_tensor(out=tmp[:], in0=tmp[:], in1=bgs[:], op=ALU.mult), r=[tmp, bgs], w=[tmp])
                S.op("dve", lambda e: e.tensor_tensor(out=mix[:], in0=mix[:], in1=tmp[:], op=ALU.add), r=[tmp, mix], w=[mix])


            def gPC():
                for j in range(2):
                    wb_, wr = win_tile(l, O_CZ + j * 512)
                    pd = nextpd()
                    mm_tm(pd, wb_, 0, 512)
                    S.op("act", lambda e, pd=pd, j=j: e.activation(out=czs[:, j * 512:(j + 1) * 512], in_=pd[:], func=AF.Silu), r=[pd], w=[czs.b[j]])
                    yield
                S.op("pool", lambda e: e.tensor_copy(out=xbcT[:, :, 0:3], in_=chist[l][:]), r=[chist[l]], w=[xbcT])
                for j3 in range(3):
                    wb_, wr = win_tile(l, O_XBC + j3 * 512)
                    for i in range(4):
                        pd = nextpd()
                        mm_fm(pd, wb_, i * 128)
                        S.op("act", lambda e, pd=pd, jj=j3 * 4 + i: e.copy(out=xbcT[:, jj, 3:131], in_=pd[:, 0:128]), r=[pd], w=[xbcT.b[j3 * 4 + i]])
                    yield
                yield
                S.op("pool", lambda e: e.tensor_copy(out=chist[l][:], in_=xbcT[:, :, 128:131]), r=[xbcT], w=[chist[l]])
                if t == T_LAST and DO_CONV:
                    for j in range(12):
                        S.op("pe", lambda e, j=j: e.transpose(out=PO[0:3, j * 128:(j + 1) * 128], in_=chist[l][:, j, :], identity=identf[:]),
                             r=[chist[l], identf], w=[PO.b[j // 4]])
                    S.op("act", lambda e: e.copy(out=tmp[0:3, 0:1024], in_=PO[0:3, 0:1024]), r=[PO.b[0], PO.b[1]], w=[tmp])
                    S.op("act", lambda e: e.copy(out=sq[0:3, 0:512], in_=PO[0:3, 1024:1536]), r=[PO.b[2]], w=[sq])
                    S.dma("pool", conv_p[l, b, :, 0:1024], tmp[0:3, 0:1024], r=[tmp], w=[OUT])
                    S.dma("pool", conv_p[l, b, :, 1024:1536], sq[0:3, 0:512], r=[sq], w=[OUT])
                wb_, wr = win_tile(l, O_DT, 16)
                pd = nextpd()
                mm_tm(pd, wb_, 0, 16)
                S.op("dve", lambda e, pd=pd: e.tensor_tensor(out=dts[:, 0, :], in0=pd[:, 0:16], in1=dtb[:, l, :], op=ALU.add), r=[pd, dtb], w=[dts])

            def gMC():
                for jj in range(12):
                    if jj % 4 == 0:
                        yield
                    eng = "dve"
                    S.op(eng, lambda e, jj=jj: e.tensor_scalar(out=xcT[:, jj, :], in0=xbcT[:, jj, 0:128], scalar1=cwc[:, l, 0, jj:jj + 1],
                                                               scalar2=cbc[:, l, jj:jj + 1], op0=ALU.mult, op1=ALU.add), r=[xbcT, cwc, cbc], w=[xcT])
                    for i in range(1, 4):
                        S.op(eng, lambda e, jj=jj, i=i: e.scalar_tensor_tensor(out=xcT[:, jj, :], in0=xbcT[:, jj, i:i + 128], scalar=cwc[:, l, i, jj:jj + 1],
                                                                               in1=xcT[:, jj, :], op0=ALU.mult, op1=ALU.add), r=[xbcT, cwc, xcT], w=[xcT])
                yield
                S.op("act", lambda e: e.activation(out=xcT[:], in_=xcT[:], func=AF.Silu), r=[xcT], w=[xcT])
                S.op("pool", lambda e: e.tensor_copy(out=bcTb[:], in_=xcT[:, 8:12, :]), r=[xcT], w=[bcTb])
                S.op("act", lambda e: e.activation(out=dts[:, 1, :], in_=dts[:, 0, :], func=AF.Exp), r=[dts], w=[dts])
                S.op("act", lambda e: e.activation(out=dts[:, 2, :], in_=dts[:, 1, :], func=AF.Ln, bias=1.0), r=[dts], w=[dts])
                S.op("dve", lambda e: e.tensor_tensor(out=dts[:, 3, :], in0=dts[:, 2, :], in1=Arow[:, l, :], op=ALU.mult), r=[dts, Arow], w=[dts])
                yield
                pd = nextpd()
                S.op("pe", lambda e, pd=pd: e.matmul(out=pd[:, 0:16], lhsT=tri[:], rhs=dts[:, 3, :], start=True, stop=True), r=[tri, dts], w=[pd])
                S.op("pe", lambda e, pd=pd: e.matmul(out=pd[:, 16:32], lhsT=ones[:], rhs=dts[:, 3, :], start=True, stop=True), r=[ones, dts], w=[pd])
                S.op("dve", lambda e, pd=pd: e.tensor_copy(out=dts[:, 4, :], in_=pd[:, 0:16]), r=[pd], w=[dts])
                S.op("dve", lambda e, pd=pd: e.tensor_scalar(out=dts[:, 5, :], in0=pd[:, 0:16], scalar1=-1.0, scalar2=None, op0=ALU.mult), r=[pd], w=[dts])
                S.op("dve", lambda e, pd=pd: e.tensor_tensor(out=dts[:, 6, :], in0=pd[:, 16:32], in1=dts[:, 4, :], op=ALU.subtract), r=[pd, dts], w=[dts])
                S.op("act", lambda e, pd=pd: e.activation(out=dts[:, 7, :], in_=pd[:, 16:32], func=AF.Exp), r=[pd], w=[dts])
                S.op("act", lambda e: e.activation(out=dts[:, 6, :], in_=dts[:, 6, :], func=AF.Exp), r=[dts], w=[dts])
                S.op("dve", lambda e: e.tensor_tensor(out=dts[:, 6, :], in0=dts[:, 6, :], in1=dts[:, 2, :], op=ALU.mult), r=[dts], w=[dts])
                S.op("act", lambda e: e.activation(out=dts[:, 1, :], in_=dts[:, 4, :], func=AF.Exp), r=[dts], w=[dts])
                for g in range(2):
                    for hg2 in range(2):
                        yield
                        hg = g * 2 + hg2
                        S.op("pool", lambda e, hg=hg: e.tensor_copy(out=dtAb[:], in_=dts[:, 3, 4 * hg:4 * hg + 4].unsqueeze(2).to_broadcast([128, 4, 128])),
                             r=[dts], w=[dtAb])
                        for hh in range(4):
                            S.op("pe", lambda e, hh=hh: e.matmul(out=PS[:, hh * 128:(hh + 1) * 128], lhsT=dtAb[:, hh, :], rhs=tri[:], start=True, stop=False),
                                 r=[dtAb, tri], w=[PS])
                            S.op("pe", lambda e, hh=hh: e.matmul(out=PS[:, hh * 128:(hh + 1) * 128], lhsT=ident[:], rhs=mnegb[:], start=False, stop=True),
                                 r=[ident, mnegb], w=[PS])
                        for hh in range(4):
                            h = hg * 4 + hh
                            S.op("act", lambda e, h=h, hh=hh, hg2=hg2: e.activation(out=LT[:, hg2 * 4 + hh, :], in_=PS[:, hh * 128:(hh + 1) * 128], func=AF.Exp,
                                                                           bias=dts[:, 5, h:h + 1]), r=[PS, dts], w=[LT])
                    pd = nextpd()
                    S.op("pe", lambda e, g=g, pd=pd: e.matmul(out=pd[:, 0:128], lhsT=bcTb[:, g, :], rhs=bcTb[:, 2 + g, :], start=True, stop=True), r=[bcTb], w=[pd])
                    S.op("dve", lambda e, g=g, pd=pd: e.tensor_tensor(out=MT[:, 8 * g:8 * g + 8, :], in0=LT[:],
                                                                      in1=pd[:, 0:128].unsqueeze(1).to_broadcast([128, 8, 128]), op=ALU.mult), r=[pd, LT], w=[MT])
                yield
                for i in range(8):
                    S.op("pe", lambda e, i=i: e.transpose(out=PO[:, 1024 + i * 128:1024 + (i + 1) * 128], in_=xcT[:, i, :], identity=identf[:]),
                         r=[xcT, identf], w=[PO.b[2 + i // 4]])
                S.op("act", lambda e: e.copy(out=xcs[:], in_=PO[:, 1024:2048]), r=[PO.b[2], PO.b[3]], w=[xcs])
                x3 = xcs[:].rearrange("p (h d) -> p h d", h=16)
                S.op("dve", lambda e: e.tensor_tensor(out=xdt[:].rearrange("p (h d) -> p h d", h=16), in0=x3,
                                                      in1=dts[:, 2, :].unsqueeze(2).to_broadcast([128, 16, 64]), op=ALU.mult), r=[xcs, dts], w=[xdt])
                S.op("pool", lambda e: e.tensor_tensor(out=xw[:].rearrange("p (h d) -> p h d", h=16), in0=x3,
                                                       in1=dts[:, 6, :].unsqueeze(2).to_broadcast([128, 16, 64]), op=ALU.mult), r=[xcs, dts], w=[xw])
                yield
                for g in range(2):
                    S.op("pe", lambda e, g=g: e.transpose(out=PTb[:, g, :], in_=bcTb[:, g, :], identity=ident[:]), r=[bcTb, ident], w=[PTb])
                S.op("act", lambda e: e.copy(out=Btm[:].rearrange("p (g n) -> p g n", g=2), in_=PTb[:, 0:2, :]), r=[PTb], w=[Btm])
                yield
                for h in range(16):
                    S.op("pe", lambda e, h=h: e.matmul(out=PO[:, h * 64:(h + 1) * 64], lhsT=MT[:, h, :], rhs=xdt[:, h * 64:(h + 1) * 64],
                                                       start=True, stop=True), r=[MT, xdt], w=[PO.b[h // 8]])
                for h in range(16):
                    S.op("pe", lambda e, h=h: e.matmul(out=PO[:, 1024 + h * 64:1024 + (h + 1) * 64], lhsT=bcTb[:, 2 + h // 8, :],
                                                       rhs=hSb[l][:, h * 64:(h + 1) * 64], start=True, stop=True), r=[bcTb, hSb[l]], w=[PO.b[2 + h // 8]])
                S.op("dve", lambda e: e.tensor_tensor(out=tmp[:].rearrange("p (h d) -> p h d", h=16), in0=PO[:, 1024:2048].rearrange("p (h d) -> p h d", h=16),
                                                      in1=dts[:, 1, :].unsqueeze(2).to_broadcast([128, 16, 64]), op=ALU.mult), r=[PO.b[2], PO.b[3], dts], w=[tmp])
                S.op("dve", lambda e: e.tensor_tensor(out=tmp[:], in0=tmp[:], in1=PO[:, 0:1024], op=ALU.add), r=[tmp, PO.b[0], PO.b[1]], w=[tmp])
                S.op("pool", lambda e: e.tensor_tensor(out=tmp2[:].rearrange("p (h d) -> p h d", h=16), in0=x3,
                                                       in1=Dsk[:, l, :].unsqueeze(2).to_broadcast([128, 16, 64]), op=ALU.mult), r=[xcs, Dsk], w=[tmp2])
                S.op("dve", lambda e: e.tensor_tensor(out=tmp[:], in0=tmp[:], in1=tmp2[:], op=ALU.add), r=[tmp, tmp2], w=[tmp])
                S.op("dve", lambda e: e.tensor_tensor(out=tmp[:], in0=tmp[:], in1=czs[:], op=ALU.mult), r=[tmp, czs], w=[tmp])
                yield
                for g in range(2):
                    pd = nextpd()
                    S.op("pe", lambda e, g=g, pd=pd: e.matmul(out=pd[:], lhsT=Btm[:, g * 128:(g + 1) * 128], rhs=xw[:, g * 512:(g + 1) * 512], start=True, stop=True),
                         r=[Btm, xw], w=[pd])
                    S.op("pool", lambda e, g=g: e.tensor_tensor(out=hS[l][:, g * 512:(g + 1) * 512].rearrange("p (h d) -> p h d", h=8),
                                                                in0=hS[l][:, g * 512:(g + 1) * 512].rearrange("p (h d) -> p h d", h=8),
                                                                in1=dts[:, 7, 8 * g:8 * g + 8].unsqueeze(2).to_broadcast([128, 8, 64]), op=ALU.mult),
                         r=[hS[l], dts, hSb[l]], w=[hS[l]])
                    S.op("dve", lambda e, g=g, pd=pd: e.tensor_tensor(out=hS[l][:, g * 512:(g + 1) * 512], in0=hS[l][:, g * 512:(g + 1) * 512], in1=pd[:], op=ALU.add),
                         r=[hS[l], pd], w=[hS[l]])
                S.op("act", lambda e: e.copy(out=hSb[l][:], in_=hS[l][:]), r=[hS[l]], w=[hSb[l]])
                yield
                for g in range(2):
                    S.op("act", lambda e, g=g: e.activation(out=sq[:, g * 512:(g + 1) * 512], in_=tmp[:, g * 512:(g + 1) * 512], func=AF.Square,
                                                            accum_out=st8[:, 16 + g:17 + g]), r=[tmp], w=[sq, st8])
                S.op("act", lambda e: e.activation(out=st8[:, 16:18], in_=st8[:, 16:18], func=AF.Ln, scale=1.0 / 512, bias=EPS), r=[st8], w=[st8])
                S.op("act", lambda e: e.activation(out=st8[:, 16:18], in_=st8[:, 16:18], func=AF.Exp, scale=-0.5), r=[st8], w=[st8])
                S.op("dve", lambda e: e.tensor_tensor(out=tmp[:].rearrange("p (g d) -> p g d", g=2), in0=tmp[:].rearrange("p (g d) -> p g d", g=2),
                                                      in1=st8[:, 16:18].unsqueeze(2).to_broadcast([128, 2, 512]), op=ALU.mult), r=[tmp, st8], w=[tmp])
                S.op("dve", lambda e: e.tensor_tensor(out=tmp[:], in0=tmp[:], in1=snwb[:, l, :], op=ALU.mult), r=[tmp, snwb], w=[tmp])
                yield "FINAL"
                S.op("dve", lambda e: e.tensor_tensor(out=mix[:], in0=tmp[:], in1=gates[:, 2, :], op=ALU.mult), r=[tmp, gates], w=[mix])


            def drain(g_):
                for _ in g_:
                    pass

            def chain(*gs):
                for g_ in gs:
                    yield from g_

            drain(gPC())
            gp = chain(gPA(), gPB())
            for tok in gMC():
                if tok == "FINAL":
                    drain(gp)
                else:
                    next(gp, None)
            drain(gp)
            ga_, gb_ = gMA(), gMB()
            alive = [ga_, gb_]
            while alive:
                for g_ in list(alive):
                    try:
                        next(g_)
                    except StopIteration:
                        alive.remove(g_)
            dense_tail(l, g1b)


        def dense_tail(l, g1b, rows=128):
            S.op("act", lambda e: e.copy(out=xn[0:rows, :], in_=mix[0:rows, :]), r=[mix], w=[xn])
            for k in range(8):
                S.op("pe", lambda e, k=k: e.transpose(out=PTb[:, k, 0:rows], in_=xn[0:rows, k * 128:(k + 1) * 128], identity=ident[0:rows, 0:rows]),
                     r=[xn, ident], w=[PTb])
            S.op("act", lambda e: e.copy(out=hT[:, :, 0:rows], in_=PTb[:, :, 0:rows]), r=[PTb], w=[hT])
            for j in range(2):
                wb_ = wload(tl(woutT, l, j), 512, dram_b["woutb%d" % l])
                pd = nextpd()
                for k in range(8):
                    S.op("pe", lambda e, k=k, pd=pd, wb_=wb_: e.matmul(out=pd[0:rows, :], lhsT=hT[:, k, 0:rows], rhs=wb_[:, k, :], start=(k == 0), stop=(k == 7)),
                         r=[hT, wb_], w=[pd])
                S.op("dve", lambda e, j=j, pd=pd: e.tensor_tensor(out=tmp[0:rows, j * 512:(j + 1) * 512], in0=pd[0:rows, :], in1=g1b[0:rows, j * 512:(j + 1) * 512], op=ALU.mult),
                     r=[pd, g1b], w=[tmp])
            S.op("dve", lambda e: e.tensor_tensor(out=xres[0:rows, :], in0=xres[0:rows, :], in1=tmp[0:rows, :], op=ALU.add), r=[tmp, xres], w=[xres])

        def mlp(l, g2b, xrs, hTs, uTs, rows=128):
            nch = len(xrs)
            cnt_ = 0
            for j in range(8):
                wb_ = wload(tl(wupT, l, j), 512, dram_b["wupb%d" % l])
                for c in range(nch):
                    hT_, uT_ = hTs[c], uTs[c]
                    pd = nextpd()
                    for k in range(8):
                        S.op("pe", lambda e, k=k, pd=pd, wb_=wb_, hT_=hT_: e.matmul(out=pd[0:rows, :], lhsT=hT_[:, k, 0:rows], rhs=wb_[:, k, :],
                                                                                  start=(k == 0), stop=(k == 7)), r=[hT_, wb_], w=[pd])
                    hsel = cnt_ % 2
                    cnt_ += 1
                    S.op("act", lambda e, pd=pd, hsel=hsel: e.activation(out=sq[0:rows, hsel * 512:(hsel + 1) * 512], in_=pd[0:rows, :], func=AF.Relu),
                         r=[pd], w=[sqh[hsel]])
                    S.op("dve", lambda e, hsel=hsel: e.tensor_tensor(out=xn[0:rows, hsel * 512:(hsel + 1) * 512], in0=sq[0:rows, hsel * 512:(hsel + 1) * 512],
                                                                      in1=sq[0:rows, hsel * 512:(hsel + 1) * 512], op=ALU.mult), r=[sqh[hsel]], w=[xnh[hsel]])
                    for i in range(4):
                        S.op("pe", lambda e, i=i, hsel=hsel: e.transpose(out=PTb[:, hsel * 4 + i, 0:rows], in_=xn[0:rows, hsel * 512 + i * 128:hsel * 512 + (i + 1) * 128],
                                                                        identity=ident[0:rows, 0:rows]), r=[xnh[hsel], ident], w=[PTb.b[hsel]])
                    S.op("act", lambda e, j=j, hsel=hsel, uT_=uT_: e.copy(out=uT_[:, j * 4:j * 4 + 4, 0:rows], in_=PTb[:, hsel * 4:hsel * 4 + 4, 0:rows]),
                         r=[PTb.b[hsel]], w=[uT_.b[j]])
            for ct in range(2):
                for sl in range(4):
                    wb_ = wload(tl(wdownT, l, ct * 4 + sl), 512, dram_b["wdownb%d" % l])
                    for c in range(nch):
                        pd = PD[c]
                        uT_ = uTs[c]
                        for k in range(8):
                            S.op("pe", lambda e, k=k, pd=pd, wb_=wb_, sl=sl, uT_=uT_: e.matmul(out=pd[0:rows, :], lhsT=uT_[:, sl * 8 + k, 0:rows], rhs=wb_[:, k, :],
                                                                                             start=(sl == 0 and k == 0), stop=(sl == 3 and k == 7)), r=[uT_, wb_], w=[pd])
                for c in range(nch):
                    pd, xr = PD[c], xrs[c]
                    S.op("dve", lambda e, ct=ct, pd=pd: e.tensor_tensor(out=tmp[0:rows, ct * 512:(ct + 1) * 512], in0=pd[0:rows, :], in1=g2b[0:rows, ct * 512:(ct + 1) * 512], op=ALU.mult),
                         r=[pd, g2b], w=[tmp])
                    S.op("dve", lambda e, ct=ct, xr=xr: e.tensor_tensor(out=xr[0:rows, ct * 512:(ct + 1) * 512], in0=xr[0:rows, ct * 512:(ct + 1) * 512],
                                                                         in1=tmp[0:rows, ct * 512:(ct + 1) * 512], op=ALU.add), r=[tmp, xr], w=[xr])

        def final_out(dst_ap, rows=128):
            c = rms_stats(xres, rows, 0)
            S.op("dve", lambda e: e.scalar_tensor_tensor(out=tmp2[0:rows, :], in0=xres[0:rows, :], scalar=st8[0:rows, c:c + 1], in1=fnwb[0:rows, :],
                                                         op0=ALU.mult, op1=ALU.mult), r=[xres, st8, fnwb], w=[tmp2])
            S.dma("pool", dst_ap, tmp2[0:rows, :], r=[tmp2], w=[OUT])

        for b in range(RUN_B):
            for l in range(DEPTH):
                S.op("pool", lambda e, l=l: e.memset(Sret[l][:], 0.0), w=[Sret[l]])
                S.op("pool", lambda e, l=l: e.memset(Sbf[l][:], 0.0), w=[Sbf[l]])
                S.op("pool", lambda e, l=l: e.memset(hS[l][:], 0.0), w=[hS[l]])
                S.op("pool", lambda e, l=l: e.memset(hSb[l][:], 0.0), w=[hSb[l]])
                S.op("pool", lambda e, l=l: e.memset(chist[l][:], 0.0), w=[chist[l]])
            for tp in range(0, RUN_T, 2):
                ts_ = [t for t in (tp, tp + 1) if t < RUN_T]
                for c, t in enumerate(ts_):
                    S.dma("sp", xresL[c][:], xp[b, t * 128:(t + 1) * 128, :], w=[xresL[c]])
                for l in range(DEPTH):
                    for c, t in enumerate(ts_):
                        xres = xresL[c]
                        hT = hT_main
                        layer_chunk(l, b, t, l == DEPTH - 1)
                        if not cast_done[1]:
                            cast_layer(1)
                            cast_done[1] = True
                        hT = hTm[c]
                        norm_to_hT(A2L[l], modcL[l], 2)
                        hT = hT_main
                    mlp(l, g2bL[l], xresL[:len(ts_)], hTm[:len(ts_)], uTL[:len(ts_)])
                for c, t in enumerate(ts_):
                    xres = xresL[c]
                    final_out(y_p[b, t * 128:(t + 1) * 128, :])
            xres = xresL[0]
            for l in range(DEPTH):
                S.dma("pool", bass.AP(ret_p.tensor, (l * NB + b) * 8 * 64 * 128, [[128, 128], [16384, 4], [1, 128]]), Sret[l][:], r=[Sret[l]], w=[OUT])
                for half in range(2):
                    for i in range(4):
                        S.op("pe", lambda e, i=i, half=half, l=l: e.transpose(out=PO[:, i * 128:(i + 1) * 128], in_=hS[l][:, (half * 4 + i) * 128:(half * 4 + i + 1) * 128],
                                                                             identity=identf[:]), r=[hS[l], identf], w=[PO.b[0]])
                    S.op("act", lambda e, half=half: e.copy(out=tmp[:, half * 512:(half + 1) * 512], in_=PO[:, 0:512]), r=[PO.b[0]], w=[tmp])
                S.dma("pool", bass.AP(ssm_p.tensor, (l * NB + b) * 16 * 64 * 128, [[128, 128], [16384, 8], [1, 128]]), tmp[:], r=[tmp], w=[OUT])


        if not cast_done[1]:
            cast_layer(1)
            cast_done[1] = True
        stP.close()
        S.barrier()
        big = sb("big", [128, 8192]); prodb = sb("prodb", [128, 8192]); Vp = sb("Vp", [128, 8192])
        pa = sb("pa", [128, 768]); pb_ = sb("pb_", [128, 512]); pc = sb("pc", [128, 512]); smallp = sb("smallp", [128, 64])
        Z = dram_b["zsd"]; OS = dram_b["osd"]
        segs = {"aq": 1024, "ak": 256, "av": 256, "bq": 512, "bk": 512, "bv": 1024, "bg": 1024, "cz": 1024, "xbc": 1536, "dt": 16,
                "ga": 1024, "gb": 1024, "gc": 1024}
        zd = {k: dt_int("z_" + k, [NS, n]) for k, n in segs.items()}
        zx = dt_int("z_x", [NS, 1024]); zB = dt_int("z_B", [NS, 256]); zC = dt_int("z_C", [NS, 256]); zBr = dt_int("z_Br", [NS, 2, 8, 128])
        zCr = dt_int("z_Cr", [NS, 2, 8, 128]); zsm = dt_int("z_sm", [NS, 16, 4])
        R16 = NS
        S.dma("sp", xres[0:R16, :], xs, w=[xres])
        rowb = lambda t, off, n, rows=R16: bass.AP(t.tensor, off, [[0, rows], [1, n]])

        def s_norm(l, nw, col_sc, col_sh):
            S.dma("pool", tmp2[0:R16, :], rowb(nw, l * D, D), w=[tmp2])
            S.dma("pool", tmp[0:R16, :], modd[l, NB:NB + NS, col_sc * D:(col_sc + 1) * D], r=[dram_b["modd"]], w=[tmp])
            S.op("dve", lambda e: e.scalar_tensor_tensor(out=tmp[0:R16, :], in0=tmp[0:R16, :], scalar=1.0, in1=tmp2[0:R16, :], op0=ALU.add, op1=ALU.mult),
                 r=[tmp, tmp2], w=[tmp])
            c = rms_stats(xres, R16, 0)
            S.op("dve", lambda e: e.scalar_tensor_tensor(out=sq[0:R16, :], in0=xres[0:R16, :], scalar=st8[0:R16, c:c + 1], in1=tmp[0:R16, :], op0=ALU.mult, op1=ALU.mult),
                 r=[xres, st8, tmp], w=[sq])
            S.dma("pool", tmp2[0:R16, :], modd[l, NB:NB + NS, col_sh * D:(col_sh + 1) * D], r=[dram_b["modd"]], w=[tmp2])
            S.op("dve", lambda e: e.tensor_tensor(out=xn[0:R16, :], in0=sq[0:R16, :], in1=tmp2[0:R16, :], op=ALU.add), r=[sq, tmp2], w=[xn])
            for k in range(8):
                S.op("pe", lambda e, k=k: e.transpose(out=PTb[:, k, 0:R16], in_=xn[0:R16, k * 128:(k + 1) * 128], identity=ident[0:R16, 0:R16]),
                     r=[xn, ident], w=[PTb])
            S.op("act", lambda e: e.copy(out=hT[:, :, 0:R16], in_=PTb[:, :, 0:R16]), r=[PTb], w=[hT])

        def pairs_out(src_ap, rows, slot, width, srcT):
            S.dma("pool", osd[slot].rearrange("s (h c) -> (s h) c", c=width), src_ap, r=[srcT], w=[OS])

        for l in range(DEPTH if RUN_SAMPLE else 0):
            s_norm(l, n1w, 1, 0)
            order = ["aq", "ak", "av", "bq", "bk", "bv", "bg", "cz", "xbc", "ga", "gb", "gc", "dt"]
            segoff = {}
            o_ = 0
            for k in order:
                segoff[k] = o_
                o_ += segs[k]
            ti = 0
            for c0 in range(0, DIN, 512):
                n = min(512, DIN - c0)
                wb_ = wload(tl(winT, l, c0 // 512, n), n, dram_b["winb%d" % l])
                pd = nextpd()
                for k in range(8):
                    S.op("pe", lambda e, k=k, pd=pd, wb_=wb_, n=n: e.matmul(out=pd[0:R16, 0:n], lhsT=hT[:, k, 0:R16], rhs=wb_[:, k, 0:n], start=(k == 0), stop=(k == 7)),
                         r=[hT, wb_], w=[pd])
                stage = pa if ti % 2 == 0 else pb_
                ti += 1
                if c0 < 1024:
                    S.op("act", lambda e, pd=pd, stage=stage: e.copy(out=stage[0:R16, 0:512].rearrange("p (g1 hq d) -> p hq g1 d", g1=2, hq=4),
                                                                 in_=pd[0:R16, 0:512].rearrange("p (hq g1 d) -> p hq g1 d", hq=4, g1=2)), r=[pd], w=[stage])
                else:
                    S.op("act", lambda e, pd=pd, stage=stage, n=n: e.copy(out=stage[0:R16, 0:n], in_=pd[0:R16, 0:n]), r=[pd], w=[stage])
                for k in order:
                    a0, a1 = max(c0, segoff[k]), min(c0 + n, segoff[k] + segs[k])
                    if a0 < a1:
                        S.dma("pool", zd[k][:, a0 - segoff[k]:a1 - segoff[k]], stage[0:R16, a0 - c0:a1 - c0], r=[stage], w=[Z])
            for gi, k in enumerate(("ga", "gb", "gc")):
                S.dma("pool", gates[0:R16, gi, :], zd[k], r=[Z], w=[gates])
            S.op("act", lambda e: e.activation(out=gates[0:R16, :, :], in_=gates[0:R16, :, :], func=AF.Sigmoid), r=[gates], w=[gates])

            S.dma("pool", wk_s[l, :, 0:127, :], cache_k[l, :, 1:128, :], w=[OUT])
            S.dma("pool", wk_s[l, :, 127, :], zd["ak"], r=[Z], w=[OUT])
            S.dma("pool", wv_s[l, :, 0:127, :], cache_v[l, :, 1:128, :], w=[OUT])
            S.dma("pool", wv_s[l, :, 127, :], zd["av"], r=[Z], w=[OUT])
            K3 = big[0:64, :].rearrange("p (k d) -> p k d", d=64)
            V3 = Vp[0:64, :].rearrange("p (k d) -> p k d", d=64)
            P3 = prodb[0:64, :].rearrange("p (k d) -> p k d", d=64)
            for s_ in range(NS):
                S.dma("sp", K3[4 * s_:4 * s_ + 4, 0:127, :], cache_k[l, s_, 1:128, :].rearrange("k (g d) -> g k d", g=4), w=[big])
                S.dma("sp", V3[4 * s_:4 * s_ + 4, 0:127, :], cache_v[l, s_, 1:128, :].rearrange("k (g d) -> g k d", g=4), w=[Vp])
            S.dma("pool", K3[:, 127, :], zd["ak"].rearrange("s (g d) -> (s g) d", g=4), r=[Z], w=[big])
            S.dma("pool", V3[:, 127, :], zd["av"].rearrange("s (g d) -> (s g) d", g=4), r=[Z], w=[Vp])
            S.dma("pool", pa[0:64, 0:256], zd["aq"].rearrange("s (g c) -> (s g) c", g=4), r=[Z], w=[pa])
            for s_ in range(NS):
                S.dma("pool", pb_[4 * s_:4 * s_ + 4, 0:512], bass.AP(vecx.tensor, 128, [[4 * 384, 4], [384, 4], [1, 128]]), r=[dram_b["vecx"]], w=[pb_])
                S.dma("pool", smallp[4 * s_:4 * s_ + 4, 0:4], sinks[l].rearrange("(g hq) -> g hq", g=4), w=[smallp])
            S.op("act", lambda e: e.activation(out=smallp[0:64, 0:4], in_=smallp[0:64, 0:4], func=AF.Exp), r=[smallp], w=[smallp])
            for hq in range(4):
                S.op("dve", lambda e, hq=hq: e.tensor_tensor(out=P3, in0=K3, in1=pa[0:64, hq * 64:(hq + 1) * 64].unsqueeze(1).to_broadcast([64, 128, 64]), op=ALU.mult),
                     r=[big, pa], w=[prodb])
                S.op("dve", lambda e, hq=hq: e.tensor_reduce(out=pc[0:64, hq * 128:(hq + 1) * 128], in_=P3, axis=AX.X, op=ALU.add), r=[prodb], w=[pc])
            S.op("dve", lambda e: e.scalar_tensor_tensor(out=pc[0:64, 0:512], in0=pc[0:64, 0:512], scalar=0.125, in1=pb_[0:64, 0:512], op0=ALU.mult, op1=ALU.add),
                 r=[pc, pb_], w=[pc])
            S.op("act", lambda e: e.activation(out=pc[0:64, 0:512], in_=pc[0:64, 0:512], func=AF.Exp), r=[pc], w=[pc])
            S.op("dve", lambda e: e.tensor_reduce(out=smallp[0:64, 4:8], in_=pc[0:64, 0:512].rearrange("p (h k) -> p h k", h=4), axis=AX.X, op=ALU.add), r=[pc], w=[smallp])
            S.op("dve", lambda e: e.tensor_tensor(out=smallp[0:64, 4:8], in0=smallp[0:64, 4:8], in1=smallp[0:64, 0:4], op=ALU.add), r=[smallp], w=[smallp])
            S.op("dve", lambda e: e.reciprocal(out=smallp[0:64, 4:8], in_=smallp[0:64, 4:8]), r=[smallp], w=[smallp])
            PV3 = prodb[0:64, :].rearrange("p (d k) -> p d k", k=128)
            for hq in range(4):
                S.op("dve", lambda e, hq=hq: e.tensor_tensor(out=PV3, in0=V3.rearrange("p k d -> p d k"),
                                                             in1=pc[0:64, hq * 128:(hq + 1) * 128].unsqueeze(1).to_broadcast([64, 64, 128]), op=ALU.mult), r=[Vp, pc], w=[prodb])
                S.op("dve", lambda e, hq=hq: e.tensor_reduce(out=pa[0:64, 256 + hq * 64:256 + (hq + 1) * 64], in_=PV3, axis=AX.X, op=ALU.add), r=[prodb], w=[pa])
            S.op("dve", lambda e: e.tensor_tensor(out=pa[0:64, 512:768].rearrange("p (h d) -> p h d", h=4), in0=pa[0:64, 256:512].rearrange("p (h d) -> p h d", h=4),
                                                  in1=smallp[0:64, 4:8].unsqueeze(2).to_broadcast([64, 4, 64]), op=ALU.mult), r=[pa, smallp], w=[pa])
            S.dma("pool", osd[0].rearrange("s (g c) -> (s g) c", g=4), pa[0:64, 512:768], r=[pa], w=[OS])
            S.dma("pool", tmp[0:R16, :], osd[0], r=[OS], w=[tmp])
            S.op("dve", lambda e: e.tensor_tensor(out=mix[0:R16, :], in0=tmp[0:R16, :], in1=gates[0:R16, 0, :], op=ALU.mult), r=[tmp, gates], w=[mix])

            S3 = big[:, :].rearrange("p (d e) -> p d e", e=128)
            PR3 = prodb[:, :].rearrange("p (d e) -> p d e", e=128)
            S.dma("sp", S3, st_ret[l].rearrange("s h d e -> (s h) d e"), w=[big])
            S.dma("pool", pa[:, 0:64], zd["bq"].rearrange("s (h d) -> (s h) d", h=8), r=[Z], w=[pa])
            S.dma("pool", pa[:, 64:128], zd["bk"].rearrange("s (h d) -> (s h) d", h=8), r=[Z], w=[pa])
            S.dma("pool", pb_[:, 0:128], zd["bv"].rearrange("s (h e) -> (s h) e", h=8), r=[Z], w=[pb_])
            S.dma("pool", smallp[:, 8:40], hc["coss"], w=[smallp])
            S.dma("pool", smallp[:, 40:41], hc["g1p"], w=[smallp])
            S.dma("pool", pc[:, 0:32], hc["sins"], w=[pc])
            for j in range(2):
                v3 = pa[:, j * 64:(j + 1) * 64].rearrange("p (f two) -> p f two", two=2)
                o3_ = pa[:, 128 + j * 64:128 + (j + 1) * 64].rearrange("p (f two) -> p f two", two=2)
                x1, x2 = v3[:, :, 0], v3[:, :, 1]
                cs_, sn_ = smallp[:, 8:40], pc[:, 0:32]
                S.op("dve", lambda e, x1=x1, cs_=cs_: e.tensor_tensor(out=pc[:, 32:64], in0=x1, in1=cs_, op=ALU.mult), r=[pa, smallp], w=[pc])
                S.op("dve", lambda e, x2=x2, sn_=sn_: e.tensor_tensor(out=pc[:, 64:96], in0=x2, in1=sn_, op=ALU.mult), r=[pa, pc], w=[pc])
                S.op("dve", lambda e, o3_=o3_: e.tensor_tensor(out=o3_[:, :, 0], in0=pc[:, 32:64], in1=pc[:, 64:96], op=ALU.subtract), r=[pc], w=[pa])
                S.op("dve", lambda e, x1=x1, sn_=sn_: e.tensor_tensor(out=pc[:, 32:64], in0=x1, in1=sn_, op=ALU.mult), r=[pa, pc], w=[pc])
                S.op("dve", lambda e, x2=x2, cs_=cs_: e.tensor_tensor(out=pc[:, 64:96], in0=x2, in1=cs_, op=ALU.mult), r=[pa, smallp], w=[pc])
                S.op("dve", lambda e, o3_=o3_: e.tensor_tensor(out=o3_[:, :, 1], in0=pc[:, 32:64], in1=pc[:, 64:96], op=ALU.add), r=[pc], w=[pa])
            S.op("dve", lambda e: e.tensor_scalar(out=pa[:, 192:256], in0=pa[:, 192:256], scalar1=0.125, scalar2=None, op0=ALU.mult), r=[pa], w=[pa])
            S.op("dve", lambda e: e.tensor_tensor(out=PR3, in0=pa[:, 192:256].unsqueeze(2).to_broadcast([128, 64, 128]),
                                                  in1=pb_[:, 0:128].unsqueeze(1).to_broadcast([128, 64, 128]), op=ALU.mult), r=[pa, pb_], w=[prodb])
            S.op("dve", lambda e: e.scalar_tensor_tensor(out=big[:, :], in0=big[:, :], scalar=smallp[:, 40:41], in1=prodb[:, :], op0=ALU.mult, op1=ALU.add),
                 r=[big, smallp, prodb], w=[big])
            S.dma("sp", ret_s[l].rearrange("s h d e -> (s h) d e"), S3, r=[big], w=[OUT])
            S.op("dve", lambda e: e.tensor_tensor(out=PR3, in0=S3, in1=pa[:, 128:192].unsqueeze(2).to_broadcast([128, 64, 128]), op=ALU.mult), r=[big, pa], w=[prodb])
            S.op("dve", lambda e: e.tensor_reduce(out=pb_[:, 128:256], in_=PR3.rearrange("p d e -> p e d"), axis=AX.X, op=ALU.add), r=[prodb], w=[pb_])
            S.dma("pool", osd[1].rearrange("s (h e) -> (s h) e", h=8), pb_[:, 128:256], r=[pb_], w=[OS])
            S.dma("pool", tmp[0:R16, :], osd[1], r=[OS], w=[tmp])
            S.dma("pool", bgs[0:R16, :], zd["bg"], r=[Z], w=[bgs])
            S.op("act", lambda e: e.activation(out=bgs[0:R16, :], in_=bgs[0:R16, :], func=AF.Silu), r=[bgs], w=[bgs])
            S.op("act", lambda e: e.activation(out=sq[0:R16, :], in_=tmp[0:R16, :], func=AF.Square), r=[tmp], w=[sq])
            S.op("dve", lambda e: e.tensor_reduce(out=st8[0:R16, 8:16], in_=sq[0:R16, :].rearrange("p (h e) -> p h e", h=8), axis=AX.X, op=ALU.add), r=[sq], w=[st8])
            S.op("act", lambda e: e.activation(out=st8[0:R16, 8:16], in_=st8[0:R16, 8:16], func=AF.Ln, scale=1.0 / 128, bias=EPS), r=[st8], w=[st8])
            S.op("act", lambda e: e.activation(out=st8[0:R16, 8:16], in_=st8[0:R16, 8:16], func=AF.Exp, scale=-0.5), r=[st8], w=[st8])
            S.op("dve", lambda e: e.tensor_tensor(out=tmp[0:R16, :].rearrange("p (h e) -> p h e", h=8), in0=tmp[0:R16, :].rearrange("p (h e) -> p h e", h=8),
                                                  in1=st8[0:R16, 8:16].unsqueeze(2).to_broadcast([R16, 8, 128]), op=ALU.mult), r=[tmp, st8], w=[tmp])
            S.op("dve", lambda e: e.tensor_tensor(out=tmp[0:R16, :], in0=tmp[0:R16, :], in1=bgs[0:R16, :], op=ALU.mult), r=[tmp, bgs], w=[tmp])
            S.op("dve", lambda e: e.tensor_tensor(out=tmp[0:R16, :], in0=tmp[0:R16, :], in1=gates[0:R16, 1, :], op=ALU.mult), r=[tmp, gates], w=[tmp])
            S.op("dve", lambda e: e.tensor_tensor(out=mix[0:R16, :], in0=mix[0:R16, :], in1=tmp[0:R16, :], op=ALU.add), r=[tmp, mix], w=[mix])

            hist = big[0:R16, 0:4608].rearrange("p (i c) -> p i c", i=3)
            cwb = Vp[0:R16, 0:6144].rearrange("p (i c) -> p i c", i=4)
            cbb, cx, acc, tb = prodb[0:R16, 0:1536], prodb[0:R16, 1536:3072], prodb[0:R16, 3072:4608], prodb[0:R16, 4608:6144]
            S.dma("sp", hist, st_conv[l], w=[big])
            S.dma("sp", cwb, bass.AP(conv_w.tensor, l * 4 * 1536, [[0, R16], [1536, 4], [1, 1536]]), w=[Vp])
            S.dma("sp", cbb, rowb(conv_b, l * 1536, 1536), w=[prodb])
            S.dma("sp", cx, zd["xbc"], r=[Z], w=[prodb])
            S.dma("pool", conv_s[l, :, 0:2, :], st_conv[l, :, 1:3, :], w=[OUT])
            S.dma("pool", conv_s[l, :, 2, :], zd["xbc"], r=[Z], w=[OUT])
            S.op("dve", lambda e: e.tensor_tensor(out=acc, in0=cx, in1=cwb[:, 3, :], op=ALU.mult), r=[prodb, big, Vp], w=[prodb])
            S.op("dve", lambda e: e.tensor_tensor(out=acc, in0=acc, in1=cbb, op=ALU.add), r=[prodb], w=[prodb])
            for i in range(3):
                S.op("dve", lambda e, i=i: e.tensor_tensor(out=tb, in0=hist[:, i, :], in1=cwb[:, i, :], op=ALU.mult), r=[big, prodb, Vp], w=[prodb])
                S.op("dve", lambda e: e.tensor_tensor(out=acc, in0=acc, in1=tb, op=ALU.add), r=[prodb], w=[prodb])
            S.op("act", lambda e: e.activation(out=acc, in_=acc, func=AF.Silu), r=[prodb], w=[prodb])
            S.dma("pool", zx, prodb[0:R16, 3072:3072 + 1024], r=[prodb], w=[Z])
            S.dma("pool", zB, prodb[0:R16, 3072 + 1024:3072 + 1280], r=[prodb], w=[Z])
            S.dma("pool", zC, prodb[0:R16, 3072 + 1280:3072 + 1536], r=[prodb], w=[Z])
            S.dma("pool", zBr.rearrange("s g r n -> (s g) r n"), bass.AP(zB.tensor, 0, [[128, 2 * NS], [0, 8], [1, 128]]), r=[Z], w=[Z])
            S.dma("pool", zCr.rearrange("s g r n -> (s g) r n"), bass.AP(zC.tensor, 0, [[128, 2 * NS], [0, 8], [1, 128]]), r=[Z], w=[Z])
            S.dma("pool", st8[0:R16, 0:16], zd["dt"], r=[Z], w=[st8])
            S.op("dve", lambda e: e.tensor_tensor(out=st8[0:R16, 0:16], in0=st8[0:R16, 0:16], in1=dtb[0:R16, l, :], op=ALU.add), r=[st8, dtb], w=[st8])
            S.op("act", lambda e: e.activation(out=st8[0:R16, 0:16], in_=st8[0:R16, 0:16], func=AF.Exp), r=[st8], w=[st8])
            sm4 = smallp[0:R16, 0:64].rearrange("p (h c) -> p h c", c=4)
            S.op("act", lambda e: e.activation(out=sm4[:, :, 0], in_=st8[0:R16, 0:16], func=AF.Ln, bias=1.0), r=[st8], w=[smallp])
            S.op("dve", lambda e: e.tensor_tensor(out=sm4[:, :, 1], in0=sm4[:, :, 0], in1=Arow[0:R16, l, :], op=ALU.mult), r=[smallp, Arow], w=[smallp])
            S.op("act", lambda e: e.activation(out=sm4[:, :, 1], in_=sm4[:, :, 1], func=AF.Exp), r=[smallp], w=[smallp])
            S.op("dve", lambda e: e.tensor_copy(out=sm4[:, :, 2], in_=Dsk[0:R16, l, :]), r=[Dsk, smallp], w=[smallp])
            S.op("dve", lambda e: e.tensor_copy(out=sm4[:, :, 3], in_=Dsk[0:R16, l, :]), r=[Dsk, smallp], w=[smallp])
            S.dma("pool", zsm.rearrange("s h c -> s (h c)"), smallp[0:R16, 0:64], r=[smallp], w=[Z])
            for half in range(2):
                s0 = half * 8
                H3 = big[:, :].rearrange("p (q n) -> p q n", n=128)
                S.dma("sp", H3, st_ssm[l, s0:s0 + 8].rearrange("s h q n -> (s h) q n"), w=[big])
                S.dma("pool", pa[:, 0:64], zx[s0:s0 + 8, :].rearrange("s (h q) -> (s h) q", h=16), r=[Z], w=[pa])
                S.dma("pool", pb_[:, 0:128], zBr[s0:s0 + 8].rearrange("s g r n -> (s g r) n"), r=[Z], w=[pb_])
                S.dma("pool", pb_[:, 128:256], zCr[s0:s0 + 8].rearrange("s g r n -> (s g r) n"), r=[Z], w=[pb_])
                S.dma("pool", pc[:, 0:4], zsm[s0:s0 + 8].rearrange("s h c -> (s h) c"), r=[Z], w=[pc])
                S.op("dve", lambda e: e.tensor_scalar(out=pa[:, 64:128], in0=pa[:, 0:64], scalar1=pc[:, 0:1], scalar2=None, op0=ALU.mult), r=[pa, pc], w=[pa])
                S.op("dve", lambda e: e.tensor_tensor(out=PR3.rearrange("p d e -> p d e"), in0=pa[:, 64:128].unsqueeze(2).to_broadcast([128, 64, 128]),
                                                      in1=pb_[:, 0:128].unsqueeze(1).to_broadcast([128, 64, 128]), op=ALU.mult), r=[pa, pb_], w=[prodb])
                S.op("dve", lambda e: e.scalar_tensor_tensor(out=big[:, :], in0=big[:, :], scalar=pc[:, 1:2], in1=prodb[:, :], op0=ALU.mult, op1=ALU.add),
                     r=[big, pc, prodb], w=[big])
                S.dma("sp", ssm_s[l, s0:s0 + 8].rearrange("s h q n -> (s h) q n"), H3, r=[big], w=[OUT])
                S.op("dve", lambda e: e.tensor_tensor(out=PR3, in0=H3, in1=pb_[:, 128:256].unsqueeze(1).to_broadcast([128, 64, 128]), op=ALU.mult), r=[big, pb_], w=[prodb])
                S.op("dve", lambda e: e.tensor_reduce(out=pa[:, 128:192], in_=PR3, axis=AX.X, op=ALU.add), r=[prodb], w=[pa])
                S.op("dve", lambda e: e.scalar_tensor_tensor(out=pa[:, 128:192], in0=pa[:, 0:64], scalar=pc[:, 2:3], in1=pa[:, 128:192], op0=ALU.mult, op1=ALU.add),
                     r=[pa, pc], w=[pa])
                S.dma("pool", osd[2, s0:s0 + 8].rearrange("s (h q) -> (s h) q", h=16), pa[:, 128:192], r=[pa], w=[OS])
            S.dma("pool", tmp[0:R16, :], osd[2], r=[OS], w=[tmp])
            S.dma("pool", bgs[0:R16, :], zd["cz"], r=[Z], w=[bgs])
            S.op("act", lambda e: e.activation(out=bgs[0:R16, :], in_=bgs[0:R16, :], func=AF.Silu), r=[bgs], w=[bgs])
            S.op("dve", lambda e: e.tensor_tensor(out=tmp[0:R16, :], in0=tmp[0:R16, :], in1=bgs[0:R16, :], op=ALU.mult), r=[tmp, bgs], w=[tmp])
            for g in range(2):
                S.op("act", lambda e, g=g: e.activation(out=sq[0:R16, g * 512:(g + 1) * 512], in_=tmp[0:R16, g * 512:(g + 1) * 512], func=AF.Square,
                                                        accum_out=st8[0:R16, 16 + g:17 + g]), r=[tmp], w=[sq, st8])
            S.op("act", lambda e: e.activation(out=st8[0:R16, 16:18], in_=st8[0:R16, 16:18], func=AF.Ln, scale=1.0 / 512, bias=EPS), r=[st8], w=[st8])
            S.op("act", lambda e: e.activation(out=st8[0:R16, 16:18], in_=st8[0:R16, 16:18], func=AF.Exp, scale=-0.5), r=[st8], w=[st8])
            S.op("dve", lambda e: e.tensor_tensor(out=tmp[0:R16, :].rearrange("p (g d) -> p g d", g=2), in0=tmp[0:R16, :].rearrange("p (g d) -> p g d", g=2),
                                                  in1=st8[0:R16, 16:18].unsqueeze(2).to_broadcast([R16, 2, 512]), op=ALU.mult), r=[tmp, st8], w=[tmp])
            S.op("dve", lambda e: e.tensor_tensor(out=tmp[0:R16, :], in0=tmp[0:R16, :], in1=snwb[0:R16, l, :], op=ALU.mult), r=[tmp, snwb], w=[tmp])
            S.op("dve", lambda e: e.tensor_tensor(out=tmp[0:R16, :], in0=tmp[0:R16, :], in1=gates[0:R16, 2, :], op=ALU.mult), r=[tmp, gates], w=[tmp])
            S.op("dve", lambda e: e.tensor_tensor(out=mix[0:R16, :], in0=mix[0:R16, :], in1=tmp[0:R16, :], op=ALU.add), r=[tmp, mix], w=[mix])
            S.dma("pool", g1bL[l][0:R16, :], modd[l, NB:NB + NS, 2 * D:3 * D], r=[dram_b["modd"]], w=[g1bL[l]])
            S.dma("pool", g2bL[l][0:R16, :], modd[l, NB:NB + NS, 5 * D:6 * D], r=[dram_b["modd"]], w=[g2bL[l]])
            dense_tail(l, g1bL[l], rows=R16)
            s_norm(l, n2w, 4, 3)
            mlp(l, g2bL[l], [xres], [hT], [uT], rows=R16)
        if RUN_SAMPLE:
            final_out(y_s, rows=R16)

        S.finish("sp")
        build.counts = dict(S.cnt)
    return nc


_NC = None


def kernel(**inp):
    global _NC
    f = lambda a: np.ascontiguousarray(np.asarray(a, dtype=np.float32))
    hcst = host_consts()
    if _NC is None:
        _NC = build()
    in_maps = []
    for c in range(RUN_CORES):
        ps, ss = slice(c * NB, (c + 1) * NB), slice(c * NS, (c + 1) * NS)
        m = {
            "xp": f(inp["x_prompt"][ps]), "xs": f(inp["x_sample"][ss, 0]),
            "cc": f(np.concatenate([np.asarray(inp["c_prompt"])[ps], np.asarray(inp["c_sample"])[ss]], 0)),
            "cache_k": f(np.asarray(inp["cache_win_k"])[:, ss].reshape(DEPTH, NS, 128, 256)),
            "cache_v": f(np.asarray(inp["cache_win_v"])[:, ss].reshape(DEPTH, NS, 128, 256)),
            "st_ret": f(np.asarray(inp["state_ret"])[:, ss]), "st_ssm": f(np.asarray(inp["state_ssm"])[:, ss]),
            "st_conv": f(np.asarray(inp["state_conv"])[:, ss]),
            "rel_tab": f(inp["rel_bias_table"]), "sinks": f(inp["attn_sinks"]), "n1w": f(inp["norm1_w"]), "n2w": f(inp["norm2_w"]),
            "ada_w": f(inp["ada_w"]), "ada_b": f(inp["ada_b"]), "w_in": f(inp["w_in"]), "conv_w": f(inp["conv_w"]), "conv_b": f(inp["conv_b"]),
            "dt_bias": f(inp["dt_bias"]), "a_log": f(inp["A_log"]), "d_skip": f(inp["D_skip"]), "snw": f(inp["ssm_norm_w"]),
            "w_out": f(inp["w_out"]), "w_up": f(inp["w_up"]), "w_down": f(inp["w_down"]), "fnw": f(inp["final_norm_w"]),
        }
        for k, v in hcst.items():
            m["c_" + k] = v
        in_maps.append(m)
    res = run_bass_kernel_spmd(_NC, in_maps, core_ids=list(range(RUN_CORES)), **({'trace': True} if TRACE else {}))
    if TRACE:
        print('EXEC_NS', res.exec_time_ns, flush=True)
    R = res.results
    cat = lambda k, ax: np.concatenate([np.asarray(r[k]) for r in R], axis=ax)
    y_p = cat("y_p", 0)
    y_s = cat("y_s", 0).reshape(RUN_CORES * NS, 1, D)
    wk_p = cat("wk_p", 1).reshape(DEPTH, RUN_CORES * NB, 128, 4, 64)
    wv_p = cat("wv_p", 1).reshape(DEPTH, RUN_CORES * NB, 128, 4, 64)
    ret_p = cat("ret_p", 1); ssm_p = cat("ssm_p", 1); conv_p = cat("conv_p", 1)
    wk_s = cat("wk_s", 1).reshape(DEPTH, RUN_CORES * NS, 128, 4, 64)
    wv_s = cat("wv_s", 1).reshape(DEPTH, RUN_CORES * NS, 128, 4, 64)
    ret_s = cat("ret_s", 1); ssm_s = cat("ssm_s", 1); conv_s = cat("conv_s", 1)
    if DEBUG:
        kernel.dbg = np.asarray(R[0]["dbg"])
    return (y_p, y_s, wk_p, wv_p, ret_p, ssm_p, conv_p, wk_s, wv_s, ret_s, ssm_s, conv_s)
```

```python
import math
from contextlib import ExitStack
import numpy as np
import concourse.bass as bass
import concourse.mybir as mybir
from concourse.bass_utils import run_bass_kernel_spmd

F32 = mybir.dt.float32
BF16 = mybir.dt.bfloat16
AF = mybir.ActivationFunctionType
ALU = mybir.AluOpType
AX = mybir.AxisListType

NCORE = 8
D = 1024
SEQ = 2048
NB = 2
NS = 16
DEPTH = 2
PAST = 16384
DIN = 10256
DFF = 4096
EPS = 1e-6
NEG = -30000.0
RUN_B, RUN_T, RUN_SAMPLE, RUN_CORES = NB, 16, True, NCORE
RUN_STAGE = 9
DEBUG = False
T_LAST = 15
TRACE = False
DO_KV = True
DO_CONV = True
DBG_T = 1
RUN_SUB = 99
QORDER = [0, 4, 1, 5, 2, 6, 3, 7, 8, 12, 9, 13, 10, 14, 11, 15]
O_AQ, O_AK, O_AV, O_BQ, O_BK, O_BV, O_BG, O_CZ, O_XBC, O_DT, O_G = 0, 1024, 1280, 1536, 2048, 2560, 3584, 4608, 5632, 7168, 7184


class Buf:
    __slots__ = ("name", "w", "r", "excl")

    def __init__(self, name):
        self.name = name
        self.w = None
        self.r = []
        self.excl = False


class TT:
    def __init__(self, t, name, nb=1):
        self.t = t
        self.b = [Buf(name + str(i)) for i in range(nb)]

    def __getitem__(self, k):
        return self.t[k]


class Sync:
    def __init__(self, nc, stack, n_dma_sems=24, n_pool_sems=70):
        self.nc = nc
        self.eng = {"pe": nc.tensor, "dve": nc.vector, "act": nc.scalar, "pool": nc.gpsimd, "sp": nc.sync}
        self.sems = {}
        self.cnt = {}
        for e in self.eng:
            self.sems[e] = stack.enter_context(nc.semaphore("s_" + e))
            self.cnt[e] = 0
        self.dsems = []
        self.n_sp = n_dma_sems
        for i in range(n_dma_sems + n_pool_sems):
            k = "d%d" % i
            self.sems[k] = stack.enter_context(nc.semaphore("dq_%d" % i))
            self.cnt[k] = 0
            self.dsems.append(k)
        self.dnext = 0
        self.dnext_pool = 0
        self.waited = {e: {} for e in self.eng}

    def _wait(self, e, ev):
        if ev is None:
            return
        k, v = ev
        if self.waited[e].get(k, 0) >= v:
            return
        self.eng[e].wait_ge(self.sems[k], v)
        self.waited[e][k] = v

    @staticmethod
    def _bl(xs):
        out = []
        for x in xs:
            if isinstance(x, TT):
                out.extend(x.b)
            elif isinstance(x, Buf):
                out.append(x)
            else:
                out.extend(x)
        return out

    def _deps(self, e, reads, writes):
        for b in reads:
            self._wait(e, b.w)
            if b.excl:
                for ev in b.r:
                    if ev[0] != e:
                        self._wait(e, ev)
        for b in writes:
            if b.w is not None and (b.w[0] != e or e != "pe"):
                self._wait(e, b.w)
            for ev in b.r:
                if ev[0] != e or e != "pe":
                    self._wait(e, ev)

    def op(self, e, fn, r=(), w=(), serial=False):
        reads, writes = self._bl(r), self._bl(w)
        self._deps(e, reads, writes)
        if serial and self.cnt[e] > 0:
            self._wait(e, (e, self.cnt[e]))
        ins = fn(self.eng[e])
        self.cnt[e] += 1
        ins.then_inc(self.sems[e], 1)
        ev = (e, self.cnt[e])
        for b in reads:
            b.r = [x for x in b.r if x[0] != e] + [ev]
        for b in writes:
            b.w = ev
            b.r = []
        return ins

    def dma(self, q, out, in_, r=(), w=(), **kw):
        reads, writes = self._bl(r), self._bl(w)
        half = self.n_sp
        if q == "pool":
            k = self.dsems[half + self.dnext_pool]
            self.dnext_pool = (self.dnext_pool + 1) % (len(self.dsems) - half)
        else:
            k = self.dsems[self.dnext]
            self.dnext = (self.dnext + 1) % half
        if self.cnt[k] > 0:
            self._wait(q, (k, self.cnt[k]))
        self._deps(q, reads, writes)
        ins = self.eng[q].dma_start(out=out, in_=in_, **kw)
        self.cnt[k] += 16
        ins.then_inc(self.sems[k], 16)
        ev = (k, self.cnt[k])
        for b in reads:
            b.r = b.r + [ev]
        for b in writes:
            b.w = ev
            b.r = []
        return ins

    def barrier(self):
        evs = [(k, v) for k, v in self.cnt.items() if v > 0]
        for e in self.eng:
            for ev in evs:
                if ev[0] != e:
                    self._wait(e, ev)

    def finish(self, e="sp"):
        for k, v in self.cnt.items():
            if v > 0 and k != e:
                self._wait(e, (k, v))


def host_consts():
    c = {}
    theta = (1.0 / (10000.0 ** np.linspace(0.0, 1.0, 32, dtype=np.float32))).astype(np.float32)
    pos = np.arange(SEQ, dtype=np.float32)
    ang = (pos[:, None] * theta[None, :]).astype(np.float32)
    c["cosp"] = np.ascontiguousarray(np.cos(ang).astype(np.float32).reshape(16, 128, 32).transpose(1, 0, 2))
    c["sinp"] = np.ascontiguousarray(np.sin(ang).astype(np.float32).reshape(16, 128, 32).transpose(1, 0, 2))
    angs = (np.float32(PAST) * theta).astype(np.float32)
    c["coss"] = np.tile(np.cos(angs).astype(np.float32)[None, :], (128, 1))
    c["sins"] = np.tile(np.sin(angs).astype(np.float32)[None, :], (128, 1))
    lg = np.log(1.0 - 2.0 ** (-5.0 - np.arange(8, dtype=np.float64)))
    i = np.arange(128, dtype=np.float64)
    diff = i[None, :] - i[:, None]
    dec = np.where(diff[None] >= 0, np.exp(lg[:, None, None] * np.maximum(diff[None], 0.0)), 0.0) * 0.125
    c["decT"] = np.ascontiguousarray(dec.transpose(1, 0, 2)).astype(np.float32)
    c["qdec"] = np.exp(lg[None, :] * (i[:, None] + 1.0)).astype(np.float32)
    c["kdec"] = (np.exp(lg[None, :] * (127.0 - i[:, None])) * 0.125).astype(np.float32)
    gl = np.exp(lg * 128.0)
    gL = np.zeros((128, 4), np.float64)
    for h in range(8):
        gL[(h % 2) * 64:(h % 2) * 64 + 64, h // 2] = gl[h]
    c["gL"] = gL.astype(np.float32)
    c["g1p"] = np.tile(np.exp(lg), 16).astype(np.float32).reshape(128, 1)
    jj = np.arange(128)
    c["tri"] = (jj[:, None] <= jj[None, :]).astype(np.float32)
    c["mneg"] = np.where(jj[:, None] <= jj[None, :], 0.0, NEG).astype(np.float32)
    return c


def bucket_of(d):
    d = np.asarray(d)
    nf = np.maximum(d, 1).astype(np.float32)
    large = 16 + (np.log(nf / np.float32(16)) / np.float32(math.log(128 / 16)) * np.float32(16)).astype(np.int32)
    large = np.minimum(large, 31)
    return np.where(d < 16, d, large)


def build():
    nc = bass.Bass("TRN2", target_bir_lowering=False)
    dt_in = lambda n, s, dt=F32: nc.dram_tensor(n, list(s), dt, kind="ExternalInput").ap()
    dt_out = lambda n, s: nc.dram_tensor(n, list(s), F32, kind="ExternalOutput").ap()
    dt_int = lambda n, s, dt=F32: nc.dram_tensor(n, list(s), dt, kind="Internal").ap()
    xp = dt_in("xp", [NB, SEQ, D]); xs = dt_in("xs", [NS, D]); cc = dt_in("cc", [NB + NS, D])
    cache_k = dt_in("cache_k", [DEPTH, NS, 128, 256]); cache_v = dt_in("cache_v", [DEPTH, NS, 128, 256])
    st_ret = dt_in("st_ret", [DEPTH, NS, 8, 64, 128]); st_ssm = dt_in("st_ssm", [DEPTH, NS, 16, 64, 128])
    st_conv = dt_in("st_conv", [DEPTH, NS, 3, 1536])
    rel_tab = dt_in("rel_tab", [32, 16]); sinks = dt_in("sinks", [DEPTH, 16])
    n1w = dt_in("n1w", [DEPTH, D]); n2w = dt_in("n2w", [DEPTH, D])
    ada_w = dt_in("ada_w", [DEPTH, D, 6 * D]); ada_b = dt_in("ada_b", [DEPTH, 6 * D])
    w_in = dt_in("w_in", [DEPTH, D, DIN]); conv_w = dt_in("conv_w", [DEPTH, 4, 1536]); conv_b = dt_in("conv_b", [DEPTH, 1536])
    dt_bias = dt_in("dt_bias", [DEPTH, 16]); a_log = dt_in("a_log", [DEPTH, 16]); d_skip = dt_in("d_skip", [DEPTH, 16])
    snw = dt_in("snw", [DEPTH, D]); w_out = dt_in("w_out", [DEPTH, D, D]); w_up = dt_in("w_up", [DEPTH, D, DFF])
    w_down = dt_in("w_down", [DEPTH, DFF, D]); fnw = dt_in("fnw", [D])
    hc = {k: dt_in("c_" + k, v.shape) for k, v in host_consts().items()}

    y_p = dt_out("y_p", [NB, SEQ, D]); y_s = dt_out("y_s", [NS, D])
    wk_p = dt_out("wk_p", [DEPTH, NB, 128, 256]); wv_p = dt_out("wv_p", [DEPTH, NB, 128, 256])
    ret_p = dt_out("ret_p", [DEPTH, NB, 8, 64, 128]); ssm_p = dt_out("ssm_p", [DEPTH, NB, 16, 64, 128])
    conv_p = dt_out("conv_p", [DEPTH, NB, 3, 1536])
    wk_s = dt_out("wk_s", [DEPTH, NS, 128, 256]); wv_s = dt_out("wv_s", [DEPTH, NS, 128, 256])
    ret_s = dt_out("ret_s", [DEPTH, NS, 8, 64, 128]); ssm_s = dt_out("ssm_s", [DEPTH, NS, 16, 64, 128])
    conv_s = dt_out("conv_s", [DEPTH, NS, 3, 1536])
    dbg = dt_out("dbg", [3, 128, D]) if DEBUG else None

    winT = dt_int("winT", [DEPTH, 21, 128, 4096], BF16); adaT = dt_int("adaT", [DEPTH, 12, 128, 4096], BF16)
    woutT = dt_int("woutT", [DEPTH, 2, 128, 4096], BF16); wupT = dt_int("wupT", [DEPTH, 8, 128, 4096], BF16)
    wdownT = dt_int("wdownT", [DEPTH, 8, 128, 4096], BF16)
    vecx = dt_int("vecx", [16, 384]); vecf = dt_int("vecf", [16, 384]); modd = dt_int("modd", [DEPTH, NB + NS, 6 * D]); zsd = dt_int("zsd", [NS, DIN])
    osd = dt_int("osd", [3, NS, D])

    with ExitStack() as st:
        S = Sync(nc, st)
        sb = lambda n, s, dt=F32, nb=1: TT(st.enter_context(nc.sbuf_tensor(n, list(s), dt)), n, nb)
        def pb(n, s, dt=F32, nb=1):
            t_ = TT(st.enter_context(nc.psum_tensor(n, list(s), dt)), n, nb)
            for b_ in t_.b:
                b_.excl = True
            return t_
        dram_b = {n: Buf(n) for n in ["winb", "adab", "woutb", "wupb", "wdownb", "vecx", "vecf", "modd", "zsd", "osd", "out"]}
        OUT = dram_b["out"]

        def cast(key, dst, src):
            b_ = Buf(key)
            dram_b.setdefault(key, []).append(b_)
            S.dma("pool", dst, src, w=[b_])

        def tl(T_, l, ti, n=512):
            return T_[l, ti, :, 0:8 * n].rearrange("p (k n) -> p k n", k=8)

        def srcv(w, l, r0, c0, n):
            return w[l, r0:r0 + 1024, c0:c0 + n].rearrange("(k p) n -> p k n", p=128)

        for l in range(DEPTH):
            for ct in range(12):
                cast("adab%d" % l, tl(adaT, l, ct), srcv(ada_w, l, 0, ct * 512, 512))
        def cast_layer(l):
            for j, h in enumerate(QORDER):
                cast("winb%d" % l, tl(winT, l, j // 8)[:, :, (j % 8) * 64:(j % 8 + 1) * 64], srcv(w_in, l, 0, h * 64, 64))
            for ti in range(2, 14):
                cast("winb%d" % l, tl(winT, l, ti), srcv(w_in, l, 0, ti * 512, 512))
            for j in range(6):
                cast("winb%d" % l, tl(winT, l, 14 + j), srcv(w_in, l, 0, O_G + j * 512, 512))
            cast("winb%d" % l, tl(winT, l, 20, 16), srcv(w_in, l, 0, O_DT, 16))
            for j in range(2):
                cast("woutb%d" % l, tl(woutT, l, j), srcv(w_out, l, 0, j * 512, 512))
            for j in range(8):
                cast("wupb%d" % l, tl(wupT, l, j), srcv(w_up, l, 0, j * 512, 512))
            for ct in range(2):
                for sl in range(4):
                    cast("wdownb%d" % l, tl(wdownT, l, ct * 4 + sl), srcv(w_down, l, sl * 1024, ct * 512, 512))


        cast_done = {1: False}
        cast_layer(0)

        identf = sb("identf", [128, 128])
        ident = sb("ident", [128, 128], BF16)
        wbuf = [sb("wbuf%d" % i, [128, 8, 512], BF16) for i in range(3)]
        PD = [pb("PD%d" % i, [128, 512]) for i in range(2)]
        PTb = pb("PTb", [128, 8, 128], BF16, nb=2)
        PS = pb("PS", [128, 512])
        PO = pb("PO", [128, 2048], F32, nb=4)
        xres = sb("xres", [128, D])
        sq = sb("sq", [128, D])
        st8 = sb("st8", [128, 32])
        xn = sb("xn", [128, D], BF16)
        tmp = sb("tmp", [128, D])
        tmp2 = sb("tmp2", [128, D])
        scT = sb("scT", [128, 8, NB + NS], BF16)
        n1c = sb("n1c", [128, DEPTH, 8])
        n2c = sb("n2c", [128, DEPTH, 8])
        cwc = sb("cwc", [128, DEPTH, 4, 12])
        cbc = sb("cbc", [128, DEPTH, 12])
        dtb = sb("dtb", [128, DEPTH, 16])
        Arow = sb("Arow", [128, DEPTH, 16])
        Dsk = sb("Dsk", [128, DEPTH, 16])
        esink = sb("esink", [128, DEPTH, 16])
        snwb = sb("snwb", [128, DEPTH, D])
        fnwb = sb("fnwb", [128, D])
        hT = sb("hT", [128, 8, 128], BF16)
        hT.b = [Buf('hT_%d' % i_) for i_ in range(8)]
        mix = sb("mix", [128, D])
        g1bL = [sb("g1b%d" % l, [128, D]) for l in range(DEPTH)]
        g2bL = [sb("g2b%d" % l, [128, D]) for l in range(DEPTH)]
        modcL = [sb("modc%d" % l, [128, 4, 8]) for l in range(DEPTH)]
        A1L = [sb("A1%d" % l, [128, 8]) for l in range(DEPTH)]
        A2L = [sb("A2%d" % l, [128, 8]) for l in range(DEPTH)]
        gates = sb("gates", [128, 3, D])
        gates.b = [Buf('gates_%d' % i_) for i_ in range(6)]
        bgs = sb("bgs", [128, D])
        bgs.b = [Buf('bgs_%d' % i_) for i_ in range(2)]
        uT = sb("uT", [128, 32, 128], BF16)
        uT.b = [Buf('uT_%d' % i_) for i_ in range(8)]
        stP = st.enter_context(ExitStack())
        sbP = lambda n, s, dt=F32, nb=1: TT(stP.enter_context(nc.sbuf_tensor(n, list(s), dt)), n, nb)
        tri = sbP("tri", [128, 128])
        ones = sbP("ones", [128, 128])
        mnegb = sbP("mnegb", [128, 128], BF16)
        Jb = sbP("Jb", [128, 128], BF16)
        decT = sbP("decT", [128, 8, 128])
        qdec = sbP("qdec", [128, 8])
        kdec = sbP("kdec", [128, 8])
        gL = sbP("gL", [128, 4])
        cosp = sbP("cosp", [128, 16, 32])
        sinp = sbP("sinp", [128, 16, 32])
        BT = sbP("BT", [128, 16, 2, 128], BF16)
        qT = sbP("qT", [128, 8, 128], BF16)
        qT.b = [Buf('qT_%d' % i_) for i_ in range(8)]
        kT = [sbP("kT%d" % l, [128, 2, 256], BF16) for l in range(DEPTH)]
        Vx = [sbP("Vx%d" % l, [128, 2, 4, 65], BF16) for l in range(DEPTH)]
        PTa = sbP("PTa", [128, 2, 512], BF16)
        rec = sbP("rec", [128, 16])
        rtq, rtk = [], []
        for i_ in range(4):
            r_ = TT(tmp.t[:, i_ * 256:(i_ + 1) * 256].rearrange("p (h f) -> p h f", h=8), "rtq%d" % i_)
            r_.b = tmp.b
            rtq.append(r_)
            r_ = TT(tmp2.t[:, i_ * 256:(i_ + 1) * 256].rearrange("p (h f) -> p h f", h=8), "rtk%d" % i_)
            r_.b = tmp2.b
            rtk.append(r_)
        bqk = sbP("bqk", [128, 1024])
        bqk.b = [Buf('bqk_%d' % i_) for i_ in range(2)]
        czs = sbP("czs", [128, D])
        czs.b = [Buf('czs_%d' % i_) for i_ in range(2)]
        qrot = sbP("qrot", [128, 512], BF16)
        krot = sbP("krot", [128, 512], BF16)
        qd = sbP("qd", [128, 512], BF16)
        kd = sbP("kd", [128, 512], BF16)
        qkT = sbP("qkT", [128, 12, 128], BF16)
        bv = sbP("bv", [128, D], BF16)
        bv.b = [Buf('bv_%d' % i_) for i_ in range(2)]
        Sret = [sbP("Sret%d" % l, [128, 4, 128]) for l in range(DEPTH)]
        Sbf = [sbP("Sbf%d" % l, [128, 4, 128], BF16) for l in range(DEPTH)]
        xbcT = sbP("xbcT", [128, 12, 131])
        xbcT.b = [Buf('xbcT_%d' % i_) for i_ in range(12)]
        chist = [sbP("chist%d" % l, [128, 12, 3]) for l in range(DEPTH)]
        xcT = sbP("xcT", [128, 12, 128])
        bcTb = sbP("bcTb", [128, 4, 128], BF16)
        dts = sbP("dts", [128, 8, 16])
        tabT = TT(dts.t[0:16, 0:2, :].rearrange("p a b -> p (a b)"), "tabT"); tabT.b = dts.b
        vx = TT(bqk.t[0:16, 0:384], "vx"); vx.b = bqk.b
        dtAb = sbP("dtAb", [128, 4, 128])
        LT = sbP("LT", [128, 8, 128])
        MT = sbP("MT", [128, 16, 128], BF16)
        xdt = sbP("xdt", [128, D], BF16)
        xw = sbP("xw", [128, D], BF16)
        Btm = sbP("Btm", [128, 256], BF16)
        hS = [sbP("hS%d" % l, [128, D]) for l in range(DEPTH)]
        hSb = [sbP("hSb%d" % l, [128, D], BF16) for l in range(DEPTH)]
        S.op("pool", lambda e: e.memset(identf[:], 0.0), w=[identf])
        S.op("pool", lambda e: e.affine_select(out=identf[:], in_=identf[:], pattern=[[-1, 128]], compare_op=ALU.not_equal,
                                               fill=1.0, base=0, channel_multiplier=1), r=[identf], w=[identf])
        S.op("dve", lambda e: e.tensor_copy(out=ident[:], in_=identf[:]), r=[identf], w=[ident])
        S.op("pool", lambda e: e.memset(ones[:], 1.0), w=[ones])
        for t, k in [(tri, "tri"), (decT, "decT"), (qdec, "qdec"), (kdec, "kdec"), (gL, "gL"), (cosp, "cosp"), (sinp, "sinp")]:
            S.dma("sp", t[:], hc[k], w=[t])
        S.dma("pool", mnegb[:], hc["mneg"], w=[mnegb])

        vxf = TT(bqk.t[0:16, 384:768], "vxf"); vxf.b = bqk.b
        S.dma("sp", tabT[:], rel_tab.rearrange("b h -> h b"), w=[tabT], allow_slow_non_contiguous=True)
        S.op("dve", lambda e: e.memset(vx[:], NEG), w=[vx])
        S.op("dve", lambda e: e.memset(vxf[:], NEG), w=[vxf])
        bk = bucket_of(np.arange(128))
        d0 = 0
        while d0 < 128:
            d1 = d0
            while d1 + 1 < 128 and bk[d1 + 1] == bk[d0]:
                d1 += 1
            b = int(bk[d0])
            S.op("dve", lambda e, lo=255 - d1, hi=255 - d0 + 1, b=b: e.tensor_scalar(
                out=vx[:, lo:hi], in0=vx[:, lo:hi], scalar1=0.0, scalar2=tabT[:, b:b + 1], op0=ALU.mult, op1=ALU.add),
                r=[tabT, vx], w=[vx])
            S.op("dve", lambda e, lo=127 + d0, hi=127 + d1 + 1, b=b: e.tensor_scalar(
                out=vxf[:, lo:hi], in0=vxf[:, lo:hi], scalar1=0.0, scalar2=tabT[:, b:b + 1], op0=ALU.mult, op1=ALU.add),
                r=[tabT, vxf], w=[vxf])
            d0 = d1 + 1
        S.dma("sp", vecx, vx[:], r=[vx], w=[dram_b["vecx"]])
        S.dma("sp", vecf, vxf[:], r=[vxf], w=[dram_b["vecf"]])
        S.op("pool", lambda e: e.memset(ones[:], 0.0), w=[ones])
        S.op("pool", lambda e: e.affine_select(out=ones[:], in_=ones[:], pattern=[[1, 128]], compare_op=ALU.not_equal,
                                               fill=1.0, base=-127, channel_multiplier=1), r=[ones], w=[ones])
        S.op("dve", lambda e: e.tensor_copy(out=Jb[:], in_=ones[:]), r=[ones], w=[Jb])
        S.op("pool", lambda e: e.memset(ones[:], 1.0), r=[Jb], w=[ones])
        for h4 in range(4):
            for hh in range(4):
                h = 4 * h4 + hh
                for blk, off in ((1, 0), (0, 128)):
                    src = bass.AP(vecf.tensor, off + 384 * h, [[1, 128], [1, 128]])
                    S.dma("sp", tmp[:, (hh * 2 + blk) * 128:(hh * 2 + blk + 1) * 128], src, r=[dram_b["vecf"]], w=[tmp])
            S.op("dve", lambda e: e.tensor_scalar(out=xn[:], in0=tmp[:], scalar1=8.0, scalar2=None, op0=ALU.mult), r=[tmp], w=[xn])
            for hf2 in range(2):
                pd = PD[hf2]
                S.op("pe", lambda e, pd=pd, hf2=hf2: e.matmul(out=pd[:], lhsT=Jb[:], rhs=xn[:, hf2 * 512:(hf2 + 1) * 512], start=True, stop=True), r=[Jb, xn], w=[pd])
                S.op("act", lambda e, pd=pd, hf2=hf2, h4=h4: e.copy(out=BT[:, 4 * h4 + 2 * hf2:4 * h4 + 2 * hf2 + 2, :, :].rearrange("p h b q -> p (h b q)"), in_=pd[:]),
                     r=[pd], w=[BT])

        csb, csil, adabias, modt = tmp, xn, tmp2, sq
        sq.b = [Buf('sq_a'), Buf('sq_b')]
        xn.b = [Buf('xn_a'), Buf('xn_b')]
        sqh, xnh = sq.b, xn.b
        S.dma("sp", csb[0:NB + NS, :], cc, w=[csb])
        S.op("act", lambda e: e.activation(out=csil[0:NB + NS, :], in_=csb[0:NB + NS, :], func=AF.Silu), r=[csb], w=[csil])
        for k in range(8):
            S.op("pe", lambda e, k=k: e.transpose(out=PTb[0:128, k, 0:NB + NS], in_=csil[0:NB + NS, k * 128:(k + 1) * 128],
                                                  identity=ident[0:NB + NS, 0:NB + NS]), r=[csil, ident], w=[PTb])
        S.op("dve", lambda e: e.tensor_copy(out=scT[:], in_=PTb[:, :, 0:NB + NS]), r=[PTb], w=[scT])
        R = NB + NS
        for l in range(DEPTH):
            for ct in range(12):
                wb_ = wbuf[ct % 3]
                S.dma("sp", wb_[:], tl(adaT, l, ct), r=[dram_b["adab%d" % l]], w=[wb_])
                S.dma("pool", adabias[0:R, 0:512], bass.AP(ada_b.tensor, l * 6 * D + ct * 512, [[0, R], [1, 512]]), w=[adabias])
                pd = PD[ct % 2]
                for k in range(8):
                    S.op("pe", lambda e, k=k, pd=pd, wb_=wb_: e.matmul(out=pd[0:R, :], lhsT=scT[:, k, :], rhs=wb_[:, k, :],
                                                                     start=(k == 0), stop=(k == 7)), r=[scT, wb_], w=[pd])
                S.op("dve", lambda e, pd=pd: e.tensor_tensor(out=modt[0:R, 0:512], in0=pd[0:R, :], in1=adabias[0:R, 0:512], op=ALU.add),
                     r=[pd, adabias], w=[modt])
                S.dma("sp", modd[l, :, ct * 512:(ct + 1) * 512], modt[0:R, 0:512], r=[modt], w=[dram_b["modd"]])

        for l in range(DEPTH):
            S.dma("sp", n1c[:, l, :], n1w[l].rearrange("(k p) -> p k", p=128), w=[n1c], allow_slow_non_contiguous=True)
            S.dma("sp", n2c[:, l, :], n2w[l].rearrange("(k p) -> p k", p=128), w=[n2c], allow_slow_non_contiguous=True)
        for l in range(DEPTH):
            for i in range(4):
                S.dma("sp", cwc[:, l, i, :], conv_w[l, i].rearrange("(j p) -> p j", p=128), w=[cwc], allow_slow_non_contiguous=True)
            S.dma("sp", cbc[:, l, :], conv_b[l].rearrange("(j p) -> p j", p=128), w=[cbc], allow_slow_non_contiguous=True)
        bc16 = lambda t, l: bass.AP(t.tensor, l * 16, [[0, 128], [1, 16]])
        for l in range(DEPTH):
            S.dma("sp", dtb[:, l, :], bc16(dt_bias, l), w=[dtb])
            S.dma("sp", Arow[:, l, :], bc16(a_log, l), w=[Arow])
            S.dma("sp", Dsk[:, l, :], bc16(d_skip, l), w=[Dsk])
            S.dma("sp", esink[:, l, :], bc16(sinks, l), w=[esink])
        S.op("act", lambda e: e.activation(out=Arow[:], in_=Arow[:], func=AF.Exp), r=[Arow], w=[Arow])
        S.op("dve", lambda e: e.tensor_scalar(out=Arow[:], in0=Arow[:], scalar1=-1.0, scalar2=None, op0=ALU.mult), r=[Arow], w=[Arow])
        S.op("act", lambda e: e.activation(out=esink[:], in_=esink[:], func=AF.Exp), r=[esink], w=[esink])
        for l in range(DEPTH):
            S.dma("sp", snwb[:, l, :], bass.AP(snw.tensor, l * D, [[0, 128], [1, D]]), w=[snwb])
        S.dma("sp", fnwb[:], bass.AP(fnw.tensor, 0, [[0, 128], [1, D]]), w=[fnwb])

        xcs = sq
        innT = TT(MT.t[:, 0:8, :], "innT"); innT.b = MT.b
        xres1 = sbP("xres1", [128, D])
        xresL = [xres, xres1]
        hT_main = hT
        hTm = [sbP("hTm%d" % i, [128, 8, 128], BF16) for i in range(2)]
        for h_ in hTm:
            h_.b = [Buf("hTm_%d" % i_) for i_ in range(8)]
        uTb = sbP("uTb", [128, 32, 128], BF16)
        uTb.b = [Buf('uTb_%d' % i_) for i_ in range(8)]
        uTL = [uT, uTb]

        wq = {"i": 0}

        def wload(src_ap, ncols=512, dep=()):
            wb_ = wbuf[wq["i"] % 3]
            wq["i"] += 1
            S.dma("sp", wb_[:, :, 0:ncols], src_ap, r=[dep], w=[wb_])
            return wb_

        def win_tile(l, c0, n=512):
            ti = 20 if c0 == O_DT else (14 + (c0 - O_G) // 512 if c0 >= O_G else c0 // 512)
            return wload(tl(winT, l, ti, n), n, dram_b["winb%d" % l]), None

        def rms_stats(xt, rows, col):
            S.op("act", lambda e: e.activation(out=sq[0:rows, :], in_=xt[0:rows, :], func=AF.Square, accum_out=st8[0:rows, col:col + 1]),
                 r=[xt], w=[sq, st8])
            S.op("act", lambda e: e.activation(out=st8[0:rows, col + 1:col + 2], in_=st8[0:rows, col:col + 1], func=AF.Ln,
                                               scale=1.0 / D, bias=EPS), r=[st8], w=[st8])
            S.op("act", lambda e: e.activation(out=st8[0:rows, col + 2:col + 3], in_=st8[0:rows, col + 1:col + 2], func=AF.Exp, scale=-0.5), r=[st8], w=[st8])
            return col + 2

        def norm_to_hT(A, shT, shj):
            sh = shT[:, shj, :]
            c = rms_stats(xres, 128, 0)
            S.op("dve", lambda e: e.tensor_scalar(out=xn[:], in0=xres[:], scalar1=st8[:, c:c + 1], scalar2=None, op0=ALU.mult),
                 r=[xres, st8], w=[xn])
            for k in range(8):
                S.op("pe", lambda e, k=k: e.transpose(out=PTb[:, k, :], in_=xn[:, k * 128:(k + 1) * 128], identity=ident[:]),
                     r=[xn, ident], w=[PTb])
            for k in range(8):
                S.op("act", lambda e, k=k: e.activation(out=hT[:, k, :], in_=PTb[:, k, :], func=AF.Identity,
                                                        scale=A[:, k:k + 1], bias=sh[:, k:k + 1]), r=[PTb, A, shT], w=[hT.b[k]])

        def mm_tm(pd, wb_, c0, n, src=None):
            src = src or hT
            for k in range(8):
                S.op("pe", lambda e, k=k: e.matmul(out=pd[:, 0:n], lhsT=src[:, k, :], rhs=wb_[:, k, c0:c0 + n],
                                                   start=(k == 0), stop=(k == 7)), r=[src, wb_], w=[pd])

        def mm_fm(pd, wb_, c0):
            for k in range(8):
                S.op("pe", lambda e, k=k: e.matmul(out=pd[:, 0:128], lhsT=wb_[:, k, c0:c0 + 128], rhs=hT[:, k, :],
                                                   start=(k == 0), stop=(k == 7)), r=[hT, wb_], w=[pd])

        pdi = {"i": 0}

        def nextpd():
            pdi["i"] += 1
            return PD[pdi["i"] % 2]

        def layer_chunk(l, b, t, last_layer):
            modc, A1, A2, g1b, g2b = modcL[l], A1L[l], A2L[l], g1bL[l], g2bL[l]
            if t == 0:
                for j, col in enumerate((0, 1, 3, 4)):
                    S.dma("pool", modc[:, j, :], modd[l, b, col * D:(col + 1) * D].rearrange("(k p) -> p k", p=128),
                          r=[dram_b["modd"]], w=[modc], allow_slow_non_contiguous=True)
                S.op("dve", lambda e: e.scalar_tensor_tensor(out=A1[:], in0=modc[:, 1, :], scalar=1.0, in1=n1c[:, l, :],
                                                             op0=ALU.add, op1=ALU.mult), r=[modc, n1c], w=[A1])
                S.op("dve", lambda e: e.scalar_tensor_tensor(out=A2[:], in0=modc[:, 3, :], scalar=1.0, in1=n2c[:, l, :],
                                                             op0=ALU.add, op1=ALU.mult), r=[modc, n2c], w=[A2])
                S.dma("pool", g1b[:], bass.AP(modd.tensor, (l * R + b) * 6 * D + 2 * D, [[0, 128], [1, D]]), r=[dram_b["modd"]], w=[g1b])
                S.dma("pool", g2b[:], bass.AP(modd.tensor, (l * R + b) * 6 * D + 5 * D, [[0, 128], [1, D]]), r=[dram_b["modd"]], w=[g2b])
            norm_to_hT(A1, modc, 0)

            def gPA():
                if t > 0:
                    S.op("pool", lambda e: e.tensor_copy(out=kT[l][:, :, 0:128], in_=kT[l][:, :, 128:256]), r=[kT[l]], w=[kT[l]])
                    S.op("pool", lambda e: e.tensor_copy(out=Vx[l][:, 0, :, :], in_=Vx[l][:, 1, :, :]), r=[Vx[l]], w=[Vx[l]])
                else:
                    S.op("pool", lambda e: e.memset(Vx[l][:, :, :, 64:65], 1.0), w=[Vx[l]])
                for half in range(2):
                    wb_, wr = win_tile(l, O_AQ + half * 512)
                    for i in range(4):
                        pd = nextpd()
                        mm_fm(pd, wb_, i * 128)
                        S.op("act", lambda e, pd=pd, i=i: e.copy(out=qT[:, half * 4 + i, :], in_=pd[:, 0:128]), r=[pd], w=[qT.b[half * 4 + i]])
                    yield
                wb_, wr = win_tile(l, O_AK)
                for i in range(2):
                    pd = nextpd()
                    mm_fm(pd, wb_, i * 128)
                    S.op("act", lambda e, pd=pd, i=i: e.copy(out=kT[l][:, i, 128:256], in_=pd[:, 0:128]), r=[pd], w=[kT[l]])
                pd = nextpd()
                mm_tm(pd, wb_, 256, 256)
                S.op("dve", lambda e, pd=pd: e.tensor_copy(out=Vx[l][:, 1, :, 0:64], in_=pd[:, 0:256].rearrange("p (g d) -> p g d", g=4)),
                     r=[pd], w=[Vx[l]])
                if t == T_LAST and DO_KV:
                    S.op("act", lambda e, pd=pd: e.copy(out=tmp2[:, 256:512], in_=pd[:, 0:256]), r=[pd], w=[tmp2])
                    pd2 = nextpd()
                    mm_tm(pd2, wb_, 0, 256)
                    S.op("act", lambda e, pd2=pd2: e.copy(out=tmp2[:, 0:256], in_=pd2[:, 0:256]), r=[pd2], w=[tmp2])
                    S.dma("pool", wk_p[l, b], tmp2[:, 0:256], r=[tmp2], w=[OUT])
                    S.dma("pool", wv_p[l, b], tmp2[:, 256:512], r=[tmp2], w=[OUT])
                yield
                for j in range(6):
                    wb_, wr = win_tile(l, O_G + j * 512)
                    pd = nextpd()
                    mm_tm(pd, wb_, 0, 512)
                    S.op("act", lambda e, pd=pd, j=j: e.activation(out=gates[:, j // 2, (j % 2) * 512:(j % 2) * 512 + 512], in_=pd[:],
                                                                   func=AF.Sigmoid), r=[pd], w=[gates.b[j]])
                    yield

            def gMA():
                blocks = (1,) if t == 0 else (0, 1)
                for g in range(4):
                    hf = (g % 2) * 64
                    for bi, blk in enumerate(blocks):
                        pd = nextpd()
                        S.op("pe", lambda e, g=g, hf=hf, blk=blk, pd=pd: e.matmul(
                            out=pd[:], lhsT=kT[l][hf:hf + 64, g // 2, blk * 128:(blk + 1) * 128],
                            rhs=qT[hf:hf + 64, (g // 2) * 4:(g // 2) * 4 + 4, :], start=True, stop=False), r=[kT[l], qT], w=[pd], serial=True)
                        S.op("pe", lambda e, g=g, blk=blk, pd=pd: e.matmul(
                            out=pd[:], lhsT=ident[:], rhs=BT[:, 4 * g:4 * g + 4, blk, :], start=False, stop=True), r=[ident, BT], w=[pd], serial=True)
                        S.op("act", lambda e, bi=bi, pd=pd: e.activation(out=PTa[:, bi, :], in_=pd[:], func=AF.Exp, scale=0.125), r=[pd], w=[PTa])
                    yield
                    for hq in range(4):
                        h = 4 * g + hq
                        hs = h % 8
                        for bi, blk in enumerate(blocks):
                            S.op("pe", lambda e, hs=hs, hq=hq, blk=blk, bi=bi: e.matmul(
                                out=PO[:, hs * 128:hs * 128 + 65], lhsT=PTa[:, bi, hq * 128:(hq + 1) * 128], rhs=Vx[l][:, blk, g, :],
                                start=(bi == 0), stop=(bi == len(blocks) - 1)), r=[PTa, Vx[l]], w=[PO.b[hs // 4]])
                    yield
                    if g % 2 == 1:
                        r_ = g // 2
                        po3 = PO[:, 0:1024].rearrange("p (h c) -> p h c", h=8)
                        S.op("dve", lambda e, r_=r_: e.tensor_tensor(out=rec[:, 8 * r_:8 * r_ + 8], in0=po3[:, :, 64], in1=esink[:, l, 8 * r_:8 * r_ + 8], op=ALU.add),
                             r=[PO.b[0], PO.b[1], esink], w=[rec])
                        S.op("dve", lambda e, r_=r_: e.reciprocal(out=rec[:, 8 * r_:8 * r_ + 8], in_=rec[:, 8 * r_:8 * r_ + 8]), r=[rec], w=[rec])
                        S.op("dve", lambda e, r_=r_: e.tensor_tensor(out=czs[:, r_ * 512:(r_ + 1) * 512].rearrange("p (h d) -> p h d", h=8), in0=po3[:, :, 0:64],
                                                              in1=rec[:, 8 * r_:8 * r_ + 8].unsqueeze(2).to_broadcast([128, 8, 64]), op=ALU.mult),
                             r=[PO.b[0], PO.b[1], rec], w=[czs])
                        yield
                S.op("dve", lambda e: e.tensor_tensor(out=czs[:], in0=czs[:], in1=gates[:, 0, :], op=ALU.mult), r=[czs, gates], w=[czs])
                S.op("dve", lambda e: e.tensor_tensor(out=mix[:], in0=mix[:], in1=czs[:], op=ALU.add), r=[czs, mix], w=[mix])

            def gPB():
                for j in range(2):
                    wb_, wr = win_tile(l, O_BQ + j * 512)
                    pd = nextpd()
                    mm_tm(pd, wb_, 0, 512)
                    S.op("act", lambda e, pd=pd, j=j: e.copy(out=bqk[:, j * 512:(j + 1) * 512], in_=pd[:]), r=[pd], w=[bqk.b[j]])
                for j in range(2):
                    wb_, wr = win_tile(l, O_BV + j * 512)
                    pd = nextpd()
                    mm_tm(pd, wb_, 0, 512)
                    S.op("act", lambda e, pd=pd, j=j: e.copy(out=bv[:, j * 512:(j + 1) * 512], in_=pd[:]), r=[pd], w=[bv.b[j]])
                    yield
                for j in range(2):
                    wb_, wr = win_tile(l, O_BG + j * 512)
                    pd = nextpd()
                    mm_tm(pd, wb_, 0, 512)
                    S.op("act", lambda e, pd=pd, j=j: e.activation(out=bgs[:, j * 512:(j + 1) * 512], in_=pd[:], func=AF.Silu), r=[pd], w=[bgs.b[j]])
                    yield
                S.op("pool", lambda e: e.tensor_tensor(out=bgs[:], in0=bgs[:], in1=gates[:, 1, :], op=ALU.mult), r=[bgs, gates], w=[bgs])

            def gMB():
                cb_ = cosp[:, t, :].unsqueeze(1).to_broadcast([128, 8, 32])
                sb_ = sinp[:, t, :].unsqueeze(1).to_broadcast([128, 8, 32])
                for j, (dst, sct, dect) in enumerate(((qrot, qd, qdec), (krot, kd, kdec))):
                    v4 = bqk[:, j * 512:(j + 1) * 512].rearrange("p (h f two) -> p h f two", h=8, two=2)
                    x1, x2 = v4[:, :, :, 0], v4[:, :, :, 1]
                    d4 = dst[:].rearrange("p (h f two) -> p h f two", h=8, two=2)
                    eng = "dve" if j == 0 else "pool"
                    rt = rtq if j == 0 else rtk
                    S.op(eng, lambda e, x1=x1: e.tensor_tensor(out=rt[0][:], in0=x1, in1=cb_, op=ALU.mult), r=[bqk, cosp], w=[rt[0]])
                    S.op(eng, lambda e, x2=x2: e.tensor_tensor(out=rt[1][:], in0=x2, in1=sb_, op=ALU.mult), r=[bqk, sinp], w=[rt[1]])
                    S.op(eng, lambda e, d4=d4: e.tensor_tensor(out=d4[:, :, :, 0], in0=rt[0][:], in1=rt[1][:], op=ALU.subtract), r=[rt[0], rt[1]], w=[dst])
                    S.op(eng, lambda e, x1=x1: e.tensor_tensor(out=rt[2][:], in0=x1, in1=sb_, op=ALU.mult), r=[bqk, sinp], w=[rt[2]])
                    S.op(eng, lambda e, x2=x2: e.tensor_tensor(out=rt[3][:], in0=x2, in1=cb_, op=ALU.mult), r=[bqk, cosp], w=[rt[3]])
                    S.op(eng, lambda e, d4=d4: e.tensor_tensor(out=d4[:, :, :, 1], in0=rt[2][:], in1=rt[3][:], op=ALU.add), r=[rt[2], rt[3]], w=[dst])
                    S.op(eng, lambda e, dst=dst, sct=sct, dect=dect: e.tensor_tensor(
                        out=sct[:].rearrange("p (h d) -> p h d", h=8), in0=dst[:].rearrange("p (h d) -> p h d", h=8),
                        in1=dect[:].unsqueeze(2).to_broadcast([128, 8, 64]), op=ALU.mult), r=[dst, dect], w=[sct])
                yield
                for gi, srcb in enumerate((qrot, qd, krot)):
                    for i in range(4):
                        S.op("pe", lambda e, srcb=srcb, i=i: e.transpose(out=PTb[:, i, :], in_=srcb[:, i * 128:(i + 1) * 128], identity=ident[:]),
                             r=[srcb, ident], w=[PTb])
                    S.op("act", lambda e, gi=gi: e.copy(out=qkT[:, gi * 4:gi * 4 + 4, :], in_=PTb[:, 0:4, :]), r=[PTb], w=[qkT])
                yield
                for h in range(8):
                    hf = (h % 2) * 64
                    S.op("pe", lambda e, h=h, hf=hf: e.matmul(out=PO[:, 1024 + h * 128:1024 + (h + 1) * 128], lhsT=qkT[hf:hf + 64, 8 + h // 2, :],
                                                              rhs=qkT[hf:hf + 64, h // 2, :], start=True, stop=True), r=[qkT], w=[PO.b[2 + h // 4]], serial=True)
                S.op("dve", lambda e: e.tensor_tensor(out=innT[:], in0=PO[:, 1024:2048].rearrange("p (h q) -> p h q", h=8), in1=decT[:], op=ALU.mult),
                     r=[PO.b[2], PO.b[3], decT], w=[innT])
                for h in range(8):
                    hf = (h % 2) * 64
                    S.op("pe", lambda e, h=h: e.matmul(out=PO[:, 1024 + h * 128:1024 + (h + 1) * 128], lhsT=innT[:, h, :],
                                                       rhs=bv[:, h * 128:(h + 1) * 128], start=True, stop=False), r=[innT, bv], w=[PO.b[2 + h // 4]])
                    S.op("pe", lambda e, h=h, hf=hf: e.matmul(out=PO[:, 1024 + h * 128:1024 + (h + 1) * 128], lhsT=qkT[hf:hf + 64, 4 + h // 2, :],
                                                              rhs=Sbf[l][hf:hf + 64, h // 2, :], start=False, stop=True), r=[qkT, Sbf[l]], w=[PO.b[2 + h // 4]], serial=True)
                yield
                for i in range(4):
                    pd = nextpd()
                    S.op("pe", lambda e, i=i, pd=pd: e.matmul(out=pd[:, 0:256], lhsT=kd[:, i * 128:(i + 1) * 128], rhs=bv[:, i * 256:(i + 1) * 256],
                                                              start=True, stop=True), r=[kd, bv], w=[pd])
                    for hfi in range(2):
                        rs = slice(hfi * 64, hfi * 64 + 64)
                        S.op("dve", lambda e, i=i, pd=pd, rs=rs, hfi=hfi: e.scalar_tensor_tensor(
                            out=Sret[l][rs, i, :], in0=Sret[l][rs, i, :], scalar=gL[rs, i:i + 1], in1=pd[rs, hfi * 128:(hfi + 1) * 128],
                            op0=ALU.mult, op1=ALU.add), r=[Sret[l], gL, pd, Sbf[l]], w=[Sret[l]])
                S.op("act", lambda e: e.copy(out=Sbf[l][:], in_=Sret[l][:]), r=[Sret[l]], w=[Sbf[l]])
                yield
                o3 = PO[:, 1024:2048].rearrange("p (h e) -> p h e", h=8)
                S.op("act", lambda e: e.activation(out=sq[:], in_=PO[:, 1024:2048], func=AF.Square), r=[PO.b[2], PO.b[3]], w=[sq])
                S.op("dve", lambda e: e.tensor_reduce(out=st8[:, 8:16], in_=sq[:].rearrange("p (h e) -> p h e", h=8), axis=AX.X, op=ALU.add),
                     r=[sq], w=[st8])
                S.op("act", lambda e: e.activation(out=st8[:, 8:16], in_=st8[:, 8:16], func=AF.Ln, scale=1.0 / 128, bias=EPS), r=[st8], w=[st8])
                S.op("act", lambda e: e.activation(out=st8[:, 8:16], in_=st8[:, 8:16], func=AF.Exp, scale=-0.5), r=[st8], w=[st8])
                S.op("dve", lambda e: e.tensor_tensor(out=tmp[:].rearrange("p (h e) -> p h e", h=8), in0=o3,
                                                      in1=st8[:, 8:16].unsqueeze(2).to_broadcast([128, 8, 128]), op=ALU.mult), r=[PO.b[2], PO.b[3], st8], w=[tmp])
                S.op("dve", lambda e: e.tensor_tensor(out=tmp[:], in0=tmp[:], in1=bgs[:], op=ALU.mult), r=[tmp, bgs], w=[tmp])
                S.op("dve", lambda e: e.tensor_tensor(out=mix[:], in0=mix[:], in1=tmp[:], op=ALU.add), r=[tmp, mix], w=[mix])


            def gPC():
                for j in range(2):
                    wb_, wr = win_tile(l, O_CZ + j * 512)
                    pd = nextpd()
                    mm_tm(pd, wb_, 0, 512)
                    S.op("act", lambda e, pd=pd, j=j: e.activation(out=czs[:, j * 512:(j + 1) * 512], in_=pd[:], func=AF.Silu), r=[pd], w=[czs.b[j]])
                    yield
                S.op("pool", lambda e: e.tensor_copy(out=xbcT[:, :, 0:3], in_=chist[l][:]), r=[chist[l]], w=[xbcT])
                for j3 in range(3):
                    wb_, wr = win_tile(l, O_XBC + j3 * 512)
                    for i in range(4):
                        pd = nextpd()
                        mm_fm(pd, wb_, i * 128)
                        S.op("act", lambda e, pd=pd, jj=j3 * 4 + i: e.copy(out=xbcT[:, jj, 3:131], in_=pd[:, 0:128]), r=[pd], w=[xbcT.b[j3 * 4 + i]])
                    yield
                yield
                S.op("pool", lambda e: e.tensor_copy(out=chist[l][:], in_=xbcT[:, :, 128:131]), r=[xbcT], w=[chist[l]])
                if t == T_LAST and DO_CONV:
                    for j in range(12):
                        S.op("pe", lambda e, j=j: e.transpose(out=PO[0:3, j * 128:(j + 1) * 128], in_=chist[l][:, j, :], identity=identf[:]),
                             r=[chist[l], identf], w=[PO.b[j // 4]])
                    S.op("act", lambda e: e.copy(out=tmp[0:3, 0:1024], in_=PO[0:3, 0:1024]), r=[PO.b[0], PO.b[1]], w=[tmp])
                    S.op("act", lambda e: e.copy(out=sq[0:3, 0:512], in_=PO[0:3, 1024:1536]), r=[PO.b[2]], w=[sq])
                    S.dma("pool", conv_p[l, b, :, 0:1024], tmp[0:3, 0:1024], r=[tmp], w=[OUT])
                    S.dma("pool", conv_p[l, b, :, 1024:1536], sq[0:3, 0:512], r=[sq], w=[OUT])
                wb_, wr = win_tile(l, O_DT, 16)
                pd = nextpd()
                mm_tm(pd, wb_, 0, 16)
                S.op("dve", lambda e, pd=pd: e.tensor_tensor(out=dts[:, 0, :], in0=pd[:, 0:16], in1=dtb[:, l, :], op=ALU.add), r=[pd, dtb], w=[dts])

            def gMC():
                for jj in range(12):
                    if jj % 4 == 0:
                        yield
                    eng = "dve"
                    S.op(eng, lambda e, jj=jj: e.tensor_scalar(out=xcT[:, jj, :], in0=xbcT[:, jj, 0:128], scalar1=cwc[:, l, 0, jj:jj + 1],
                                                               scalar2=cbc[:, l, jj:jj + 1], op0=ALU.mult, op1=ALU.add), r=[xbcT, cwc, cbc], w=[xcT])
                    for i in range(1, 4):
                        S.op(eng, lambda e, jj=jj, i=i: e.scalar_tensor_tensor(out=xcT[:, jj, :], in0=xbcT[:, jj, i:i + 128], scalar=cwc[:, l, i, jj:jj + 1],
                                                                               in1=xcT[:, jj, :], op0=ALU.mult, op1=ALU.add), r=[xbcT, cwc, xcT], w=[xcT])
                yield
                S.op("act", lambda e: e.activation(out=xcT[:], in_=xcT[:], func=AF.Silu), r=[xcT], w=[xcT])
                S.op("pool", lambda e: e.tensor_copy(out=bcTb[:], in_=xcT[:, 8:12, :]), r=[xcT], w=[bcTb])
                S.op("act", lambda e: e.activation(out=dts[:, 1, :], in_=dts[:, 0, :], func=AF.Exp), r=[dts], w=[dts])
                S.op("act", lambda e: e.activation(out=dts[:, 2, :], in_=dts[:, 1, :], func=AF.Ln, bias=1.0), r=[dts], w=[dts])
                S.op("dve", lambda e: e.tensor_tensor(out=dts[:, 3, :], in0=dts[:, 2, :], in1=Arow[:, l, :], op=ALU.mult), r=[dts, Arow], w=[dts])
                yield
                pd = nextpd()
                S.op("pe", lambda e, pd=pd: e.matmul(out=pd[:, 0:16], lhsT=tri[:], rhs=dts[:, 3, :], start=True, stop=True), r=[tri, dts], w=[pd])
                S.op("pe", lambda e, pd=pd: e.matmul(out=pd[:, 16:32], lhsT=ones[:], rhs=dts[:, 3, :], start=True, stop=True), r=[ones, dts], w=[pd])
                S.op("dve", lambda e, pd=pd: e.tensor_copy(out=dts[:, 4, :], in_=pd[:, 0:16]), r=[pd], w=[dts])
                S.op("dve", lambda e, pd=pd: e.tensor_scalar(out=dts[:, 5, :], in0=pd[:, 0:16], scalar1=-1.0, scalar2=None, op0=ALU.mult), r=[pd], w=[dts])
                S.op("dve", lambda e, pd=pd: e.tensor_tensor(out=dts[:, 6, :], in0=pd[:, 16:32], in1=dts[:, 4, :], op=ALU.subtract), r=[pd, dts], w=[dts])
                S.op("act", lambda e, pd=pd: e.activation(out=dts[:, 7, :], in_=pd[:, 16:32], func=AF.Exp), r=[pd], w=[dts])
                S.op("act", lambda e: e.activation(out=dts[:, 6, :], in_=dts[:, 6, :], func=AF.Exp), r=[dts], w=[dts])
                S.op("dve", lambda e: e.tensor_tensor(out=dts[:, 6, :], in0=dts[:, 6, :], in1=dts[:, 2, :], op=ALU.mult), r=[dts], w=[dts])
                S.op("act", lambda e: e.activation(out=dts[:, 1, :], in_=dts[:, 4, :], func=AF.Exp), r=[dts], w=[dts])
                for g in range(2):
                    for hg2 in range(2):
                        yield
                        hg = g * 2 + hg2
                        S.op("pool", lambda e, hg=hg: e.tensor_copy(out=dtAb[:], in_=dts[:, 3, 4 * hg:4 * hg + 4].unsqueeze(2).to_broadcast([128, 4, 128])),
                             r=[dts], w=[dtAb])
                        for hh in range(4):
                            S.op("pe", lambda e, hh=hh: e.matmul(out=PS[:, hh * 128:(hh + 1) * 128], lhsT=dtAb[:, hh, :], rhs=tri[:], start=True, stop=False),
                                 r=[dtAb, tri], w=[PS])
                            S.op("pe", lambda e, hh=hh: e.matmul(out=PS[:, hh * 128:(hh + 1) * 128], lhsT=ident[:], rhs=mnegb[:], start=False, stop=True),
                                 r=[ident, mnegb], w=[PS])
                        for hh in range(4):
                            h = hg * 4 + hh
                            S.op("act", lambda e, h=h, hh=hh, hg2=hg2: e.activation(out=LT[:, hg2 * 4 + hh, :], in_=PS[:, hh * 128:(hh + 1) * 128], func=AF.Exp,
                                                                           bias=dts[:, 5, h:h + 1]), r=[PS, dts], w=[LT])
                    pd = nextpd()
                    S.op("pe", lambda e, g=g, pd=pd: e.matmul(out=pd[:, 0:128], lhsT=bcTb[:, g, :], rhs=bcTb[:, 2 + g, :], start=True, stop=True), r=[bcTb], w=[pd])
                    S.op("dve", lambda e, g=g, pd=pd: e.tensor_tensor(out=MT[:, 8 * g:8 * g + 8, :], in0=LT[:],
                                                                      in1=pd[:, 0:128].unsqueeze(1).to_broadcast([128, 8, 128]), op=ALU.mult), r=[pd, LT], w=[MT])
                yield
                for i in range(8):
                    S.op("pe", lambda e, i=i: e.transpose(out=PO[:, 1024 + i * 128:1024 + (i + 1) * 128], in_=xcT[:, i, :], identity=identf[:]),
                         r=[xcT, identf], w=[PO.b[2 + i // 4]])
                S.op("act", lambda e: e.copy(out=xcs[:], in_=PO[:, 1024:2048]), r=[PO.b[2], PO.b[3]], w=[xcs])
                x3 = xcs[:].rearrange("p (h d) -> p h d", h=16)
                S.op("dve", lambda e: e.tensor_tensor(out=xdt[:].rearrange("p (h d) -> p h d", h=16), in0=x3,
                                                      in1=dts[:, 2, :].unsqueeze(2).to_broadcast([128, 16, 64]), op=ALU.mult), r=[xcs, dts], w=[xdt])
                S.op("pool", lambda e: e.tensor_tensor(out=xw[:].rearrange("p (h d) -> p h d", h=16), in0=x3,
                                                       in1=dts[:, 6, :].unsqueeze(2).to_broadcast([128, 16, 64]), op=ALU.mult), r=[xcs, dts], w=[xw])
                yield
                for g in range(2):
                    S.op("pe", lambda e, g=g: e.transpose(out=PTb[:, g, :], in_=bcTb[:, g, :], identity=ident[:]), r=[bcTb, ident], w=[PTb])
                S.op("act", lambda e: e.copy(out=Btm[:].rearrange("p (g n) -> p g n", g=2), in_=PTb[:, 0:2, :]), r=[PTb], w=[Btm])
                yield
                for h in range(16):
                    S.op("pe", lambda e, h=h: e.matmul(out=PO[:, h * 64:(h + 1) * 64], lhsT=MT[:, h, :], rhs=xdt[:, h * 64:(h + 1) * 64],
                                                       start=True, stop=True), r=[MT, xdt], w=[PO.b[h // 8]])
                for h in range(16):
                    S.op("pe", lambda e, h=h: e.matmul(out=PO[:, 1024 + h * 64:1024 + (h + 1) * 64], lhsT=bcTb[:, 2 + h // 8, :],
                                                       rhs=hSb[l][:, h * 64:(h + 1) * 64], start=True, stop=True), r=[bcTb, hSb[l]], w=[PO.b[2 + h // 8]])
                S.op("dve", lambda e: e.tensor_tensor(out=tmp[:].rearrange("p (h d) -> p h d", h=16), in0=PO[:, 1024:2048].rearrange("p (h d) -> p h d", h=16),
                                                      in1=dts[:, 1, :].unsqueeze(2).to_broadcast([128, 16, 64]), op=ALU.mult), r=[PO.b[2], PO.b[3], dts], w=[tmp])
                S.op("dve", lambda e: e.tensor_tensor(out=tmp[:], in0=tmp[:], in1=PO[:, 0:1024], op=ALU.add), r=[tmp, PO.b[0], PO.b[1]], w=[tmp])
                S.op("pool", lambda e: e.tensor_tensor(out=tmp2[:].rearrange("p (h d) -> p h d", h=16), in0=x3,
                                                       in1=Dsk[:, l, :].unsqueeze(2).to_broadcast([128, 16, 64]), op=ALU.mult), r=[xcs, Dsk], w=[tmp2])
                S.op("dve", lambda e: e.tensor_tensor(out=tmp[:], in0=tmp[:], in1=tmp2[:], op=ALU.add), r=[tmp, tmp2], w=[tmp])
                S.op("dve", lambda e: e.tensor_tensor(out=tmp[:], in0=tmp[:], in1=czs[:], op=ALU.mult), r=[tmp, czs], w=[tmp])
                yield
                for g in range(2):
                    pd = nextpd()
                    S.op("pe", lambda e, g=g, pd=pd: e.matmul(out=pd[:], lhsT=Btm[:, g * 128:(g + 1) * 128], rhs=xw[:, g * 512:(g + 1) * 512], start=True, stop=True),
                         r=[Btm, xw], w=[pd])
                    S.op("pool", lambda e, g=g: e.tensor_tensor(out=hS[l][:, g * 512:(g + 1) * 512].rearrange("p (h d) -> p h d", h=8),
                                                                in0=hS[l][:, g * 512:(g + 1) * 512].rearrange("p (h d) -> p h d", h=8),
                                                                in1=dts[:, 7, 8 * g:8 * g + 8].unsqueeze(2).to_broadcast([128, 8, 64]), op=ALU.mult),
                         r=[hS[l], dts, hSb[l]], w=[hS[l]])
                    S.op("dve", lambda e, g=g, pd=pd: e.tensor_tensor(out=hS[l][:, g * 512:(g + 1) * 512], in0=hS[l][:, g * 512:(g + 1) * 512], in1=pd[:], op=ALU.add),
                         r=[hS[l], pd], w=[hS[l]])
                S.op("act", lambda e: e.copy(out=hSb[l][:], in_=hS[l][:]), r=[hS[l]], w=[hSb[l]])
                yield
                for g in range(2):
                    S.op("act", lambda e, g=g: e.activation(out=sq[:, g * 512:(g + 1) * 512], in_=tmp[:, g * 512:(g + 1) * 512], func=AF.Square,
                                                            accum_out=st8[:, 16 + g:17 + g]), r=[tmp], w=[sq, st8])
                S.op("act", lambda e: e.activation(out=st8[:, 16:18], in_=st8[:, 16:18], func=AF.Ln, scale=1.0 / 512, bias=EPS), r=[st8], w=[st8])
                S.op("act", lambda e: e.activation(out=st8[:, 16:18], in_=st8[:, 16:18], func=AF.Exp, scale=-0.5), r=[st8], w=[st8])
                S.op("dve", lambda e: e.tensor_tensor(out=tmp[:].rearrange("p (g d) -> p g d", g=2), in0=tmp[:].rearrange("p (g d) -> p g d", g=2),
                                                      in1=st8[:, 16:18].unsqueeze(2).to_broadcast([128, 2, 512]), op=ALU.mult), r=[tmp, st8], w=[tmp])
                S.op("dve", lambda e: e.tensor_tensor(out=tmp[:], in0=tmp[:], in1=snwb[:, l, :], op=ALU.mult), r=[tmp, snwb], w=[tmp])
                yield "FINAL"
                S.op("dve", lambda e: e.tensor_tensor(out=mix[:], in0=tmp[:], in1=gates[:, 2, :], op=ALU.mult), r=[tmp, gates], w=[mix])


            def drain(g_):
                for _ in g_:
                    pass

            def chain(*gs):
                for g_ in gs:
                    yield from g_

            drain(gPC())
            gp = chain(gPA(), gPB())
            for tok in gMC():
                if tok == "FINAL":
                    drain(gp)
                else:
                    next(gp, None)
            drain(gp)
            ga_, gb_ = gMA(), gMB()
            alive = [ga_, gb_]
            while alive:
                for g_ in list(alive):
                    try:
                        next(g_)
                    except StopIteration:
                        alive.remove(g_)
            dense_tail(l, g1b)


        def dense_tail(l, g1b, rows=128):
            S.op("act", lambda e: e.copy(out=xn[0:rows, :], in_=mix[0:rows, :]), r=[mix], w=[xn])
            for k in range(8):
                S.op("pe", lambda e, k=k: e.transpose(out=PTb[:, k, 0:rows], in_=xn[0:rows, k * 128:(k + 1) * 128], identity=ident[0:rows, 0:rows]),
                     r=[xn, ident], w=[PTb])
            S.op("act", lambda e: e.copy(out=hT[:, :, 0:rows], in_=PTb[:, :, 0:rows]), r=[PTb], w=[hT])
            for j in range(2):
                wb_ = wload(tl(woutT, l, j), 512, dram_b["woutb%d" % l])
                pd = nextpd()
                for k in range(8):
                    S.op("pe", lambda e, k=k, pd=pd, wb_=wb_: e.matmul(out=pd[0:rows, :], lhsT=hT[:, k, 0:rows], rhs=wb_[:, k, :], start=(k == 0), stop=(k == 7)),
                         r=[hT, wb_], w=[pd])
                S.op("dve", lambda e, j=j, pd=pd: e.tensor_tensor(out=tmp[0:rows, j * 512:(j + 1) * 512], in0=pd[0:rows, :], in1=g1b[0:rows, j * 512:(j + 1) * 512], op=ALU.mult),
                     r=[pd, g1b], w=[tmp])
            S.op("dve", lambda e: e.tensor_tensor(out=xres[0:rows, :], in0=xres[0:rows, :], in1=tmp[0:rows, :], op=ALU.add), r=[tmp, xres], w=[xres])

        def mlp(l, g2b, xrs, hTs, uTs, rows=128):
            nch = len(xrs)
            cnt_ = 0
            for j in range(8):
                wb_ = wload(tl(wupT, l, j), 512, dram_b["wupb%d" % l])
                for c in range(nch):
                    hT_, uT_ = hTs[c], uTs[c]
                    pd = nextpd()
                    for k in range(8):
                        S.op("pe", lambda e, k=k, pd=pd, wb_=wb_, hT_=hT_: e.matmul(out=pd[0:rows, :], lhsT=hT_[:, k, 0:rows], rhs=wb_[:, k, :],
                                                                                  start=(k == 0), stop=(k == 7)), r=[hT_, wb_], w=[pd])
                    hsel = cnt_ % 2
                    cnt_ += 1
                    S.op("dve", lambda e, pd=pd, hsel=hsel: e.tensor_scalar(out=sq[0:rows, hsel * 512:(hsel + 1) * 512], in0=pd[0:rows, :], scalar1=0.0, scalar2=None, op0=ALU.max),
                         r=[pd], w=[sqh[hsel]])
                    S.op("dve", lambda e, hsel=hsel: e.tensor_tensor(out=xn[0:rows, hsel * 512:(hsel + 1) * 512], in0=sq[0:rows, hsel * 512:(hsel + 1) * 512],
                                                                      in1=sq[0:rows, hsel * 512:(hsel + 1) * 512], op=ALU.mult), r=[sqh[hsel]], w=[xnh[hsel]])
                    for i in range(4):
                        S.op("pe", lambda e, i=i, hsel=hsel: e.transpose(out=PTb[:, hsel * 4 + i, 0:rows], in_=xn[0:rows, hsel * 512 + i * 128:hsel * 512 + (i + 1) * 128],
                                                                        identity=ident[0:rows, 0:rows]), r=[xnh[hsel], ident], w=[PTb.b[hsel]])
                    S.op("act", lambda e, j=j, hsel=hsel, uT_=uT_: e.copy(out=uT_[:, j * 4:j * 4 + 4, 0:rows], in_=PTb[:, hsel * 4:hsel * 4 + 4, 0:rows]),
                         r=[PTb.b[hsel]], w=[uT_.b[j]])
            for ct in range(2):
                for sl in range(4):
                    wb_ = wload(tl(wdownT, l, ct * 4 + sl), 512, dram_b["wdownb%d" % l])
                    for c in range(nch):
                        pd = PD[c]
                        uT_ = uTs[c]
                        for k in range(8):
                            S.op("pe", lambda e, k=k, pd=pd, wb_=wb_, sl=sl, uT_=uT_: e.matmul(out=pd[0:rows, :], lhsT=uT_[:, sl * 8 + k, 0:rows], rhs=wb_[:, k, :],
                                                                                             start=(sl == 0 and k == 0), stop=(sl == 3 and k == 7)), r=[uT_, wb_], w=[pd])
                for c in range(nch):
                    pd, xr = PD[c], xrs[c]
                    S.op("dve", lambda e, ct=ct, pd=pd: e.tensor_tensor(out=tmp[0:rows, ct * 512:(ct + 1) * 512], in0=pd[0:rows, :], in1=g2b[0:rows, ct * 512:(ct + 1) * 512], op=ALU.mult),
                         r=[pd, g2b], w=[tmp])
                    S.op("dve", lambda e, ct=ct, xr=xr: e.tensor_tensor(out=xr[0:rows, ct * 512:(ct + 1) * 512], in0=xr[0:rows, ct * 512:(ct + 1) * 512],
                                                                         in1=tmp[0:rows, ct * 512:(ct + 1) * 512], op=ALU.add), r=[tmp, xr], w=[xr])

        def final_out(dst_ap, rows=128):
            c = rms_stats(xres, rows, 0)
            S.op("dve", lambda e: e.scalar_tensor_tensor(out=tmp2[0:rows, :], in0=xres[0:rows, :], scalar=st8[0:rows, c:c + 1], in1=fnwb[0:rows, :],
                                                         op0=ALU.mult, op1=ALU.mult), r=[xres, st8, fnwb], w=[tmp2])
            S.dma("pool", dst_ap, tmp2[0:rows, :], r=[tmp2], w=[OUT])

        for b in range(RUN_B):
            for l in range(DEPTH):
                S.op("pool", lambda e, l=l: e.memset(Sret[l][:], 0.0), w=[Sret[l]])
                S.op("pool", lambda e, l=l: e.memset(Sbf[l][:], 0.0), w=[Sbf[l]])
                S.op("pool", lambda e, l=l: e.memset(hS[l][:], 0.0), w=[hS[l]])
                S.op("pool", lambda e, l=l: e.memset(hSb[l][:], 0.0), w=[hSb[l]])
                S.op("pool", lambda e, l=l: e.memset(chist[l][:], 0.0), w=[chist[l]])
            for tp in range(0, RUN_T, 2):
                ts_ = [t for t in (tp, tp + 1) if t < RUN_T]
                for c, t in enumerate(ts_):
                    S.dma("sp", xresL[c][:], xp[b, t * 128:(t + 1) * 128, :], w=[xresL[c]])
                for l in range(DEPTH):
                    for c, t in enumerate(ts_):
                        xres = xresL[c]
                        hT = hT_main
                        layer_chunk(l, b, t, l == DEPTH - 1)
                        if not cast_done[1]:
                            cast_layer(1)
                            cast_done[1] = True
                        hT = hTm[c]
                        norm_to_hT(A2L[l], modcL[l], 2)
                        hT = hT_main
                    mlp(l, g2bL[l], xresL[:len(ts_)], hTm[:len(ts_)], uTL[:len(ts_)])
                for c, t in enumerate(ts_):
                    xres = xresL[c]
                    final_out(y_p[b, t * 128:(t + 1) * 128, :])
            xres = xresL[0]
            for l in range(DEPTH):
                S.dma("pool", bass.AP(ret_p.tensor, (l * NB + b) * 8 * 64 * 128, [[128, 128], [16384, 4], [1, 128]]), Sret[l][:], r=[Sret[l]], w=[OUT])
                for half in range(2):
                    for i in range(4):
                        S.op("pe", lambda e, i=i, half=half, l=l: e.transpose(out=PO[:, i * 128:(i + 1) * 128], in_=hS[l][:, (half * 4 + i) * 128:(half * 4 + i + 1) * 128],
                                                                             identity=identf[:]), r=[hS[l], identf], w=[PO.b[0]])
                    S.op("act", lambda e, half=half: e.copy(out=tmp[:, half * 512:(half + 1) * 512], in_=PO[:, 0:512]), r=[PO.b[0]], w=[tmp])
                S.dma("pool", bass.AP(ssm_p.tensor, (l * NB + b) * 16 * 64 * 128, [[128, 128], [16384, 8], [1, 128]]), tmp[:], r=[tmp], w=[OUT])


        if not cast_done[1]:
            cast_layer(1)
            cast_done[1] = True
        stP.close()
        S.barrier()
        big = sb("big", [128, 8192]); prodb = sb("prodb", [128, 8192]); Vp = sb("Vp", [128, 8192])
        pa = sb("pa", [128, 768]); pb_ = sb("pb_", [128, 512]); pc = sb("pc", [128, 512]); smallp = sb("smallp", [128, 64])
        Z = dram_b["zsd"]; OS = dram_b["osd"]
        segs = {"aq": 1024, "ak": 256, "av": 256, "bq": 512, "bk": 512, "bv": 1024, "bg": 1024, "cz": 1024, "xbc": 1536, "dt": 16,
                "ga": 1024, "gb": 1024, "gc": 1024}
        zd = {k: dt_int("z_" + k, [NS, n]) for k, n in segs.items()}
        zx = dt_int("z_x", [NS, 1024]); zB = dt_int("z_B", [NS, 256]); zC = dt_int("z_C", [NS, 256]); zBr = dt_int("z_Br", [NS, 2, 8, 128])
        zCr = dt_int("z_Cr", [NS, 2, 8, 128]); zsm = dt_int("z_sm", [NS, 16, 4])
        R16 = NS
        S.dma("sp", xres[0:R16, :], xs, w=[xres])
        rowb = lambda t, off, n, rows=R16: bass.AP(t.tensor, off, [[0, rows], [1, n]])

        def s_norm(l, nw, col_sc, col_sh):
            S.dma("pool", tmp2[0:R16, :], rowb(nw, l * D, D), w=[tmp2])
            S.dma("pool", tmp[0:R16, :], modd[l, NB:NB + NS, col_sc * D:(col_sc + 1) * D], r=[dram_b["modd"]], w=[tmp])
            S.op("dve", lambda e: e.scalar_tensor_tensor(out=tmp[0:R16, :], in0=tmp[0:R16, :], scalar=1.0, in1=tmp2[0:R16, :], op0=ALU.add, op1=ALU.mult),
                 r=[tmp, tmp2], w=[tmp])
            c = rms_stats(xres, R16, 0)
            S.op("dve", lambda e: e.scalar_tensor_tensor(out=sq[0:R16, :], in0=xres[0:R16, :], scalar=st8[0:R16, c:c + 1], in1=tmp[0:R16, :], op0=ALU.mult, op1=ALU.mult),
                 r=[xres, st8, tmp], w=[sq])
            S.dma("pool", tmp2[0:R16, :], modd[l, NB:NB + NS, col_sh * D:(col_sh + 1) * D], r=[dram_b["modd"]], w=[tmp2])
            S.op("dve", lambda e: e.tensor_tensor(out=xn[0:R16, :], in0=sq[0:R16, :], in1=tmp2[0:R16, :], op=ALU.add), r=[sq, tmp2], w=[xn])
            for k in range(8):
                S.op("pe", lambda e, k=k: e.transpose(out=PTb[:, k, 0:R16], in_=xn[0:R16, k * 128:(k + 1) * 128], identity=ident[0:R16, 0:R16]),
                     r=[xn, ident], w=[PTb])
            S.op("act", lambda e: e.copy(out=hT[:, :, 0:R16], in_=PTb[:, :, 0:R16]), r=[PTb], w=[hT])

        def pairs_out(src_ap, rows, slot, width, srcT):
            S.dma("pool", osd[slot].rearrange("s (h c) -> (s h) c", c=width), src_ap, r=[srcT], w=[OS])

        for l in range(DEPTH if RUN_SAMPLE else 0):
            s_norm(l, n1w, 1, 0)
            order = ["aq", "ak", "av", "bq", "bk", "bv", "bg", "cz", "xbc", "ga", "gb", "gc", "dt"]
            segoff = {}
            o_ = 0
            for k in order:
                segoff[k] = o_
                o_ += segs[k]
            ti = 0
            for c0 in range(0, DIN, 512):
                n = min(512, DIN - c0)
                wb_ = wload(tl(winT, l, c0 // 512, n), n, dram_b["winb%d" % l])
                pd = nextpd()
                for k in range(8):
                    S.op("pe", lambda e, k=k, pd=pd, wb_=wb_, n=n: e.matmul(out=pd[0:R16, 0:n], lhsT=hT[:, k, 0:R16], rhs=wb_[:, k, 0:n], start=(k == 0), stop=(k == 7)),
                         r=[hT, wb_], w=[pd])
                stage = pa if ti % 2 == 0 else pb_
                ti += 1
                if c0 < 1024:
                    S.op("act", lambda e, pd=pd, stage=stage: e.copy(out=stage[0:R16, 0:512].rearrange("p (g1 hq d) -> p hq g1 d", g1=2, hq=4),
                                                                 in_=pd[0:R16, 0:512].rearrange("p (hq g1 d) -> p hq g1 d", hq=4, g1=2)), r=[pd], w=[stage])
                else:
                    S.op("act", lambda e, pd=pd, stage=stage, n=n: e.copy(out=stage[0:R16, 0:n], in_=pd[0:R16, 0:n]), r=[pd], w=[stage])
                for k in order:
                    a0, a1 = max(c0, segoff[k]), min(c0 + n, segoff[k] + segs[k])
                    if a0 < a1:
                        S.dma("pool", zd[k][:, a0 - segoff[k]:a1 - segoff[k]], stage[0:R16, a0 - c0:a1 - c0], r=[stage], w=[Z])
            for gi, k in enumerate(("ga", "gb", "gc")):
                S.dma("pool", gates[0:R16, gi, :], zd[k], r=[Z], w=[gates])
            S.op("act", lambda e: e.activation(out=gates[0:R16, :, :], in_=gates[0:R16, :, :], func=AF.Sigmoid), r=[gates], w=[gates])

            S.dma("pool", wk_s[l, :, 0:127, :], cache_k[l, :, 1:128, :], w=[OUT])
            S.dma("pool", wk_s[l, :, 127, :], zd["ak"], r=[Z], w=[OUT])
            S.dma("pool", wv_s[l, :, 0:127, :], cache_v[l, :, 1:128, :], w=[OUT])
            S.dma("pool", wv_s[l, :, 127, :], zd["av"], r=[Z], w=[OUT])
            K3 = big[0:64, :].rearrange("p (k d) -> p k d", d=64)
            V3 = Vp[0:64, :].rearrange("p (k d) -> p k d", d=64)
            P3 = prodb[0:64, :].rearrange("p (k d) -> p k d", d=64)
            for s_ in range(NS):
                S.dma("sp", K3[4 * s_:4 * s_ + 4, 0:127, :], cache_k[l, s_, 1:128, :].rearrange("k (g d) -> g k d", g=4), w=[big])
                S.dma("sp", V3[4 * s_:4 * s_ + 4, 0:127, :], cache_v[l, s_, 1:128, :].rearrange("k (g d) -> g k d", g=4), w=[Vp])
            S.dma("pool", K3[:, 127, :], zd["ak"].rearrange("s (g d) -> (s g) d", g=4), r=[Z], w=[big])
            S.dma("pool", V3[:, 127, :], zd["av"].rearrange("s (g d) -> (s g) d", g=4), r=[Z], w=[Vp])
            S.dma("pool", pa[0:64, 0:256], zd["aq"].rearrange("s (g c) -> (s g) c", g=4), r=[Z], w=[pa])
            for s_ in range(NS):
                S.dma("pool", pb_[4 * s_:4 * s_ + 4, 0:512], bass.AP(vecx.tensor, 128, [[4 * 384, 4], [384, 4], [1, 128]]), r=[dram_b["vecx"]], w=[pb_])
                S.dma("pool", smallp[4 * s_:4 * s_ + 4, 0:4], sinks[l].rearrange("(g hq) -> g hq", g=4), w=[smallp])
            S.op("act", lambda e: e.activation(out=smallp[0:64, 0:4], in_=smallp[0:64, 0:4], func=AF.Exp), r=[smallp], w=[smallp])
            for hq in range(4):
                S.op("dve", lambda e, hq=hq: e.tensor_tensor(out=P3, in0=K3, in1=pa[0:64, hq * 64:(hq + 1) * 64].unsqueeze(1).to_broadcast([64, 128, 64]), op=ALU.mult),
                     r=[big, pa], w=[prodb])
                S.op("dve", lambda e, hq=hq: e.tensor_reduce(out=pc[0:64, hq * 128:(hq + 1) * 128], in_=P3, axis=AX.X, op=ALU.add), r=[prodb], w=[pc])
            S.op("dve", lambda e: e.scalar_tensor_tensor(out=pc[0:64, 0:512], in0=pc[0:64, 0:512], scalar=0.125, in1=pb_[0:64, 0:512], op0=ALU.mult, op1=ALU.add),
                 r=[pc, pb_], w=[pc])
            S.op("act", lambda e: e.activation(out=pc[0:64, 0:512], in_=pc[0:64, 0:512], func=AF.Exp), r=[pc], w=[pc])
            S.op("dve", lambda e: e.tensor_reduce(out=smallp[0:64, 4:8], in_=pc[0:64, 0:512].rearrange("p (h k) -> p h k", h=4), axis=AX.X, op=ALU.add), r=[pc], w=[smallp])
            S.op("dve", lambda e: e.tensor_tensor(out=smallp[0:64, 4:8], in0=smallp[0:64, 4:8], in1=smallp[0:64, 0:4], op=ALU.add), r=[smallp], w=[smallp])
            S.op("dve", lambda e: e.reciprocal(out=smallp[0:64, 4:8], in_=smallp[0:64, 4:8]), r=[smallp], w=[smallp])
            PV3 = prodb[0:64, :].rearrange("p (d k) -> p d k", k=128)
            for hq in range(4):
                S.op("dve", lambda e, hq=hq: e.tensor_tensor(out=PV3, in0=V3.rearrange("p k d -> p d k"),
                                                             in1=pc[0:64, hq * 128:(hq + 1) * 128].unsqueeze(1).to_broadcast([64, 64, 128]), op=ALU.mult), r=[Vp, pc], w=[prodb])
                S.op("dve", lambda e, hq=hq: e.tensor_reduce(out=pa[0:64, 256 + hq * 64:256 + (hq + 1) * 64], in_=PV3, axis=AX.X, op=ALU.add), r=[prodb], w=[pa])
            S.op("dve", lambda e: e.tensor_tensor(out=pa[0:64, 512:768].rearrange("p (h d) -> p h d", h=4), in0=pa[0:64, 256:512].rearrange("p (h d) -> p h d", h=4),
                                                  in1=smallp[0:64, 4:8].unsqueeze(2).to_broadcast([64, 4, 64]), op=ALU.mult), r=[pa, smallp], w=[pa])
            S.dma("pool", osd[0].rearrange("s (g c) -> (s g) c", g=4), pa[0:64, 512:768], r=[pa], w=[OS])
            S.dma("pool", tmp[0:R16, :], osd[0], r=[OS], w=[tmp])
            S.op("dve", lambda e: e.tensor_tensor(out=mix[0:R16, :], in0=tmp[0:R16, :], in1=gates[0:R16, 0, :], op=ALU.mult), r=[tmp, gates], w=[mix])

            S3 = big[:, :].rearrange("p (d e) -> p d e", e=128)
            PR3 = prodb[:, :].rearrange("p (d e) -> p d e", e=128)
            S.dma("sp", S3, st_ret[l].rearrange("s h d e -> (s h) d e"), w=[big])
            S.dma("pool", pa[:, 0:64], zd["bq"].rearrange("s (h d) -> (s h) d", h=8), r=[Z], w=[pa])
            S.dma("pool", pa[:, 64:128], zd["bk"].rearrange("s (h d) -> (s h) d", h=8), r=[Z], w=[pa])
            S.dma("pool", pb_[:, 0:128], zd["bv"].rearrange("s (h e) -> (s h) e", h=8), r=[Z], w=[pb_])
            S.dma("pool", smallp[:, 8:40], hc["coss"], w=[smallp])
            S.dma("pool", smallp[:, 40:41], hc["g1p"], w=[smallp])
            S.dma("pool", pc[:, 0:32], hc["sins"], w=[pc])
            for j in range(2):
                v3 = pa[:, j * 64:(j + 1) * 64].rearrange("p (f two) -> p f two", two=2)
                o3_ = pa[:, 128 + j * 64:128 + (j + 1) * 64].rearrange("p (f two) -> p f two", two=2)
                x1, x2 = v3[:, :, 0], v3[:, :, 1]
                cs_, sn_ = smallp[:, 8:40], pc[:, 0:32]
                S.op("dve", lambda e, x1=x1, cs_=cs_: e.tensor_tensor(out=pc[:, 32:64], in0=x1, in1=cs_, op=ALU.mult), r=[pa, smallp], w=[pc])
                S.op("dve", lambda e, x2=x2, sn_=sn_: e.tensor_tensor(out=pc[:, 64:96], in0=x2, in1=sn_, op=ALU.mult), r=[pa, pc], w=[pc])
                S.op("dve", lambda e, o3_=o3_: e.tensor_tensor(out=o3_[:, :, 0], in0=pc[:, 32:64], in1=pc[:, 64:96], op=ALU.subtract), r=[pc], w=[pa])
                S.op("dve", lambda e, x1=x1, sn_=sn_: e.tensor_tensor(out=pc[:, 32:64], in0=x1, in1=sn_, op=ALU.mult), r=[pa, pc], w=[pc])
                S.op("dve", lambda e, x2=x2, cs_=cs_: e.tensor_tensor(out=pc[:, 64:96], in0=x2, in1=cs_, op=ALU.mult), r=[pa, smallp], w=[pc])
                S.op("dve", lambda e, o3_=o3_: e.tensor_tensor(out=o3_[:, :, 1], in0=pc[:, 32:64], in1=pc[:, 64:96], op=ALU.add), r=[pc], w=[pa])
            S.op("dve", lambda e: e.tensor_scalar(out=pa[:, 192:256], in0=pa[:, 192:256], scalar1=0.125, scalar2=None, op0=ALU.mult), r=[pa], w=[pa])
            S.op("dve", lambda e: e.tensor_tensor(out=PR3, in0=pa[:, 192:256].unsqueeze(2).to_broadcast([128, 64, 128]),
                                                  in1=pb_[:, 0:128].unsqueeze(1).to_broadcast([128, 64, 128]), op=ALU.mult), r=[pa, pb_], w=[prodb])
            S.op("dve", lambda e: e.scalar_tensor_tensor(out=big[:, :], in0=big[:, :], scalar=smallp[:, 40:41], in1=prodb[:, :], op0=ALU.mult, op1=ALU.add),
                 r=[big, smallp, prodb], w=[big])
            S.dma("sp", ret_s[l].rearrange("s h d e -> (s h) d e"), S3, r=[big], w=[OUT])
            S.op("dve", lambda e: e.tensor_tensor(out=PR3, in0=S3, in1=pa[:, 128:192].unsqueeze(2).to_broadcast([128, 64, 128]), op=ALU.mult), r=[big, pa], w=[prodb])
            S.op("dve", lambda e: e.tensor_reduce(out=pb_[:, 128:256], in_=PR3.rearrange("p d e -> p e d"), axis=AX.X, op=ALU.add), r=[prodb], w=[pb_])
            S.dma("pool", osd[1].rearrange("s (h e) -> (s h) e", h=8), pb_[:, 128:256], r=[pb_], w=[OS])
            S.dma("pool", tmp[0:R16, :], osd[1], r=[OS], w=[tmp])
            S.dma("pool", bgs[0:R16, :], zd["bg"], r=[Z], w=[bgs])
            S.op("act", lambda e: e.activation(out=bgs[0:R16, :], in_=bgs[0:R16, :], func=AF.Silu), r=[bgs], w=[bgs])
            S.op("act", lambda e: e.activation(out=sq[0:R16, :], in_=tmp[0:R16, :], func=AF.Square), r=[tmp], w=[sq])
            S.op("dve", lambda e: e.tensor_reduce(out=st8[0:R16, 8:16], in_=sq[0:R16, :].rearrange("p (h e) -> p h e", h=8), axis=AX.X, op=ALU.add), r=[sq], w=[st8])
            S.op("act", lambda e: e.activation(out=st8[0:R16, 8:16], in_=st8[0:R16, 8:16], func=AF.Ln, scale=1.0 / 128, bias=EPS), r=[st8], w=[st8])
            S.op("act", lambda e: e.activation(out=st8[0:R16, 8:16], in_=st8[0:R16, 8:16], func=AF.Exp, scale=-0.5), r=[st8], w=[st8])
            S.op("dve", lambda e: e.tensor_tensor(out=tmp[0:R16, :].rearrange("p (h e) -> p h e", h=8), in0=tmp[0:R16, :].rearrange("p (h e) -> p h e", h=8),
                                                  in1=st8[0:R16, 8:16].unsqueeze(2).to_broadcast([R16, 8, 128]), op=ALU.mult), r=[tmp, st8], w=[tmp])
            S.op("dve", lambda e: e.tensor_tensor(out=tmp[0:R16, :], in0=tmp[0:R16, :], in1=bgs[0:R16, :], op=ALU.mult), r=[tmp, bgs], w=[tmp])
            S.op("dve", lambda e: e.tensor_tensor(out=tmp[0:R16, :], in0=tmp[0:R16, :], in1=gates[0:R16, 1, :], op=ALU.mult), r=[tmp, gates], w=[tmp])
            S.op("dve", lambda e: e.tensor_tensor(out=mix[0:R16, :], in0=mix[0:R16, :], in1=tmp[0:R16, :], op=ALU.add), r=[tmp, mix], w=[mix])

            hist = big[0:R16, 0:4608].rearrange("p (i c) -> p i c", i=3)
            cwb = Vp[0:R16, 0:6144].rearrange("p (i c) -> p i c", i=4)
            cbb, cx, acc, tb = prodb[0:R16, 0:1536], prodb[0:R16, 1536:3072], prodb[0:R16, 3072:4608], prodb[0:R16, 4608:6144]
            S.dma("sp", hist, st_conv[l], w=[big])
            S.dma("sp", cwb, bass.AP(conv_w.tensor, l * 4 * 1536, [[0, R16], [1536, 4], [1, 1536]]), w=[Vp])
            S.dma("sp", cbb, rowb(conv_b, l * 1536, 1536), w=[prodb])
            S.dma("sp", cx, zd["xbc"], r=[Z], w=[prodb])
            S.dma("pool", conv_s[l, :, 0:2, :], st_conv[l, :, 1:3, :], w=[OUT])
            S.dma("pool", conv_s[l, :, 2, :], zd["xbc"], r=[Z], w=[OUT])
            S.op("dve", lambda e: e.tensor_tensor(out=acc, in0=cx, in1=cwb[:, 3, :], op=ALU.mult), r=[prodb, big, Vp], w=[prodb])
            S.op("dve", lambda e: e.tensor_tensor(out=acc, in0=acc, in1=cbb, op=ALU.add), r=[prodb], w=[prodb])
            for i in range(3):
                S.op("dve", lambda e, i=i: e.tensor_tensor(out=tb, in0=hist[:, i, :], in1=cwb[:, i, :], op=ALU.mult), r=[big, prodb, Vp], w=[prodb])
                S.op("dve", lambda e: e.tensor_tensor(out=acc, in0=acc, in1=tb, op=ALU.add), r=[prodb], w=[prodb])
            S.op("act", lambda e: e.activation(out=acc, in_=acc, func=AF.Silu), r=[prodb], w=[prodb])
            S.dma("pool", zx, prodb[0:R16, 3072:3072 + 1024], r=[prodb], w=[Z])
            S.dma("pool", zB, prodb[0:R16, 3072 + 1024:3072 + 1280], r=[prodb], w=[Z])
            S.dma("pool", zC, prodb[0:R16, 3072 + 1280:3072 + 1536], r=[prodb], w=[Z])
            S.dma("pool", zBr.rearrange("s g r n -> (s g) r n"), bass.AP(zB.tensor, 0, [[128, 2 * NS], [0, 8], [1, 128]]), r=[Z], w=[Z])
            S.dma("pool", zCr.rearrange("s g r n -> (s g) r n"), bass.AP(zC.tensor, 0, [[128, 2 * NS], [0, 8], [1, 128]]), r=[Z], w=[Z])
            S.dma("pool", st8[0:R16, 0:16], zd["dt"], r=[Z], w=[st8])
            S.op("dve", lambda e: e.tensor_tensor(out=st8[0:R16, 0:16], in0=st8[0:R16, 0:16], in1=dtb[0:R16, l, :], op=ALU.add), r=[st8, dtb], w=[st8])
            S.op("act", lambda e: e.activation(out=st8[0:R16, 0:16], in_=st8[0:R16, 0:16], func=AF.Exp), r=[st8], w=[st8])
            sm4 = smallp[0:R16, 0:64].rearrange("p (h c) -> p h c", c=4)
            S.op("act", lambda e: e.activation(out=sm4[:, :, 0], in_=st8[0:R16, 0:16], func=AF.Ln, bias=1.0), r=[st8], w=[smallp])
            S.op("dve", lambda e: e.tensor_tensor(out=sm4[:, :, 1], in0=sm4[:, :, 0], in1=Arow[0:R16, l, :], op=ALU.mult), r=[smallp, Arow], w=[smallp])
            S.op("act", lambda e: e.activation(out=sm4[:, :, 1], in_=sm4[:, :, 1], func=AF.Exp), r=[smallp], w=[smallp])
            S.op("dve", lambda e: e.tensor_copy(out=sm4[:, :, 2], in_=Dsk[0:R16, l, :]), r=[Dsk, smallp], w=[smallp])
            S.op("dve", lambda e: e.tensor_copy(out=sm4[:, :, 3], in_=Dsk[0:R16, l, :]), r=[Dsk, smallp], w=[smallp])
            S.dma("pool", zsm.rearrange("s h c -> s (h c)"), smallp[0:R16, 0:64], r=[smallp], w=[Z])
            for half in range(2):
                s0 = half * 8
                H3 = big[:, :].rearrange("p (q n) -> p q n", n=128)
                S.dma("sp", H3, st_ssm[l, s0:s0 + 8].rearrange("s h q n -> (s h) q n"), w=[big])
                S.dma("pool", pa[:, 0:64], zx[s0:s0 + 8, :].rearrange("s (h q) -> (s h) q", h=16), r=[Z], w=[pa])
                S.dma("pool", pb_[:, 0:128], zBr[s0:s0 + 8].rearrange("s g r n -> (s g r) n"), r=[Z], w=[pb_])
                S.dma("pool", pb_[:, 128:256], zCr[s0:s0 + 8].rearrange("s g r n -> (s g r) n"), r=[Z], w=[pb_])
                S.dma("pool", pc[:, 0:4], zsm[s0:s0 + 8].rearrange("s h c -> (s h) c"), r=[Z], w=[pc])
                S.op("dve", lambda e: e.tensor_scalar(out=pa[:, 64:128], in0=pa[:, 0:64], scalar1=pc[:, 0:1], scalar2=None, op0=ALU.mult), r=[pa, pc], w=[pa])
                S.op("dve", lambda e: e.tensor_tensor(out=PR3.rearrange("p d e -> p d e"), in0=pa[:, 64:128].unsqueeze(2).to_broadcast([128, 64, 128]),
                                                      in1=pb_[:, 0:128].unsqueeze(1).to_broadcast([128, 64, 128]), op=ALU.mult), r=[pa, pb_], w=[prodb])
                S.op("dve", lambda e: e.scalar_tensor_tensor(out=big[:, :], in0=big[:, :], scalar=pc[:, 1:2], in1=prodb[:, :], op0=ALU.mult, op1=ALU.add),
                     r=[big, pc, prodb], w=[big])
                S.dma("sp", ssm_s[l, s0:s0 + 8].rearrange("s h q n -> (s h) q n"), H3, r=[big], w=[OUT])
                S.op("dve", lambda e: e.tensor_tensor(out=PR3, in0=H3, in1=pb_[:, 128:256].unsqueeze(1).to_broadcast([128, 64, 128]), op=ALU.mult), r=[big, pb_], w=[prodb])
                S.op("dve", lambda e: e.tensor_reduce(out=pa[:, 128:192], in_=PR3, axis=AX.X, op=ALU.add), r=[prodb], w=[pa])
                S.op("dve", lambda e: e.scalar_tensor_tensor(out=pa[:, 128:192], in0=pa[:, 0:64], scalar=pc[:, 2:3], in1=pa[:, 128:192], op0=ALU.mult, op1=ALU.add),
                     r=[pa, pc], w=[pa])
                S.dma("pool", osd[2, s0:s0 + 8].rearrange("s (h q) -> (s h) q", h=16), pa[:, 128:192], r=[pa], w=[OS])
            S.dma("pool", tmp[0:R16, :], osd[2], r=[OS], w=[tmp])
            S.dma("pool", bgs[0:R16, :], zd["cz"], r=[Z], w=[bgs])
            S.op("act", lambda e: e.activation(out=bgs[0:R16, :], in_=bgs[0:R16, :], func=AF.Silu), r=[bgs], w=[bgs])
            S.op("dve", lambda e: e.tensor_tensor(out=tmp[0:R16, :], in0=tmp[0:R16, :], in1=bgs[0:R16, :], op=ALU.mult), r=[tmp, bgs], w=[tmp])
            for g in range(2):
                S.op("act", lambda e, g=g: e.activation(out=sq[0:R16, g * 512:(g + 1) * 512], in_=tmp[0:R16, g * 512:(g + 1) * 512], func=AF.Square,
                                                        accum_out=st8[0:R16, 16 + g:17 + g]), r=[tmp], w=[sq, st8])
            S.op("act", lambda e: e.activation(out=st8[0:R16, 16:18], in_=st8[0:R16, 16:18], func=AF.Ln, scale=1.0 / 512, bias=EPS), r=[st8], w=[st8])
            S.op("act", lambda e: e.activation(out=st8[0:R16, 16:18], in_=st8[0:R16, 16:18], func=AF.Exp, scale=-0.5), r=[st8], w=[st8])
            S.op("dve", lambda e: e.tensor_tensor(out=tmp[0:R16, :].rearrange("p (g d) -> p g d", g=2), in0=tmp[0:R16, :].rearrange("p (g d) -> p g d", g=2),
                                                  in1=st8[0:R16, 16:18].unsqueeze(2).to_broadcast([R16, 2, 512]), op=ALU.mult), r=[tmp, st8], w=[tmp])
            S.op("dve", lambda e: e.tensor_tensor(out=tmp[0:R16, :], in0=tmp[0:R16, :], in1=snwb[0:R16, l, :], op=ALU.mult), r=[tmp, snwb], w=[tmp])
            S.op("dve", lambda e: e.tensor_tensor(out=tmp[0:R16, :], in0=tmp[0:R16, :], in1=gates[0:R16, 2, :], op=ALU.mult), r=[tmp, gates], w=[tmp])
            S.op("dve", lambda e: e.tensor_tensor(out=mix[0:R16, :], in0=mix[0:R16, :], in1=tmp[0:R16, :], op=ALU.add), r=[tmp, mix], w=[mix])
            S.dma("pool", g1bL[l][0:R16, :], modd[l, NB:NB + NS, 2 * D:3 * D], r=[dram_b["modd"]], w=[g1bL[l]])
            S.dma("pool", g2bL[l][0:R16, :], modd[l, NB:NB + NS, 5 * D:6 * D], r=[dram_b["modd"]], w=[g2bL[l]])
            dense_tail(l, g1bL[l], rows=R16)
            s_norm(l, n2w, 4, 3)
            mlp(l, g2bL[l], [xres], [hT], [uT], rows=R16)
        if RUN_SAMPLE:
            final_out(y_s, rows=R16)

        S.finish("sp")
        build.counts = dict(S.cnt)
    return nc


_NC = None


def kernel(**inp):
    global _NC
    f = lambda a: np.ascontiguousarray(np.asarray(a, dtype=np.float32))
    hcst = host_consts()
    if _NC is None:
        _NC = build()
    in_maps = []
    for c in range(RUN_CORES):
        ps, ss = slice(c * NB, (c + 1) * NB), slice(c * NS, (c + 1) * NS)
        m = {
            "xp": f(inp["x_prompt"][ps]), "xs": f(inp["x_sample"][ss, 0]),
            "cc": f(np.concatenate([np.asarray(inp["c_prompt"])[ps], np.asarray(inp["c_sample"])[ss]], 0)),
            "cache_k": f(np.asarray(inp["cache_win_k"])[:, ss].reshape(DEPTH, NS, 128, 256)),
            "cache_v": f(np.asarray(inp["cache_win_v"])[:, ss].reshape(DEPTH, NS, 128, 256)),
            "st_ret": f(np.asarray(inp["state_ret"])[:, ss]), "st_ssm": f(np.asarray(inp["state_ssm"])[:, ss]),
            "st_conv": f(np.asarray(inp["state_conv"])[:, ss]),
            "rel_tab": f(inp["rel_bias_table"]), "sinks": f(inp["attn_sinks"]), "n1w": f(inp["norm1_w"]), "n2w": f(inp["norm2_w"]),
            "ada_w": f(inp["ada_w"]), "ada_b": f(inp["ada_b"]), "w_in": f(inp["w_in"]), "conv_w": f(inp["conv_w"]), "conv_b": f(inp["conv_b"]),
            "dt_bias": f(inp["dt_bias"]), "a_log": f(inp["A_log"]), "d_skip": f(inp["D_skip"]), "snw": f(inp["ssm_norm_w"]),
            "w_out": f(inp["w_out"]), "w_up": f(inp["w_up"]), "w_down": f(inp["w_down"]), "fnw": f(inp["final_norm_w"]),
        }
        for k, v in hcst.items():
            m["c_" + k] = v
        in_maps.append(m)
    res = run_bass_kernel_spmd(_NC, in_maps, core_ids=list(range(RUN_CORES)), **({'trace': True} if TRACE else {}))
    if TRACE:
        print('EXEC_NS', res.exec_time_ns, flush=True)
    R = res.results
    cat = lambda k, ax: np.concatenate([np.asarray(r[k]) for r in R], axis=ax)
    y_p = cat("y_p", 0)
    y_s = cat("y_s", 0).reshape(RUN_CORES * NS, 1, D)
    wk_p = cat("wk_p", 1).reshape(DEPTH, RUN_CORES * NB, 128, 4, 64)
    wv_p = cat("wv_p", 1).reshape(DEPTH, RUN_CORES * NB, 128, 4, 64)
    ret_p = cat("ret_p", 1); ssm_p = cat("ssm_p", 1); conv_p = cat("conv_p", 1)
    wk_s = cat("wk_s", 1).reshape(DEPTH, RUN_CORES * NS, 128, 4, 64)
    wv_s = cat("wv_s", 1).reshape(DEPTH, RUN_CORES * NS, 128, 4, 64)
    ret_s = cat("ret_s", 1); ssm_s = cat("ssm_s", 1); conv_s = cat("conv_s", 1)
    if DEBUG:
        kernel.dbg = np.asarray(R[0]["dbg"])
    return (y_p, y_s, wk_p, wv_p, ret_p, ssm_p, conv_p, wk_s, wv_s, ret_s, ssm_s, conv_s)
```
